# Optimizing a Trainium2 kernel written in Bass

```python
import math
import jax, jax.numpy as jnp
from jax import lax
import numpy as np

D_MODEL = 1024
BATCH = 8
SEQ = 2048
DEPTH = 1

CHUNK = 64
Q_BLOCK = 128
HEAD_DIM = 64
N_DIFF_HEADS = 4
N_SB_HEADS = 8
DIFF_WIDTH = N_DIFF_HEADS * 2 * HEAD_DIM
SB_WIDTH = N_SB_HEADS * HEAD_DIM
MIX_WIDTH = DIFF_WIDTH + SB_WIDTH
IN_WIDTH = 3 * MIX_WIDTH
D_FF = 4 * D_MODEL
ROPE_THETA = 10000.0
NORM_EPS = 1e-6
N_MOD = 6

kernel_name = "hybrid_diff_stickbreaking_block"


def rms_norm(x, g):
    xf = x.astype(jnp.float32)
    y = xf * lax.rsqrt(jnp.mean(xf * xf, axis=-1, keepdims=True) + NORM_EPS)
    return (y * g.astype(jnp.float32)).astype(x.dtype)


def rope_tables(seq_len, dim):
    inv = 1.0 / (ROPE_THETA ** (jnp.arange(0, dim, 2, dtype=jnp.float32) / dim))
    ang = jnp.arange(seq_len, dtype=jnp.float32)[:, None] * inv[None, :]
    ang = jnp.concatenate([ang, ang], axis=-1)
    return jnp.cos(ang), jnp.sin(ang)


def apply_rope(x, cos, sin):
    x1, x2 = jnp.split(x, 2, axis=-1)
    rot = jnp.concatenate([-x2, x1], axis=-1)
    return (x * cos + rot * sin).astype(x.dtype)


def diff_attention(q, k, v, lam, subln_g, lambda_init):
    seq_len = q.shape[3]
    scale = HEAD_DIM ** -0.5
    outs = []
    for blk in range(seq_len // Q_BLOCK):
        q0 = blk * Q_BLOCK
        kv_len = q0 + Q_BLOCK
        qb = q[:, :, :, q0:kv_len]
        kb = k[:, :, :, :kv_len]
        vb = v[:, :, :kv_len]
        s = jnp.einsum("bhmqd,bhmkd->bhmqk", qb, kb).astype(jnp.float32) * scale
        q_chunk = (q0 + jnp.arange(Q_BLOCK)) // CHUNK
        k_chunk = jnp.arange(kv_len) // CHUNK
        mask = k_chunk[None, :] <= q_chunk[:, None]
        p = jax.nn.softmax(jnp.where(mask, s, -jnp.inf), axis=-1)
        w = p[:, :, 0] - lam * p[:, :, 1]
        outs.append(jnp.einsum("bhqk,bhkd->bhqd", w.astype(vb.dtype), vb))
    o = jnp.concatenate(outs, axis=2)
    return rms_norm(o, subln_g) * (1.0 - lambda_init)


def stick_breaking_attention(q, k, v):
    seq_len = q.shape[2]
    scale = HEAD_DIM ** -0.5
    outs = []
    for blk in range(seq_len // Q_BLOCK):
        q0 = blk * Q_BLOCK
        kv_len = q0 + Q_BLOCK
        qb = q[:, :, q0:kv_len]
        kb = k[:, :, :kv_len]
        vb = v[:, :, :kv_len]
        z = jnp.einsum("bhqd,bhkd->bhqk", qb, kb).astype(jnp.float32) * scale
        t_idx = q0 + jnp.arange(Q_BLOCK)
        s_idx = jnp.arange(kv_len)
        causal = s_idx[None, :] < t_idx[:, None]
        log_beta = jax.nn.log_sigmoid(z)
        log_1m = jnp.where(causal, jax.nn.log_sigmoid(-z), 0.0)
        suffix = lax.cumsum(log_1m, axis=3, reverse=True) - log_1m
        a = jnp.where(causal, jnp.exp(log_beta + suffix), 0.0)
        outs.append(jnp.einsum("bhqk,bhkd->bhqd", a.astype(vb.dtype), vb))
    return jnp.concatenate(outs, axis=2)


def setup_inputs(seed: int = 0) -> dict:
    key = jax.random.key(seed)
    ks = jax.random.split(key, 16)
    f32 = jnp.float32
    x = jax.random.normal(ks[0], (BATCH, SEQ, D_MODEL), f32)
    c = jax.random.normal(ks[1], (BATCH, D_MODEL), f32)
    ada_w = jax.random.normal(ks[2], (DEPTH, D_MODEL, N_MOD * D_MODEL), f32) * (0.5 * D_MODEL ** -0.5)
    ada_b = jax.random.normal(ks[3], (DEPTH, N_MOD * D_MODEL), f32) * 0.01
    mix_norm_g = 1.0 + 0.02 * jax.random.normal(ks[4], (DEPTH, D_MODEL), f32)
    w_in = jax.random.normal(ks[5], (DEPTH, D_MODEL, IN_WIDTH), f32) * D_MODEL ** -0.5
    lambda_q1 = 0.1 * jax.random.normal(ks[6], (DEPTH, HEAD_DIM), f32)
    lambda_k1 = 0.1 * jax.random.normal(ks[7], (DEPTH, HEAD_DIM), f32)
    lambda_q2 = 0.1 * jax.random.normal(ks[8], (DEPTH, HEAD_DIM), f32)
    lambda_k2 = 0.1 * jax.random.normal(ks[9], (DEPTH, HEAD_DIM), f32)
    diff_subln_g = 1.0 + 0.02 * jax.random.normal(ks[10], (DEPTH, 2 * HEAD_DIM), f32)
    w_out = jax.random.normal(ks[11], (DEPTH, MIX_WIDTH, D_MODEL), f32) * MIX_WIDTH ** -0.5
    ffn_norm_g = 1.0 + 0.02 * jax.random.normal(ks[12], (DEPTH, D_MODEL), f32)
    w_ff1 = jax.random.normal(ks[13], (DEPTH, D_MODEL, D_FF), f32) * D_MODEL ** -0.5
    w_ff2 = jax.random.normal(ks[14], (DEPTH, D_FF, D_MODEL), f32) * D_FF ** -0.5
    final_norm_g = 1.0 + 0.02 * jax.random.normal(ks[15], (D_MODEL,), f32)
    return {"x": x, "c": c, "ada_w": ada_w, "ada_b": ada_b, "mix_norm_g": mix_norm_g,
            "w_in": w_in, "lambda_q1": lambda_q1, "lambda_k1": lambda_k1,
            "lambda_q2": lambda_q2, "lambda_k2": lambda_k2, "diff_subln_g": diff_subln_g,
            "w_out": w_out, "ffn_norm_g": ffn_norm_g, "w_ff1": w_ff1, "w_ff2": w_ff2,
            "final_norm_g": final_norm_g}


def reference(x, c, ada_w, ada_b, mix_norm_g, w_in, lambda_q1, lambda_k1, lambda_q2,
              lambda_k2, diff_subln_g, w_out, ffn_norm_g, w_ff1, w_ff2, final_norm_g):
    bsz, seq_len, _ = x.shape
    cos, sin = rope_tables(seq_len, HEAD_DIM)
    cos = cos.astype(x.dtype)
    sin = sin.astype(x.dtype)
    c_act = jax.nn.silu(c)
    for layer in range(DEPTH):
        lambda_init = 0.8 - 0.6 * math.exp(-0.3 * layer)
        mod = c_act @ ada_w[layer] + ada_b[layer]
        sh_m, sc_m, g_m, sh_f, sc_f, g_f = [m[:, None, :] for m in jnp.split(mod, N_MOD, axis=-1)]

        h = rms_norm(x, mix_norm_g[layer]) * (1.0 + sc_m) + sh_m
        proj = h @ w_in[layer]
        dq, dk, dv, sq, sk, sv = jnp.split(proj, 6, axis=-1)
        dq = dq.reshape(bsz, seq_len, N_DIFF_HEADS, 2, HEAD_DIM).transpose(0, 2, 3, 1, 4)
        dk = dk.reshape(bsz, seq_len, N_DIFF_HEADS, 2, HEAD_DIM).transpose(0, 2, 3, 1, 4)
        dv = dv.reshape(bsz, seq_len, N_DIFF_HEADS, 2 * HEAD_DIM).transpose(0, 2, 1, 3)
        dq = apply_rope(dq, cos, sin)
        dk = apply_rope(dk, cos, sin)
        lam = (jnp.exp(jnp.sum(lambda_q1[layer].astype(jnp.float32) * lambda_k1[layer].astype(jnp.float32)))
               - jnp.exp(jnp.sum(lambda_q2[layer].astype(jnp.float32) * lambda_k2[layer].astype(jnp.float32)))
               + lambda_init)
        o_diff = diff_attention(dq, dk, dv, lam, diff_subln_g[layer], lambda_init)
        o_diff = o_diff.transpose(0, 2, 1, 3).reshape(bsz, seq_len, DIFF_WIDTH)

        sq = sq.reshape(bsz, seq_len, N_SB_HEADS, HEAD_DIM).transpose(0, 2, 1, 3)
        sk = sk.reshape(bsz, seq_len, N_SB_HEADS, HEAD_DIM).transpose(0, 2, 1, 3)
        sv = sv.reshape(bsz, seq_len, N_SB_HEADS, HEAD_DIM).transpose(0, 2, 1, 3)
        o_sb = stick_breaking_attention(sq, sk, sv)
        o_sb = o_sb.transpose(0, 2, 1, 3).reshape(bsz, seq_len, SB_WIDTH)

        mixed = jnp.concatenate([o_diff, o_sb], axis=-1) @ w_out[layer]
        x = x + g_m * mixed

        h = rms_norm(x, ffn_norm_g[layer]) * (1.0 + sc_f) + sh_f
        f = jnp.square(jax.nn.relu(h @ w_ff1[layer])) @ w_ff2[layer]
        x = x + g_f * f
    return rms_norm(x, final_norm_g)
```

```python
import contextlib
import math
import numpy as np
import ml_dtypes
import concourse.bass as bass
import concourse.mybir as mybir
from concourse.bass_utils import run_bass_kernel_spmd

F32 = mybir.dt.float32
BF16 = mybir.dt.bfloat16
AF = mybir.ActivationFunctionType
ALU = mybir.AluOpType
AX = mybir.AxisListType

P = 128
S = 2048
D = 1024
NT = 16
NU = 4
DFF = 4096
EPS = 1e-6
LAMBDA_INIT = 0.8 - 0.6 * math.exp(-0.3 * 0)

COMPUTE = ("pe", "act", "dve", "pool")
DMAQ = ("sp", "poolq")
STREAM_OF = {"pe": "pe", "act": "act", "dve": "dve", "pool": "pool", "sp": "sp", "poolq": "pool"}
NDMASEM = 8


class Op:
    __slots__ = ("eng", "fn", "deps", "signal", "sem", "val", "inc", "idx", "prewait")

    def __init__(self, eng, fn):
        self.eng = eng
        self.fn = fn
        self.deps = []
        self.signal = False
        self.sem = None
        self.val = None
        self.inc = 1
        self.idx = 0
        self.prewait = None


class Sched:
    def __init__(self):
        self.streams = {s: [] for s in ("pe", "act", "dve", "pool", "sp")}
        self.last_w = {}
        self.readers = {}
        self.dma_ops = {q: [] for q in DMAQ}

    def add(self, eng, fn, reads=(), writes=(), after=()):
        op = Op(eng, fn)
        raw = set()
        other = set()
        for k in reads:
            w = self.last_w.get(k)
            if w is not None:
                raw.add(w)
            if isinstance(k, tuple) and k[0] == "ps":
                for r in self.readers.get(k, ()):
                    if r.eng != eng:
                        other.add(r)
        for k in list(writes) + list(after):
            w = self.last_w.get(k)
            if w is not None:
                other.add(w)
            for r in self.readers.get(k, ()):
                other.add(r)
        deps = set()
        for d in raw:
            if d.eng == "pe" and eng == "pe":
                continue
            deps.add(d)
        for d in other:
            if d.eng == "pe" and eng == "pe":
                continue
            deps.add(d)
        for d in deps:
            d.signal = True
        op.deps = list(deps)
        if eng in DMAQ:
            op.signal = True
            k = len(self.dma_ops[eng])
            op.idx = k
            if k >= NDMASEM:
                op.prewait = self.dma_ops[eng][k - NDMASEM]
            self.dma_ops[eng].append(op)
        for k in reads:
            self.readers.setdefault(k, []).append(op)
        for k in writes:
            self.last_w[k] = op
            self.readers[k] = []
        self.streams[STREAM_OF[eng]].append(op)
        return op

    def emit(self, nc, final_wait_ops=()):
        with contextlib.ExitStack() as st:
            sems = {}
            for e in COMPUTE:
                sems[e] = st.enter_context(nc.semaphore("s_" + e))
            for q in DMAQ:
                sems[q] = [st.enter_context(nc.semaphore("d_%s%d" % (q, i))) for i in range(NDMASEM)]
            cnt = {e: 0 for e in COMPUTE}
            for ops in self.streams.values():
                for op in ops:
                    if op.eng in COMPUTE:
                        if op.signal:
                            cnt[op.eng] += 1
                            op.sem = sems[op.eng]
                            op.val = cnt[op.eng]
                            op.inc = 1
                    else:
                        op.sem = sems[op.eng][op.idx % NDMASEM]
                        op.val = 16 * (op.idx // NDMASEM + 1)
                        op.inc = 16
            block = st.enter_context(nc.Block())
            engines = {"pe": "tensor", "act": "scalar", "dve": "vector", "pool": "gpsimd", "sp": "sync"}

            def make(stream):
                ops = self.streams[stream]

                def body(eng):
                    waited = {}
                    for op in ops:
                        dl = list(op.deps)
                        if op.prewait is not None:
                            dl.append(op.prewait)
                        mx = {}
                        for d in dl:
                            key = id(d.sem)
                            if waited.get(key, 0) >= d.val:
                                continue
                            if key not in mx or mx[key][1] < d.val:
                                mx[key] = (d.sem, d.val)
                        wl = list(mx.items())
                        for key, (sm, v) in wl[:-1]:
                            waited[key] = v
                            eng.wait_ge(sm, v)
                        ins = op.fn(eng)
                        if wl:
                            key, (sm, v) = wl[-1]
                            waited[key] = v
                            ins._wait_ge(sm, v)
                        if op.signal:
                            ins.then_inc(op.sem, op.inc)
                    if stream == "sp":
                        for d in final_wait_ops:
                            eng.wait_ge(d.sem, d.val)

                return body

            for stream, attr in engines.items():
                getattr(block, attr)(make(stream))


def MM(out, lhsT, rhs, start=True, stop=True, skip=False):
    if skip:
        return lambda g: g.matmul(out, lhsT, rhs, start=start, stop=stop, skip_group_check=True)
    return lambda g: g.matmul(out, lhsT, rhs, start=start, stop=stop)


def TR(out, in_, ident):
    return lambda g: g.transpose(out, in_, ident)


def ACT(out, in_, func, **kw):
    return lambda g: g.activation(out=out, in_=in_, func=func, **kw)


def TS(out, in0, s1, s2, op0, op1=None):
    if op1 is None:
        return lambda g: g.tensor_scalar(out, in0, s1, None, op0)
    return lambda g: g.tensor_scalar(out, in0, s1, s2, op0, op1)


def TT(out, in0, in1, op):
    return lambda g: g.tensor_tensor(out, in0, in1, op)


def STT(out, in0, scalar, in1, op0, op1):
    return lambda g: g.scalar_tensor_tensor(out, in0, scalar, in1, op0, op1)


def CP(out, in_):
    return lambda g: g.tensor_copy(out, in_)


def MEMSET(ap, v):
    return lambda g: g.memset(ap, v)


def DMA(out, in_):
    return lambda g: g.dma_start(out=out, in_=in_)


def RED(out, in_, op=ALU.add):
    return lambda g: g.tensor_reduce(out, in_, AX.X, op)


def RECIP(out, in_):
    return lambda g: g.reciprocal(out, in_)


def build_nc(debug=False):
    nc = bass.Bass("TRN2", target_bir_lowering=False)
    dt = nc.dram_tensor
    x_d = dt("x", [S, D], F32, kind="ExternalInput").ap()
    cfm_d = dt("cfm", [P, 8], F32, kind="ExternalInput").ap()
    adaw_d = dt("ada_w", [D, 6 * D], F32, kind="ExternalInput").ap()
    adab_d = dt("ada_b", [P, 6 * D], F32, kind="ExternalInput").ap()
    gfm_d = dt("gfm", [P, 16], F32, kind="ExternalInput").ap()
    win_d = dt("w_in", [D, 4096], F32, kind="ExternalInput").ap()
    lam_d = dt("lam", [P, 4, 64], F32, kind="ExternalInput").ap()
    subg_d = dt("subg", [P, 128], F32, kind="ExternalInput").ap()
    wout_d = dt("w_out", [D, D], F32, kind="ExternalInput").ap()
    w1_d = dt("w_ff1", [D, DFF], F32, kind="ExternalInput").ap()
    w2_d = dt("w_ff2", [DFF, D], F32, kind="ExternalInput").ap()
    fng_d = dt("fng", [P, D], F32, kind="ExternalInput").ap()
    cb_d = dt("cb", [P, 528], BF16, kind="ExternalInput").ap()
    ind_d = dt("ind", [P, 2048], BF16, kind="ExternalInput").ap()
    cf_d = dt("cf", [P, 128], F32, kind="ExternalInput").ap()
    cs_d = dt("cs", [P, 4096], F32, kind="ExternalInput").ap()
    out_d = dt("out", [S, D], F32, kind="ExternalOutput").ap()
    dbg = {}
    if debug:
        dbg["hT"] = dt("dbg_hT", [P, 8 * S], BF16, kind="ExternalOutput").ap()
        dbg["qk"] = dt("dbg_qk", [P, 8 * S], BF16, kind="ExternalOutput").ap()
        dbg["V"] = dt("dbg_V", [P, 16512], BF16, kind="ExternalOutput").ap()
        dbg["OT"] = dt("dbg_OT", [P, 8 * S], BF16, kind="ExternalOutput").ap()
        dbg["x1"] = dt("dbg_x1", [P, 16 * D], F32, kind="ExternalOutput").ap()
        dbg["mod"] = dt("dbg_mod", [P, 64], F32, kind="ExternalOutput").ap()
        dbg["h1"] = dt("dbg_h1", [P, 8 * S], BF16, kind="ExternalOutput").ap()
        dbg["qk0"] = dt("dbg_qk0", [P, 8 * S], BF16, kind="ExternalOutput").ap()

    Sd = Sched()
    import os as _os
    _stop = _os.environ.get("KSTOP", "")
    _cut = [False]

    def add(*a, **k):
        if _cut[0]:
            return None
        return Sd.add(*a, **k)

    def mark(name):
        if name == _stop:
            _cut[0] = True
    with contextlib.ExitStack() as st:
        sb = lambda name, shape, dtype: st.enter_context(nc.sbuf_tensor(name, shape, dtype))
        RA = sb("RA", [P, 16384], BF16)
        RDK = sb("RDK", [P, 8192 + 8320], BF16)
        RB = sb("RB", [P, 16384], BF16)
        RC = sb("RC", [P, 16384], BF16)
        RE = sb("RE", [P, 16384], BF16)
        RF = sb("RF", [P, 8192], BF16)
        RG = sb("RG", [P, 2048], F32)
        cb = sb("cbs", [P, 528], BF16)
        ind = sb("inds", [P, 2048], BF16)
        identf = sb("identf", [P, 128], F32)
        gm_bc = sb("gm_bc", [P, D], F32)
        gf_bc = sb("gf_bc", [P, D], F32)
        small = sb("small", [P, 256], F32)
        PS = st.enter_context(nc.psum_tensor("PS", [P, 4096], F32))

        ident = cb[:, 0:128]
        trineg = cb[:, 128:256]
        masku = cb[:, 256:384]
        vsel = cb[:, 384:528]

        def wsel_i(i):
            return vsel[:, 16 - i:16 - i + 128]
        indv = ind[:, :].rearrange("p (i s) -> p i s", i=16)

        def bank(b, n=1):
            return PS[:, b * 512:(b + n) * 512]

        def bankbf(b):
            return PS[:, b * 512:(b + 1) * 512].bitcast(BF16)

        qz = RA[:, :].rearrange("p (c t) -> p c t", c=8)
        ka = RDK[:, 0:8192].rearrange("p (c t) -> p c t", c=4)
        hT = RB[:, :].rearrange("p (c t) -> p c t", c=8)
        OT = RC[:, :].rearrange("p (c t) -> p c t", c=8)
        xn_bf = RC[:, 0:8192].rearrange("p (j f) -> p j f", j=8)
        cs_sb = RC[:, 8192:16384].bitcast(F32)
        cosT = cs_sb[:, 0:2048]
        sinT = cs_sb[:, 2048:4096]
        xs = RA[:, :].bitcast(F32).rearrange("p (j f) -> p j f", j=8)
        dvaug = RDK[:, 8192:16512].rearrange("p (t h d) -> p t h d", t=16, h=4)
        svb = RDK[:, 8192:16384].rearrange("p (t f) -> p t f", t=16)
        uT = RDK[:, 0:16384].rearrange("p (c t) -> p c t", c=8)
        wslot = [RE[:, s * 4096:(s + 1) * 4096].rearrange("p (k n) -> p k n", k=8) for s in range(4)]
        woutg = RE[:, 0:8192].rearrange("p (k n) -> p k n", k=8)
        w2slot = [RE[:, 8192 + s * 4096:8192 + (s + 1) * 4096].rearrange("p (k n) -> p k n", k=4) for s in range(2)]
        xr_lo = RA[:, :].bitcast(F32).rearrange("p (j f) -> p j f", j=8)
        xr_hi = RB[:, :].bitcast(F32).rearrange("p (j f) -> p j f", j=8)

        def xr(tt):
            return xr_lo[:, tt, :] if tt < 8 else xr_hi[:, tt - 8, :]

        lneg = RB[:, 0:8192].rearrange("p (i t) -> p i t", i=16)
        stage = [RG[:, 0:1024], RG[:, 1024:2048]]
        cfm = small[:, 0:8]
        cact = small[:, 8:16]
        gfm = small[:, 16:32]
        modT = small[:, 32:64]
        a_m = small[:, 64:72]
        a_f = small[:, 72:80]
        ssq = small[:, 80:96]
        rstd = small[:, 96:112]
        lamw = small[:, 112:116]
        neglam = small[:, 116:117]
        ssq2 = small[:, 120:136]
        rstd2 = small[:, 136:152]
        ssq3 = small[:, 152:168]
        rstd3 = small[:, 168:184]
        dsm = small[:, 184:256]
        crep = sb("crep", [P, 8, 128], BF16)
        lam_sb = sb("lam_sb", [P, 4, 64], F32)
        subgs = sb("subgs", [P, 128], F32)
        fngs = gm_bc
        junk_t = sb("junk", [P, 2 * D], BF16)
        junk = junk_t[:, 0:D]
        junk_alt = junk_t[:, D:2 * D]

        add("sp", DMA(cb[:, :], cb_d), writes=["cb"])
        add("sp", DMA(small[:, 0:8], cfm_d), writes=["cfm"])
        add("sp", DMA(small[:, 16:32], gfm_d), writes=["gfm"])
        add("sp", DMA(identf[:, :], cf_d), writes=["identf"])
        add("sp", DMA(ind[:, :], ind_d), writes=["ind"])
        add("sp", DMA(lam_sb[:, :, :], lam_d), writes=["lam_sb"])
        add("sp", DMA(subgs[:, :], subg_d), writes=["subgs"])
        add("pool", MEMSET(small[:, 80:96], 0.0), writes=["xn_ss"])
        add("pool", MEMSET(small[:, 120:136], 0.0), writes=["xn2_ss"])
        add("pool", MEMSET(small[:, 152:168], 0.0), writes=["ssq3"])
        add("pool", MEMSET(small[:, 184:256], 0.0), writes=[("dsm", k_, 3) for k_ in range(2)])

        add("act", ACT(cact, cfm, AF.Silu), reads=["cfm"], writes=["cact"])
        add("dve", CP(crep[:, :, :], cact.unsqueeze(2).to_broadcast([P, 8, 128])), reads=["cact"], writes=["crep"])
        add("dve", TT(lam_sb[:, 0:2, :], lam_sb[:, 0:4:2, :], lam_sb[:, 1:4:2, :], ALU.mult), reads=["lam_sb"], writes=["lam_sb2"])
        add("dve", RED(lamw[:, 0:2], lam_sb[:, 0:2, :]), reads=["lam_sb2"], writes=["lamw"])
        add("act", ACT(lamw[:, 2:4], lamw[:, 0:2], AF.Exp), reads=["lamw"], writes=["lamw2"])
        add("dve", STT(neglam, lamw[:, 3:4], -LAMBDA_INIT, lamw[:, 2:3], ALU.add, ALU.subtract), reads=["lamw2"], writes=["neglam"])
        add("dve", TS(subgs[:, :], subgs[:, :], (1.0 - LAMBDA_INIT) * math.sqrt(128.0), None, ALU.mult), reads=["subgs"], writes=["subgs"])

        for t_ in range(8):
            add("sp", DMA(xs[:, t_ % 8, :], x_d[t_ * 128:(t_ + 1) * 128, :]), writes=[("xs", t_ % 8)])
        add("sp", DMA(cs_sb, cs_d), writes=["cs"])
        adaw_v = adaw_d.rearrange("(k p) n -> p k n", p=P)
        wcount = [0]

        def wslot_load(src_ap, tag):
            s = wcount[0] % 4
            wcount[0] += 1
            add("poolq", DMA(wslot[s][:, :, :], src_ap), writes=[("w", s)])
            return s

        adab_slots = [RF[:, 0:1024].bitcast(F32), RF[:, 1024:2048].bitcast(F32)]
        tmp_mod_e = RF[:, 2048:3072].bitcast(F32)
        tmp_mod2_e = RF[:, 3072:4096].bitcast(F32)
        piece_order = [2, 3, 0, 1, 4, 5, 6, 7, 8, 9, 10, 11]
        modcol = {0: 0, 1: 4, 2: 8, 3: 12, 6: 16, 7: 20, 8: 24, 9: 28}
        def ada_piece(n_, pc, late=None):
            if late is None:
                s = wslot_load(adaw_v[:, :, pc * 512:(pc + 1) * 512], "ada")
                ab = adab_slots[n_ % 2]
                abk = ("adab", n_ % 2)
                add("sp", DMA(ab, adab_d[:, pc * 512:(pc + 1) * 512]), writes=[abk])
                b = n_ % 2
                tmp_mod, tmp_mod2, tk, tk2, aft = tmp_mod_e, tmp_mod2_e, "tmp_mod", "tmp_mod2", []
            else:
                s, ab, abk, b = late["slot"], late["ab"], late["abk"], 7
                tmp_mod, tmp_mod2, tk, tk2, aft = late["tmp"], late["tmp2"], "tmp_modB", "tmp_mod2B", late["after"]
                if late["stage"] == "load":
                    add("poolq", DMA(wslot[s][:, :, :], adaw_v[:, :, pc * 512:(pc + 1) * 512]), writes=[("w", s)])
                    add("sp", DMA(ab, adab_d[:, pc * 512:(pc + 1) * 512]), writes=[abk], after=aft)
                    return
            for kc in range(8):
                add("pe", MM(bank(b), crep[:, kc, :], wslot[s][:, kc, :], start=(kc == 0), stop=(kc == 7)),
                    reads=["crep", ("w", s)], writes=[("ps", b)])
            if pc in (4, 5):
                add("dve", TT(gm_bc[:, (pc - 4) * 512:(pc - 3) * 512], bank(b), ab, ALU.add),
                    reads=[("ps", b), abk], writes=[("gm", pc - 4)])
            elif pc in (10, 11):
                add("dve", TT(gf_bc[:, (pc - 10) * 512:(pc - 9) * 512], bank(b), ab, ALU.add),
                    reads=[("ps", b), abk], writes=[("gf", pc - 10)])
            else:
                add("dve", TT(tmp_mod, bank(b), ab, ALU.add), reads=[("ps", b), abk], writes=[tk], after=aft)
                add("dve", TT(tmp_mod2.rearrange("p (a b) -> p a b", a=4), tmp_mod.rearrange("p (a b) -> p a b", a=4),
                              identf[:, :].unsqueeze(1).to_broadcast([P, 4, 128]), ALU.mult),
                    reads=[tk, "identf"], writes=[tk2], after=aft)
                c0 = modcol[pc]
                add("dve", RED(modT[:, c0:c0 + 4], tmp_mod2.rearrange("p (a b) -> p a b", a=4)),
                    reads=[tk2], writes=[("modT", c0)])
            if pc == 3:
                add("dve", STT(a_m, modT[:, 8:16], 1.0, gfm[:, 0:8], ALU.add, ALU.mult),
                    reads=[("modT", 8), ("modT", 12), "gfm"], writes=["a_m"])
            if pc == 9:
                add("dve", STT(a_f, modT[:, 24:32], 1.0, gfm[:, 8:16], ALU.add, ALU.mult),
                    reads=[("modT", 24), ("modT", 28), "gfm"], writes=["a_f"])

        for n_, pc in enumerate(piece_order[:4]):
            ada_piece(n_, pc)

        mark("p0")
        def norm_phase(src_tile, ssq_, rstd_, xnbuf, aff_a, aff_sh, dstT, src_reads, xn_key, dst_key, banks, junk, extra_reads, xn_after=(), dst_after=(), pre_group=None, groups=None):
            bi = [0]
            for tt in [4 * g_ + j_ for g_ in (groups if groups is not None else range(NU)) for j_ in range(4)]:
                U = tt // 4
                slot = tt % 8
                if pre_group is not None and tt % 4 == 0:
                    pre_group(U)
                add("act", ACT(junk if tt % 2 == 0 else junk_alt, src_tile(tt), AF.Square, accum_out=ssq_[:, tt:tt + 1]),
                    reads=src_reads(tt) + [xn_key + "_ss"], writes=[(xn_key + "_ssq", tt), ("junk", tt % 2)])
                if tt % 4 == 3:
                    g0 = 4 * U
                    mark("n_sq")
                    add("act", ACT(rstd_[:, g0:g0 + 4], ssq_[:, g0:g0 + 4], AF.Ln, bias=D * EPS),
                        reads=[(xn_key + "_ssq", g0 + j) for j in range(4)], writes=[(xn_key + "_ln", U)])
                    add("act", ACT(rstd_[:, g0:g0 + 4], rstd_[:, g0:g0 + 4], AF.Exp, scale=-0.5),
                        reads=[(xn_key + "_ln", U)], writes=[(xn_key + "_rstd", U)])
                    mark("n_ln")
                    for j in range(4):
                        t_ = g0 + j
                        add("dve", TS(xnbuf[:, t_ % 8, :], src_tile(t_), rstd_[:, t_:t_ + 1], 32.0, ALU.mult, ALU.mult),
                            reads=src_reads(t_) + [(xn_key + "_rstd", U)], writes=[(xn_key, t_ % 8)], after=xn_after)
                    mark("n_norm")
                    for kp in range(4):
                        b = banks[bi[0] % len(banks)]
                        bi[0] += 1
                        pb = bankbf(b)
                        for k2 in range(2):
                            kc = kp * 2 + k2
                            for j in range(4):
                                add("pe", TR(pb[:, k2 * 512 + j * 128:k2 * 512 + (j + 1) * 128],
                                             xnbuf[:, (U % 2) * 4 + j, kc * 128:(kc + 1) * 128], ident),
                                    reads=[(xn_key, (U % 2) * 4 + j), "cb"], writes=[("ps", b)])
                        mark("n_tr")
                        for k2 in range(2):
                            kc = kp * 2 + k2
                            dst = dstT[:, kc, U * 512:(U + 1) * 512]
                            src = pb[:, k2 * 512:(k2 + 1) * 512]
                            if kp % 2 == 0:
                                add("act", ACT(dst, src, AF.Identity, scale=aff_a[:, kc:kc + 1], bias=aff_sh[:, kc:kc + 1]),
                                    reads=[("ps", b)] + extra_reads, writes=[(dst_key, kc, U)], after=dst_after)
                                mark("n_ea")
                            else:
                                add("dve", TS(dst, src, aff_a[:, kc:kc + 1], aff_sh[:, kc:kc + 1], ALU.mult, ALU.add),
                                    reads=[("ps", b)] + extra_reads, writes=[(dst_key, kc, U)], after=dst_after)

        win_v = win_d.rearrange("(k p) n -> p k n", p=P)
        pre_slots = {}
        for pcs_ in ((0, 6), (1, 7)):
            pre_slots[pcs_] = (wslot_load(win_v[:, :, pcs_[0] * 512:(pcs_[0] + 1) * 512], "win"),
                               wslot_load(win_v[:, :, pcs_[1] * 512:(pcs_[1] + 1) * 512], "win"))

        def load_x_group(U):
            groups = [] if U == 0 else ([U + 1] if U + 1 < NU else [])
            for g_ in groups:
                for j in range(4):
                    t_ = 4 * g_ + j
                    add("sp", DMA(xs[:, t_ % 8, :], x_d[t_ * 128:(t_ + 1) * 128, :]), writes=[("xs", t_ % 8)])

        norm_phase(lambda tt: xs[:, tt % 8, :], ssq, rstd, xn_bf, a_m, modT[:, 0:8], hT,
                   lambda tt: [("xs", tt % 8)], "xn", "hT", [0, 1, 2, 3], junk, ["a_m", ("modT", 0), ("modT", 4)],
                   pre_group=load_x_group)

        mark("p1")
        t1s = [RF[:, 5120 + i * 1024:5120 + (i + 1) * 1024].bitcast(F32) for i in range(2)]
        t2s = [RF[:, 7168 + i * 512:7168 + (i + 1) * 512] for i in range(0)]
        t2a = [RG[:, 0:512], RG[:, 512:1024], RG[:, 1024:1536], RG[:, 1536:2048]]
        ropec = [0]
        pbk = [4, 5, 6, 7]
        xs_keys = [("xs", i) for i in range(8)]
        for c_ in range(4):
            add("pool", MEMSET(qz[64:128, 2 * c_, :], 0.0), after=xs_keys, writes=[("qzero", 2 * c_)])
            add("dve", MEMSET(qz[0:64, 2 * c_ + 1, :], 0.0), after=xs_keys, writes=[("qzero", 2 * c_ + 1)])
        for (pa, pb_, dest, dkey, scale) in [(0, 6, None, "qz", 0.125), (1, 7, ka, "ka", 1.0)]:
            sa, sb_ = pre_slots[(pa, pb_)]
            for cc in range(4):
                for U in range(NU):
                    r = ropec[0]
                    ropec[0] += 1
                    ba = pbk[(2 * r) % 4]
                    bb = pbk[(2 * r + 1) % 4]
                    for kc in range(8):
                        add("pe", MM(bank(ba), wslot[sa][:, kc, cc * 128:(cc + 1) * 128], hT[:, kc, U * 512:(U + 1) * 512],
                                     start=(kc == 0), stop=(kc == 7)),
                            reads=[("w", sa), ("hT", kc, U)], writes=[("ps", ba)])
                    for kc in range(8):
                        add("pe", MM(bank(bb), wslot[sb_][:, kc, cc * 128:(cc + 1) * 128], hT[:, kc, U * 512:(U + 1) * 512],
                                     start=(kc == 0), stop=(kc == 7)),
                            reads=[("w", sb_), ("hT", kc, U)], writes=[("ps", bb)])
                    t1 = t1s[r % 2]
                    t2 = t2a[r % 2]
                    add("dve", STT(t1, bank(ba), scale, cosT[:, U * 512:(U + 1) * 512], ALU.mult, ALU.mult),
                        reads=[("ps", ba), "cs"], writes=[("t1", r % 2)])
                    add("dve", STT(t2, bank(bb), scale, sinT[:, U * 512:(U + 1) * 512], ALU.mult, ALU.mult),
                        reads=[("ps", bb), "cs"], writes=[("t2", r % 2)])
                    if dest is None:
                        for m_ in range(2):
                            add("pool", TT(qz[m_ * 64:(m_ + 1) * 64, 2 * cc + m_, U * 512:(U + 1) * 512],
                                           t1[m_ * 64:(m_ + 1) * 64, :], t2[m_ * 64:(m_ + 1) * 64, :], ALU.add),
                                reads=[("t1", r % 2), ("t2", r % 2), ("qzero", 2 * cc + m_)], writes=[("qz", 2 * cc + m_, U)])
                    else:
                        add("pool", TT(dest[:, cc, U * 512:(U + 1) * 512], t1, t2, ALU.add),
                            reads=[("t1", r % 2), ("t2", r % 2)], writes=[(dkey, cc, U)])
        add("pool", MEMSET(dvaug[:, :, :, 128:130], 1.0), writes=["dvones"])
        sv_ = wslot_load(win_v[:, :, 2 * 512:3 * 512], "win")
        for tt in range(NT):
            b = pbk[tt % 4]
            for kc in range(8):
                add("pe", MM(bank(b), hT[:, kc, tt * 128:(tt + 1) * 128], wslot[sv_][:, kc, :], start=(kc == 0), stop=(kc == 7)),
                    reads=[("w", sv_), ("hT", kc, tt // 4)], writes=[("ps", b)])
            eng = "act" if tt % 2 == 0 else "dve"
            src = bank(b).rearrange("p (h d) -> p h d", h=4)
            if eng == "act":
                add("act", ACT(dvaug[:, tt, :, 0:128], src, AF.Copy), reads=[("ps", b), "dvones"], writes=[("dv", tt)])
            else:
                add("dve", CP(dvaug[:, tt, :, 0:128], src), reads=[("ps", b), "dvones"], writes=[("dv", tt)])

        if debug:
            add("sp", DMA(dbg["h1"], RB[:, :]), reads=[("hT", kc, U) for kc in range(8) for U in range(NU)], writes=["dbgh1"])
        s_sq = wslot_load(win_v[:, :, 3 * 512:4 * 512], "win")
        s_sk = wslot_load(win_v[:, :, 4 * 512:5 * 512], "win")
        s_sv = wslot_load(win_v[:, :, 5 * 512:6 * 512], "win")

        mark("p1a")
        et = [RF[:, k_ * 512:(k_ + 1) * 512].rearrange("p (m t) -> p m t", m=2) for k_ in range(4)]
        od = [RF[:, 2048 + k_ * 256:2048 + (k_ + 1) * 256].rearrange("p (j f) -> p j f", j=2) for k_ in range(2)]
        tmpd = [RF[:, 3072:3328].bitcast(F32), RF[:, 3328:3584].bitcast(F32)]
        od32 = [RF[:, 3584 + i * 256:3840 + i * 256].bitcast(F32) for i in range(4)]
        phase_rf = ["tmp_mod", "tmp_mod2", ("adab", 0), ("adab", 1)]
        NSL = 4
        DIST = 3
        djobs = []
        pend = []
        rnd = 0
        round_order = []
        for hA, hB in ((0, 1), (2, 3)):
            for k_ in range(8):
                round_order.append((hA, k_))
                round_order.append((hB, 7 - k_))
        for (h, R) in round_order:
                for i in range(2 * R + 2):
                    djobs.append(("t", h, R, i, rnd % 2))
                    for p_ in pend:
                        p_[0] -= 1
                    while pend and pend[0][0] <= 0:
                        djobs.append(pend.pop(0)[1])
                pend.append([5, ("f3", h, R, 0, rnd % 2)])
                rnd += 1
        for _ in range(DIST + 2):
            djobs.append(("nop", 0, 0, 0, 0))
        for p_ in pend:
            djobs.append(p_[1])

        def d_s1(job, sl, first):
            kind, h, R, i, fs = job
            if kind == "nop":
                return
            if kind == "f3":
                sm = dsm[:, fs * 16:fs * 16 + 16]
                for j in range(2):
                    add("dve", (lambda o_, i_, a_: (lambda g: g.scalar_tensor_tensor(o_, i_, 1.0, i_, ALU.mult, ALU.mult, accum_out=a_)))(
                        junk_t[:, (fs * 2 + j) * 128:(fs * 2 + j + 1) * 128], od32[fs * 2 + j], sm[:, 8 + j:9 + j]),
                        reads=[("od32", fs * 2 + j), ("dsm", fs, 3)], writes=[("dsm", fs, 4, j), ("junk", 0)])
                add("act", ACT(sm[:, 12:14], sm[:, 8:10], AF.Ln, bias=128.0 * EPS), reads=[("dsm", fs, 4, j) for j in range(2)], writes=[("dsm", fs, 5)])
                add("act", ACT(sm[:, 12:14], sm[:, 12:14], AF.Exp, scale=-0.5), reads=[("dsm", fs, 5)], writes=[("dsm", fs, 6)])
                for j in range(2):
                    add("dve", STT(od[fs][:, j, :], od32[fs * 2 + j], sm[:, 12 + j:13 + j], subgs[:, :], ALU.mult, ALU.mult),
                        reads=[("od32", fs * 2 + j), ("dsm", fs, 6), "subgs"], writes=[("od", fs, j)])
                add("dve", MEMSET(sm[:, 8:10], 0.0), reads=[("dsm", fs, 5)], writes=[("dsm", fs, 3)])
                return
            t0 = max(R * 256, i * 128)
            w = (R + 1) * 256 - t0
            for m in range(2):
                add("pe", MM(bank(sl)[:, m * 256:m * 256 + w], ka[:, h, i * 128:(i + 1) * 128], qz[:, 2 * h + m, t0:t0 + w]),
                    reads=[("ka", h, i // 4), ("qzero", 2 * h + m), ("qz", 2 * h + m, t0 // 512)], writes=[("ps", sl)])
            scv = bank(sl).rearrange("p (m t) -> p m t", m=2)
            add("act", ACT(et[sl][:, :, 0:w], scv[:, :, 0:w], AF.Exp),
                reads=[("ps", sl)], writes=[("et", sl)], after=(phase_rf + [("t1", 0), ("t1", 1)] if first else []))
            if i >= 2 * R:
                add("pool", MEMSET(et[sl][64:128, :, 0:64], 0.0), reads=[], writes=[("et", sl)])

        def d_s2(job, sl):
            kind, h, R, i, fs = job
            if kind == "nop":
                return
            if kind == "f3":
                pb = bankbf(sl)
                for j in range(2):
                    add("pe", TR(pb[:, j * 128:(j + 1) * 128], od[fs][:, j, :], ident),
                        reads=[("od", fs, j), "cb"], writes=[("ps", sl)])
                add("dve", CP(OT[:, h, R * 256:(R + 1) * 256], pb[:, 0:256]),
                    reads=[("ps", sl)], after=["cs"] + [("xn", s_) for s_ in range(8)], writes=[("OT", h, R)])
                return
            t0 = max(R * 256, i * 128)
            for T in range(max(2 * R, i), 2 * R + 2):
                bacc = 4 + 2 * fs + (T % 2)
                c0 = T * 128 - t0
                for m in range(2):
                    add("pe", MM(bank(bacc)[:, m * 132:m * 132 + 129], et[sl][:, m, c0:c0 + 128], dvaug[:, i, h, 0:129],
                                 start=(i == 0 and m == 0), stop=(i == T), skip=True),
                        reads=[("et", sl), ("dv", i)], writes=[("ps", bacc)])
            if i != 2 * R + 1:
                return
            sm = dsm[:, fs * 16:fs * 16 + 16]
            ab0 = 4 + 2 * fs
            accs = [("ps", ab0), ("ps", ab0 + 1)]
            add("dve", RECIP(sm[:, 0:2], PS[:, ab0 * 512 + 128:(ab0 + 2) * 512:512]), reads=accs, writes=[("dsm", fs, 0)])
            add("dve", RECIP(sm[:, 4:6], PS[:, ab0 * 512 + 260:(ab0 + 2) * 512:512]), reads=accs, writes=[("dsm", fs, 1)])
            add("dve", TS(sm[:, 4:6], sm[:, 4:6], neglam, None, ALU.mult), reads=[("dsm", fs, 1), "neglam"], writes=[("dsm", fs, 1)])
            for j in range(2):
                bacc = ab0 + j
                acc0 = bank(bacc)[:, 0:128]
                acc1 = bank(bacc)[:, 132:260]
                o32 = od32[fs * 2 + j]
                add("dve", TS(tmpd[j], acc1, sm[:, 4 + j:5 + j], None, ALU.mult), reads=[("ps", bacc), ("dsm", fs, 1)], writes=[("tmpd", j)])
                add("dve", STT(o32, acc0, sm[:, j:j + 1], tmpd[j], ALU.mult, ALU.add),
                    reads=[("ps", bacc), ("dsm", fs, 0), ("tmpd", j)], writes=[("od32", fs * 2 + j)])

        hist = []
        for n_, job in enumerate(djobs + [None] * DIST):
            if job is not None:
                d_s1(job, n_ % NSL, n_ == 0)
            hist.append(job)
            if n_ >= DIST and hist[n_ - DIST] is not None:
                d_s2(hist[n_ - DIST], (n_ - DIST) % NSL)

        mark("p2a")
        pc_ = [0]
        dv_all = [("dv", t) for t in range(16)] + ["dvones"]
        for (sw, dest, dkey, scale) in [(s_sq, None, "qz", 0.125), (s_sk, ka, "ka", 1.0)]:
            for cc in range(4):
                for U in range(NU):
                    b = pc_[0] % 8
                    pc_[0] += 1
                    for kc in range(8):
                        add("pe", MM(bank(b), wslot[sw][:, kc, cc * 128:(cc + 1) * 128], hT[:, kc, U * 512:(U + 1) * 512],
                                     start=(kc == 0), stop=(kc == 7)),
                            reads=[("w", sw), ("hT", kc, U)], writes=[("ps", b)])
                    if dest is None:
                        for m_ in range(2):
                            dst_ = qz[m_ * 64:(m_ + 1) * 64, 2 * cc + m_, U * 512:(U + 1) * 512]
                            src_ = bank(b)[m_ * 64:(m_ + 1) * 64, :]
                            if pc_[0] % 2 == 0:
                                add("act", ACT(dst_, src_, AF.Identity, scale=scale), reads=[("ps", b)], writes=[("qz", 2 * cc + m_, U)])
                            else:
                                add("dve", TS(dst_, src_, scale, None, ALU.mult), reads=[("ps", b)], writes=[("qz", 2 * cc + m_, U)])
                    elif pc_[0] % 2 == 0:
                        add("act", ACT(dest[:, cc, U * 512:(U + 1) * 512], bank(b), AF.Identity, scale=scale),
                            reads=[("ps", b)], writes=[(dkey, cc, U)])
                    else:
                        add("dve", TS(dest[:, cc, U * 512:(U + 1) * 512], bank(b), scale, None, ALU.mult),
                            reads=[("ps", b)], writes=[(dkey, cc, U)])
        for tt in range(NT):
            b = pc_[0] % 8
            pc_[0] += 1
            for kc in range(8):
                add("pe", MM(bank(b), hT[:, kc, tt * 128:(tt + 1) * 128], wslot[s_sv][:, kc, :], start=(kc == 0), stop=(kc == 7)),
                    reads=[("w", s_sv), ("hT", kc, tt // 4)], writes=[("ps", b)])
            if tt % 2 == 0:
                add("act", ACT(svb[:, tt, :], bank(b), AF.Copy), reads=[("ps", b)], after=dv_all, writes=[("sv", tt)])
            else:
                add("dve", CP(svb[:, tt, :], bank(b)), reads=[("ps", b)], after=dv_all, writes=[("sv", tt)])

        stc = [0]

        def wout_prep(kcs, fin_):
            for kc in kcs:
                ss_ = stc[0] % 2
                stc[0] += 1
                add("sp", DMA(stage[ss_], wout_d[kc * 128:(kc + 1) * 128, :]), writes=[("stage", ss_), ("t2", 0), ("t2", 1)])
                add("pool", TT(woutg[:, kc, :], stage[ss_], gm_bc[:, :], ALU.mult),
                    reads=[("stage", ss_), ("gm", 0), ("gm", 1)], writes=[("w", 0), ("w", 1)] if kc == 0 else [("woutg", kc)])
            if fin_:
                add("sp", DMA(fngs[:, :], fng_d), writes=["fngs"], after=[("gm", 0), ("gm", 1)])
                add("pool", TS(fngs[:, :], fngs[:, :], 32.0, None, ALU.mult), reads=["fngs"], writes=["fngs"])

        mark("p1b")
        etmp = [RF[:, 4096:5120].bitcast(F32), RF[:, 5120:6144].bitcast(F32)]
        at = [RF[:, 6144:6656], RF[:, 6656:7168], RF[:, 7168:7680]]
        negr = [RF[:, 7680:8192], RF[:, 0:512]]
        hT_keys = [("hT", kc, U) for kc in range(8) for U in range(NU)]
        units = [(U, h) for U in range(NU) for h in range(8)]
        BB = 2
        sbc = {"za": 0, "zb": 0, "at": 0}

        def geom(U, i):
            t0 = max(U * 512, i * 128)
            w = (U + 1) * 512 - t0
            return t0, w, t0 - U * 512

        def qk_reads(cc, U, i, t0, h_):
            return [("ka", cc, i // 4)] + [("qz", h_, uu) for uu in range(t0 // 512, U + 1)]

        def A1(un, i, st):
            U, h = units[un]
            cc, r0 = h // 2, (h % 2) * 64
            t0, w, c0 = geom(U, i)
            b = sbc["za"] % 2
            sbc["za"] += 1
            st["zab"] = b
            add("pe", MM(bank(b)[:, 0:w], ka[:, cc, i * 128:(i + 1) * 128], qz[:, h, t0:t0 + w]),
                reads=qk_reads(cc, U, i, t0, h), writes=[("ps", b)])
            add("act", ACT(bank(b)[:, 0:w], bank(b)[:, 0:w], AF.Exp), reads=[("ps", b)], writes=[("ps", b)])
            add("act", ACT(lneg[:, i, 0:w], bank(b)[:, 0:w], AF.Ln, bias=1.0), reads=[("ps", b)],
                after=(hT_keys if un < 8 else []), writes=[("lneg", i)])
            if i >= 4 * U:
                add("dve", TT(lneg[:, i, 0:128], lneg[:, i, 0:128], masku, ALU.mult), reads=[("lneg", i), "cb"], writes=[("lneg", i)])

        def A2(un, i):
            U, h = units[un]
            nI = 4 * U + 4
            t0, w, c0 = geom(U, i)
            add("pe", MM(bank(BB)[:, c0:c0 + w], wsel_i(i), lneg[:, i, 0:w], start=(i == 0), stop=(i == nI - 1)),
                reads=[("lneg", i), "cb"], writes=[("ps", BB)])
            if i == nI - 1:
                add("dve", CP(negr[un % 2][:, :], bank(BB)[:, :]), reads=[("ps", BB)], writes=[("negr", un % 2)],
                    after=([("et", 0), ("et", 1), ("et", 2), ("et", 3)] if un < 2 else []))

        def B1(un, i, st):
            U, h = units[un]
            cc, r0 = h // 2, (h % 2) * 64
            nI = 4 * U + 4
            t0, w, c0 = geom(U, i)
            b = 3 + (sbc["zb"] % 2)
            sbc["zb"] += 1
            add("pe", MM(bank(b)[:, 0:w], ka[:, cc, i * 128:(i + 1) * 128], qz[:, h, t0:t0 + w], start=True, stop=False),
                reads=qk_reads(cc, U, i, t0, h), writes=[("ps", b)])
            if i < nI - 1:
                add("pe", MM(bank(b)[:, 0:w], indv[:, i, :], negr[un % 2][:, c0:c0 + w], start=False, stop=False),
                    reads=[("negr", un % 2), "ind"], writes=[("ps", b)])
            add("pe", MM(bank(b)[:, 0:w], trineg, lneg[:, i, 0:w], start=False, stop=True),
                reads=[("lneg", i), "cb"], writes=[("ps", b)])
            k = sbc["at"] % 3
            sbc["at"] += 1
            st[("at", i)] = k
            add("act", ACT(at[k][:, 0:w], bank(b)[:, 0:w], AF.Exp), reads=[("ps", b)], writes=[("at", k)],
                after=([("t1", 0), ("t1", 1)] if un == 0 else []))
            if i >= 4 * U:
                add("dve", TT(at[k][:, 0:128], at[k][:, 0:128], masku, ALU.mult), reads=[("at", k), "cb"], writes=[("at", k)])

        def B2(un, i, st):
            U, h = units[un]
            cc, r0 = h // 2, (h % 2) * 64
            nI = 4 * U + 4
            t0, w, c0 = geom(U, i)
            ob = 5 + (un % 2)
            k = st[("at", i)]
            add("pe", MM(bank(ob)[:, c0:c0 + w], svb[:, i, cc * 128:(cc + 1) * 128], at[k][:, 0:w], start=(i == 0), stop=(i == nI - 1)),
                reads=[("at", k), ("sv", i)], writes=[("ps", ob)])
            if i == nI - 1:
                if False:
                    add("act", ACT(OT[r0:r0 + 64, 4 + cc, U * 512:(U + 1) * 512], bank(ob)[r0:r0 + 64, :], AF.Copy),
                        reads=[("ps", ob)], writes=[("OT", 4 + cc, U, h % 2)])
                else:
                    add("dve", CP(OT[r0:r0 + 64, 4 + cc, U * 512:(U + 1) * 512], bank(ob)[r0:r0 + 64, :]),
                        reads=[("ps", ob)], writes=[("OT", 4 + cc, U, h % 2)])

        nI0 = 4 * units[0][0] + 4
        stA = {}
        for t in range(nI0 + 1):
            if t < nI0:
                A1(0, t, stA)
            if t >= 1:
                A2(0, t - 1)
        late_pcs = piece_order[4:]
        diff_scratch = [("et", k_) for k_ in range(4)] + [("od", f_, j_) for f_ in range(2) for j_ in range(2)] + \
                       [("tmpd", 0), ("tmpd", 1)] + [("od32", j_) for j_ in range(4)]
        late_ab = [RF[:, 512:1536].bitcast(F32), RF[:, 4096:5120].bitcast(F32)]

        def late_cfg(k_, stage_):
            return dict(slot=2 + (k_ % 2), ab=late_ab[k_ % 2], abk=("adab2", k_ % 2), tmp=RF[:, 2048:3072].bitcast(F32),
                        tmp2=RF[:, 3072:4096].bitcast(F32), after=(diff_scratch if k_ < 2 else []), stage=stage_)

        for un in range(len(units)):
            if un % 2 == 0 and 10 <= un <= 10 + 2 * (len(late_pcs) - 1):
                k_ = (un - 10) // 2
                ada_piece(4 + k_, late_pcs[k_], late=late_cfg(k_, "compute"))
            if un % 2 == 0 and 8 <= un <= 8 + 2 * (len(late_pcs) - 1):
                k_ = (un - 8) // 2
                ada_piece(4 + k_, late_pcs[k_], late=late_cfg(k_, "load"))
            if 17 <= un <= 24:
                wout_prep([un - 17], un == 24)
            nIu = 4 * units[un][0] + 4
            nIn = 4 * units[un + 1][0] + 4 if un + 1 < len(units) else 0
            stB = {}
            for t in range(max(nIu, nIn) + 1):
                if t < nIu:
                    B1(un, t, stB)
                if t < nIn:
                    A1(un + 1, t, stA)
                if 1 <= t <= nIu:
                    B2(un, t - 1, stB)
                if 1 <= t <= nIn:
                    A2(un + 1, t - 1)


        mark("p2b")
        qk_keys = [("qz", c, U) for c in range(8) for U in range(4)] + [("qzero", c) for c in range(8)]
        ka_keys = [("ka", c, U) for c in range(4) for U in range(4)]
        rb_keys = [("lneg", i) for i in range(16)] + hT_keys
        OT_all = [("OT", c, R_) for c in range(4) for R_ in range(8)] + [("OT", 4 + c, U, k) for c in range(4) for U in range(4) for k in range(2)]
        wg_keys = [("w", 0), ("w", 1)] + [("woutg", kc) for kc in range(1, 8)]
        for tt in range(NT):
            add("sp", DMA(xr(tt), x_d[tt * 128:(tt + 1) * 128, :]),
                writes=[("xr", tt)], after=(["dbgqk"] if debug else []) + (qk_keys if tt < 8 else rb_keys))
        p3 = [0]

        def p3_group(U_):
          for tt in range(4 * U_, 4 * U_ + 4):
            for nh in range(2):
                b = p3[0] % 8
                p3[0] += 1
                for fc in range(8):
                    rk = [("OT", fc, tt // 2)] if fc < 4 else [("OT", fc, tt // 4, 0), ("OT", fc, tt // 4, 1)]
                    add("pe", MM(bank(b), OT[:, fc, tt * 128:(tt + 1) * 128], woutg[:, fc, nh * 512:(nh + 1) * 512], start=(fc == 0), stop=(fc == 7)),
                        reads=rk + wg_keys, writes=[("ps", b)])
                add("dve", TT(xr(tt)[:, nh * 512:(nh + 1) * 512], bank(b), xr(tt)[:, nh * 512:(nh + 1) * 512], ALU.add),
                    reads=[("ps", b), ("xr", tt)], writes=[("xr", tt)])
        mark("p3")
        xn2 = RF[:, :].rearrange("p (j f) -> p j f", j=8)
        h2T = RC[:, :].rearrange("p (c t) -> p c t", c=8)
        junkb = junk[:, :]
        rf_keys = [("at", 0), ("at", 1), ("at", 2), ("negr", 0), ("negr", 1), ("et", 0), ("et", 1), ("et", 2), ("et", 3),
                   ("tmpd", 0), ("tmpd", 1)] + [("od32", j) for j in range(4)]
        def n2_group(U_):
            ot_blk = [("OT", c, R_) for c in range(4) for R_ in (2 * U_, 2 * U_ + 1)] + \
                     [("OT", 4 + c, U_, k) for c in range(4) for k in range(2)]
            norm_phase(lambda tt: xr(tt), ssq2, rstd2, xn2, a_f, modT[:, 16:24], h2T,
                       lambda tt: [("xr", tt)], "xn2", "h2T", [0, 1, 2, 3], junkb,
                       ["a_f", ("modT", 16), ("modT", 20)],
                       xn_after=rf_keys + [("od", 0, j) for j in range(2)] + [("od", 1, j) for j in range(2)],
                       dst_after=ot_blk, groups=[U_])

        p3_group(0)
        p3_group(1)
        n2_group(0)
        p3_group(2)
        n2_group(1)
        p3_group(3)
        n2_group(2)
        n2_group(3)

        w1_v = w1_d.rearrange("(k p) n -> p k n", p=P)
        rbuf = [RF[:, i * 1024:(i + 1) * 1024].bitcast(F32) for i in range(4)]
        xn2_keys = [("xn2", s_) for s_ in range(8)]
        rc_ = [0]
        p4 = [0]
        for fb in range(4):
            ws1 = []
            for half in range(2):
                s = half
                add("poolq", DMA(wslot[s][:, :, :], w1_v[:, :, fb * 1024 + half * 512: fb * 1024 + (half + 1) * 512]),
                    reads=wg_keys if fb == 0 else [], writes=[("w", s)])
                ws1.append(s)
            for half in range(2):
                for fc in range(4):
                    ss_ = stc[0] % 2
                    stc[0] += 1
                    r_ = fb * 1024 + half * 512 + fc * 128
                    add("sp", DMA(stage[ss_], w2_d[r_:r_ + 128, :]), writes=[("stage", ss_)])
                    add("pool", TT(w2slot[half][:, fc, :], stage[ss_], gf_bc[:, :], ALU.mult),
                        reads=[("stage", ss_), ("gf", 0), ("gf", 1)], writes=[("w", 2 + half)] if fc == 0 else [("w2g", half, fc)])
            for half in range(2):
                for fc in range(4):
                    for U in range(NU):
                        b = p4[0] % 8
                        p4[0] += 1
                        for kc in range(8):
                            add("pe", MM(bank(b), wslot[ws1[half]][:, kc, fc * 128:(fc + 1) * 128], h2T[:, kc, U * 512:(U + 1) * 512],
                                         start=(kc == 0), stop=(kc == 7)),
                                reads=[("w", ws1[half]), ("h2T", kc, U)], writes=[("ps", b)])
                        rb_ = rc_[0] % 4
                        rc_[0] += 1
                        add("act", ACT(rbuf[rb_], bank(b), AF.Relu), reads=[("ps", b)], after=(xn2_keys if rc_[0] <= 4 else []), writes=[("rbuf", rb_)])
                        eng = "dve" if rc_[0] % 2 == 0 else "pool"
                        add(eng, TT(uT[:, half * 4 + fc, U * 512:(U + 1) * 512], rbuf[rb_], rbuf[rb_], ALU.mult),
                            reads=[("rbuf", rb_)], after=(ka_keys + [("dv", t) for t in range(16)] + [("sv", t) for t in range(16)] if fb == 0 else []),
                            writes=[("uT", half * 4 + fc, U)])
            for tt in range(NT):
                for nh in range(2):
                    b = p4[0] % 8
                    p4[0] += 1
                    for j in range(8):
                        half, fc = j // 4, j % 4
                        add("pe", MM(bank(b), uT[:, j, tt * 128:(tt + 1) * 128], w2slot[half][:, fc, nh * 512:(nh + 1) * 512],
                                     start=(j == 0), stop=(j == 7)),
                            reads=[("uT", j, tt // 4), ("w", 2 + half)] + [("w2g", half, f_) for f_ in range(1, 4)], writes=[("ps", b)])
                    add("dve", TT(xr(tt)[:, nh * 512:(nh + 1) * 512], bank(b), xr(tt)[:, nh * 512:(nh + 1) * 512], ALU.add),
                        reads=[("ps", b), ("xr", tt)], writes=[("xr", tt)])
                if fb == 3:
                    add("act", ACT(junk if tt % 2 == 0 else junk_alt, xr(tt), AF.Square, accum_out=ssq3[:, tt:tt + 1]),
                        reads=[("xr", tt), "ssq3"], writes=[("ssq3", tt), ("junk", tt % 2)])
                    if tt % 4 == 3:
                        g0 = tt - 3
                        add("act", ACT(rstd3[:, g0:g0 + 4], ssq3[:, g0:g0 + 4], AF.Ln, bias=D * EPS), reads=[("ssq3", g0 + j) for j in range(4)], writes=[("ln3", g0)])
                        add("act", ACT(rstd3[:, g0:g0 + 4], rstd3[:, g0:g0 + 4], AF.Exp, scale=-0.5), reads=[("ln3", g0)], writes=[("rstd3", g0)])
                        for t_ in range(g0, g0 + 4):
                            add("dve", STT(xr(t_), xr(t_), rstd3[:, t_:t_ + 1], fngs[:, :], ALU.mult, ALU.mult),
                                reads=[("xr", t_), ("rstd3", g0), "fngs"], writes=[("xr", t_)])
                            add("sp", DMA(out_d[t_ * 128:(t_ + 1) * 128, :], xr(t_)), reads=[("xr", t_)], writes=[("out", t_)])

        mark("p4")
        if debug:
            add("sp", DMA(dbg["hT"], RC[:, :]), reads=[("h2T", kc, U) for kc in range(8) for U in range(4)], writes=["dbghT"])
        _cut[0] = False
        Sd.emit(nc, final_wait_ops=list(Sd.dma_ops["sp"]))
    return nc


def _rope_tables_host():
    try:
        import jax
        import jax.numpy as jnp
        cpu = jax.devices("cpu")[0]
        with jax.default_device(cpu):
            inv = 1.0 / (10000.0 ** (jnp.arange(0, 64, 2, dtype=jnp.float32) / 64))
            ang = jnp.arange(S, dtype=jnp.float32)[:, None] * inv[None, :]
            ang = jnp.concatenate([ang, ang], axis=-1)
            cos = np.asarray(jnp.cos(ang), dtype=np.float32)
            sin = np.asarray(jnp.sin(ang), dtype=np.float32)
        if cos.shape == (S, 64) and np.isfinite(cos).all() and np.isfinite(sin).all():
            return cos, sin
    except Exception:
        pass
    inv = (1.0 / (np.float32(10000.0) ** (np.arange(0, 64, 2, dtype=np.float32) / np.float32(64)))).astype(np.float32)
    ang = np.arange(S, dtype=np.float32)[:, None] * inv[None, :]
    ang = np.concatenate([ang, ang], axis=-1)
    return np.cos(ang).astype(np.float32), np.sin(ang).astype(np.float32)


def _consts():
    bf = ml_dtypes.bfloat16
    j = np.arange(128)
    cbm = np.zeros((128, 528), np.float32)
    cbm[:, 0:128] = np.eye(128)
    cbm[:, 128:256] = -(j[:, None] >= j[None, :]).astype(np.float32)
    cbm[:, 256:384] = (j[None, :] > j[:, None]).astype(np.float32)
    cbm[:, 384:400] = 1.0
    ind = np.zeros((128, 16, 128), np.float32)
    for i in range(16):
        ind[i, i, :] = -1.0
    cf = np.eye(128, dtype=np.float32)
    cos, sin = _rope_tables_host()
    sgn = np.concatenate([-np.ones(32, np.float32), np.ones(32, np.float32)])
    cosT = np.tile(cos.T, (2, 1))
    sinT = np.tile((sin * sgn[None, :]).T, (2, 1))
    cs = np.concatenate([cosT, sinT], axis=1).astype(np.float32)
    return cbm.astype(bf), ind.reshape(128, 2048).astype(bf), cf, np.ascontiguousarray(cs)


def _prep_inputs(x, c, ada_w, ada_b, mix_norm_g, w_in, lambda_q1, lambda_k1, lambda_q2, lambda_k2,
                 diff_subln_g, w_out, ffn_norm_g, w_ff1, w_ff2, final_norm_g):
    f = np.float32
    x = np.asarray(x, f)
    c = np.asarray(c, f)
    w_in0 = np.asarray(w_in, f)[0]
    perm = np.arange(512).reshape(8, 2, 32)[:, ::-1, :].reshape(512)
    w_in_ext = np.concatenate([w_in0, w_in0[:, 0:512][:, perm], w_in0[:, 512:1024][:, perm]], axis=1)
    cbm, ind, cf, cs = _consts()
    shared = {
        "ada_w": np.ascontiguousarray(np.asarray(ada_w, f)[0]),
        "ada_b": np.ascontiguousarray(np.broadcast_to(np.asarray(ada_b, f)[0][None, :], (P, 6 * D))),
        "gfm": np.ascontiguousarray(np.concatenate([np.asarray(mix_norm_g, f)[0].reshape(8, P).T,
                                                    np.asarray(ffn_norm_g, f)[0].reshape(8, P).T], axis=1)),
        "w_in": np.ascontiguousarray(w_in_ext),
        "lam": np.ascontiguousarray(np.broadcast_to(np.stack([np.asarray(v, f)[0] for v in
                                    (lambda_q1, lambda_k1, lambda_q2, lambda_k2)])[None], (P, 4, 64))),
        "subg": np.ascontiguousarray(np.broadcast_to(np.asarray(diff_subln_g, f)[0][None, :], (P, 128))),
        "w_out": np.ascontiguousarray(np.asarray(w_out, f)[0]),
        "w_ff1": np.ascontiguousarray(np.asarray(w_ff1, f)[0]),
        "w_ff2": np.ascontiguousarray(np.asarray(w_ff2, f)[0]),
        "fng": np.ascontiguousarray(np.broadcast_to(np.asarray(final_norm_g, f)[None, :], (P, D))),
        "cb": cbm, "ind": ind, "cf": cf, "cs": cs,
    }
    in_maps = []
    for b in range(x.shape[0]):
        m = dict(shared)
        m["x"] = np.ascontiguousarray(x[b])
        m["cfm"] = np.ascontiguousarray(c[b].reshape(8, P).T)
        in_maps.append(m)
    return in_maps


_NC_CACHE = {}


def kernel(**inputs):
    in_maps = _prep_inputs(**inputs)
    if "nc" not in _NC_CACHE:
        _NC_CACHE["nc"] = build_nc(False)
    nc = _NC_CACHE["nc"]
    n = len(in_maps)
    res = run_bass_kernel_spmd(nc, in_maps, core_ids=list(range(n)))
    return np.stack([np.asarray(r["out"], np.float32) for r in res.results], axis=0)
```

```python
import contextlib
import math
import numpy as np
import ml_dtypes
import concourse.bass as bass
import concourse.mybir as mybir
from concourse.bass_utils import run_bass_kernel_spmd

F32 = mybir.dt.float32
BF16 = mybir.dt.bfloat16
AF = mybir.ActivationFunctionType
ALU = mybir.AluOpType
AX = mybir.AxisListType

P = 128
S = 2048
D = 1024
NT = 16
NU = 4
DFF = 4096
EPS = 1e-6
LAMBDA_INIT = 0.8 - 0.6 * math.exp(-0.3 * 0)

COMPUTE = ("pe", "act", "dve", "pool")
DMAQ = ("sp", "poolq")
STREAM_OF = {"pe": "pe", "act": "act", "dve": "dve", "pool": "pool", "sp": "sp", "poolq": "pool"}
NDMASEM = 8


class Op:
    __slots__ = ("eng", "fn", "deps", "signal", "sem", "val", "inc", "idx", "prewait")

    def __init__(self, eng, fn):
        self.eng = eng
        self.fn = fn
        self.deps = []
        self.signal = False
        self.sem = None
        self.val = None
        self.inc = 1
        self.idx = 0
        self.prewait = None


class Sched:
    def __init__(self):
        self.streams = {s: [] for s in ("pe", "act", "dve", "pool", "sp")}
        self.last_w = {}
        self.readers = {}
        self.dma_ops = {q: [] for q in DMAQ}

    def add(self, eng, fn, reads=(), writes=(), after=()):
        op = Op(eng, fn)
        raw = set()
        other = set()
        for k in reads:
            w = self.last_w.get(k)
            if w is not None:
                raw.add(w)
            if isinstance(k, tuple) and k[0] == "ps":
                for r in self.readers.get(k, ()):
                    if r.eng != eng:
                        other.add(r)
        for k in list(writes) + list(after):
            w = self.last_w.get(k)
            if w is not None:
                other.add(w)
            for r in self.readers.get(k, ()):
                other.add(r)
        deps = set()
        for d in raw:
            if d.eng == "pe" and eng == "pe":
                continue
            deps.add(d)
        for d in other:
            if d.eng == "pe" and eng == "pe":
                continue
            deps.add(d)
        for d in deps:
            d.signal = True
        op.deps = list(deps)
        if eng in DMAQ:
            op.signal = True
            k = len(self.dma_ops[eng])
            op.idx = k
            if k >= NDMASEM:
                op.prewait = self.dma_ops[eng][k - NDMASEM]
            self.dma_ops[eng].append(op)
        for k in reads:
            self.readers.setdefault(k, []).append(op)
        for k in writes:
            self.last_w[k] = op
            self.readers[k] = []
        self.streams[STREAM_OF[eng]].append(op)
        return op

    def emit(self, nc, final_wait_ops=()):
        with contextlib.ExitStack() as st:
            sems = {}
            for e in COMPUTE:
                sems[e] = st.enter_context(nc.semaphore("s_" + e))
            for q in DMAQ:
                sems[q] = [st.enter_context(nc.semaphore("d_%s%d" % (q, i))) for i in range(NDMASEM)]
            cnt = {e: 0 for e in COMPUTE}
            for ops in self.streams.values():
                for op in ops:
                    if op.eng in COMPUTE:
                        if op.signal:
                            cnt[op.eng] += 1
                            op.sem = sems[op.eng]
                            op.val = cnt[op.eng]
                            op.inc = 1
                    else:
                        op.sem = sems[op.eng][op.idx % NDMASEM]
                        op.val = 16 * (op.idx // NDMASEM + 1)
                        op.inc = 16
            block = st.enter_context(nc.Block())
            engines = {"pe": "tensor", "act": "scalar", "dve": "vector", "pool": "gpsimd", "sp": "sync"}

            def make(stream):
                ops = self.streams[stream]

                def body(eng):
                    waited = {}
                    for op in ops:
                        dl = list(op.deps)
                        if op.prewait is not None:
                            dl.append(op.prewait)
                        mx = {}
                        for d in dl:
                            key = id(d.sem)
                            if waited.get(key, 0) >= d.val:
                                continue
                            if key not in mx or mx[key][1] < d.val:
                                mx[key] = (d.sem, d.val)
                        wl = list(mx.items())
                        for key, (sm, v) in wl[:-1]:
                            waited[key] = v
                            eng.wait_ge(sm, v)
                        ins = op.fn(eng)
                        if wl:
                            key, (sm, v) = wl[-1]
                            waited[key] = v
                            ins._wait_ge(sm, v)
                        if op.signal:
                            ins.then_inc(op.sem, op.inc)
                    if stream == "sp":
                        for d in final_wait_ops:
                            eng.wait_ge(d.sem, d.val)

                return body

            for stream, attr in engines.items():
                getattr(block, attr)(make(stream))


def MM(out, lhsT, rhs, start=True, stop=True, skip=False):
    if skip:
        return lambda g: g.matmul(out, lhsT, rhs, start=start, stop=stop, skip_group_check=True)
    return lambda g: g.matmul(out, lhsT, rhs, start=start, stop=stop)


def TR(out, in_, ident):
    return lambda g: g.transpose(out, in_, ident)


def ACT(out, in_, func, **kw):
    return lambda g: g.activation(out=out, in_=in_, func=func, **kw)


def TS(out, in0, s1, s2, op0, op1=None):
    if op1 is None:
        return lambda g: g.tensor_scalar(out, in0, s1, None, op0)
    return lambda g: g.tensor_scalar(out, in0, s1, s2, op0, op1)


def TT(out, in0, in1, op):
    return lambda g: g.tensor_tensor(out, in0, in1, op)


def STT(out, in0, scalar, in1, op0, op1):
    return lambda g: g.scalar_tensor_tensor(out, in0, scalar, in1, op0, op1)


def CP(out, in_):
    return lambda g: g.tensor_copy(out, in_)


def MEMSET(ap, v):
    return lambda g: g.memset(ap, v)


def DMA(out, in_):
    return lambda g: g.dma_start(out=out, in_=in_)


def RED(out, in_, op=ALU.add):
    return lambda g: g.tensor_reduce(out, in_, AX.X, op)


def RECIP(out, in_):
    return lambda g: g.reciprocal(out, in_)


def build_nc(debug=False):
    nc = bass.Bass("TRN2", target_bir_lowering=False)
    dt = nc.dram_tensor
    x_d = dt("x", [S, D], F32, kind="ExternalInput").ap()
    cfm_d = dt("cfm", [P, 8], F32, kind="ExternalInput").ap()
    adaw_d = dt("ada_w", [D, 6 * D], F32, kind="ExternalInput").ap()
    adab_d = dt("ada_b", [P, 6 * D], F32, kind="ExternalInput").ap()
    gfm_d = dt("gfm", [P, 16], F32, kind="ExternalInput").ap()
    win_d = dt("w_in", [D, 4096], F32, kind="ExternalInput").ap()
    lam_d = dt("lam", [P, 4, 64], F32, kind="ExternalInput").ap()
    subg_d = dt("subg", [P, 128], F32, kind="ExternalInput").ap()
    wout_d = dt("w_out", [D, D], F32, kind="ExternalInput").ap()
    w1_d = dt("w_ff1", [D, DFF], F32, kind="ExternalInput").ap()
    w2_d = dt("w_ff2", [DFF, D], F32, kind="ExternalInput").ap()
    fng_d = dt("fng", [P, D], F32, kind="ExternalInput").ap()
    cb_d = dt("cb", [P, 528], BF16, kind="ExternalInput").ap()
    ind_d = dt("ind", [P, 2048], BF16, kind="ExternalInput").ap()
    cf_d = dt("cf", [P, 128], F32, kind="ExternalInput").ap()
    cs_d = dt("cs", [P, 4096], F32, kind="ExternalInput").ap()
    out_d = dt("out", [S, D], F32, kind="ExternalOutput").ap()
    dbg = {}
    if debug:
        dbg["hT"] = dt("dbg_hT", [P, 8 * S], BF16, kind="ExternalOutput").ap()
        dbg["qk"] = dt("dbg_qk", [P, 8 * S], BF16, kind="ExternalOutput").ap()
        dbg["V"] = dt("dbg_V", [P, 16512], BF16, kind="ExternalOutput").ap()
        dbg["OT"] = dt("dbg_OT", [P, 8 * S], BF16, kind="ExternalOutput").ap()
        dbg["x1"] = dt("dbg_x1", [P, 16 * D], F32, kind="ExternalOutput").ap()
        dbg["mod"] = dt("dbg_mod", [P, 64], F32, kind="ExternalOutput").ap()
        dbg["h1"] = dt("dbg_h1", [P, 8 * S], BF16, kind="ExternalOutput").ap()
        dbg["qk0"] = dt("dbg_qk0", [P, 8 * S], BF16, kind="ExternalOutput").ap()

    Sd = Sched()
    import os as _os
    _stop = _os.environ.get("KSTOP", "")
    _cut = [False]

    def add(*a, **k):
        if _cut[0]:
            return None
        return Sd.add(*a, **k)

    def mark(name):
        if name == _stop:
            _cut[0] = True
    with contextlib.ExitStack() as st:
        sb = lambda name, shape, dtype: st.enter_context(nc.sbuf_tensor(name, shape, dtype))
        RA = sb("RA", [P, 16384], BF16)
        RDK = sb("RDK", [P, 8192 + 8320], BF16)
        RB = sb("RB", [P, 16384], BF16)
        RC = sb("RC", [P, 16384], BF16)
        RE = sb("RE", [P, 16384], BF16)
        RF = sb("RF", [P, 8192], BF16)
        RG = sb("RG", [P, 2048], F32)
        cb = sb("cbs", [P, 528], BF16)
        ind = sb("inds", [P, 2048], BF16)
        identf = sb("identf", [P, 128], F32)
        gm_bc = sb("gm_bc", [P, D], F32)
        gf_bc = sb("gf_bc", [P, D], F32)
        small = sb("small", [P, 256], F32)
        PS = st.enter_context(nc.psum_tensor("PS", [P, 4096], F32))

        ident = cb[:, 0:128]
        trineg = cb[:, 128:256]
        masku = cb[:, 256:384]
        vsel = cb[:, 384:528]

        def wsel_i(i):
            return vsel[:, 16 - i:16 - i + 128]
        indv = ind[:, :].rearrange("p (i s) -> p i s", i=16)

        def bank(b, n=1):
            return PS[:, b * 512:(b + n) * 512]

        def bankbf(b):
            return PS[:, b * 512:(b + 1) * 512].bitcast(BF16)

        qz = RA[:, :].rearrange("p (c t) -> p c t", c=8)
        ka = RDK[:, 0:8192].rearrange("p (c t) -> p c t", c=4)
        hT = RB[:, :].rearrange("p (c t) -> p c t", c=8)
        OT = RC[:, :].rearrange("p (c t) -> p c t", c=8)
        xn_bf = RC[:, 0:8192].rearrange("p (j f) -> p j f", j=8)
        cs_sb = RC[:, 8192:16384].bitcast(F32)
        cosT = cs_sb[:, 0:2048]
        sinT = cs_sb[:, 2048:4096]
        xs = RA[:, :].bitcast(F32).rearrange("p (j f) -> p j f", j=8)
        dvaug = RDK[:, 8192:16512].rearrange("p (t h d) -> p t h d", t=16, h=4)
        svb = RDK[:, 8192:16384].rearrange("p (t f) -> p t f", t=16)
        uT = RDK[:, 0:16384].rearrange("p (c t) -> p c t", c=8)
        wslot = [RE[:, s * 4096:(s + 1) * 4096].rearrange("p (k n) -> p k n", k=8) for s in range(4)]
        woutg = RE[:, 0:8192].rearrange("p (k n) -> p k n", k=8)
        w2slot = [RE[:, 8192 + s * 4096:8192 + (s + 1) * 4096].rearrange("p (k n) -> p k n", k=4) for s in range(2)]
        xr_lo = RA[:, :].bitcast(F32).rearrange("p (j f) -> p j f", j=8)
        xr_hi = RB[:, :].bitcast(F32).rearrange("p (j f) -> p j f", j=8)

        def xr(tt):
            return xr_lo[:, tt, :] if tt < 8 else xr_hi[:, tt - 8, :]

        lneg = RB[:, 0:8192].rearrange("p (i t) -> p i t", i=16)
        stage = [RG[:, 0:1024], RG[:, 1024:2048]]
        cfm = small[:, 0:8]
        cact = small[:, 8:16]
        gfm = small[:, 16:32]
        modT = small[:, 32:64]
        a_m = small[:, 64:72]
        a_f = small[:, 72:80]
        ssq = small[:, 80:96]
        rstd = small[:, 96:112]
        lamw = small[:, 112:116]
        neglam = small[:, 116:117]
        ssq2 = small[:, 120:136]
        rstd2 = small[:, 136:152]
        ssq3 = small[:, 152:168]
        rstd3 = small[:, 168:184]
        dsm = small[:, 184:256]
        crep = sb("crep", [P, 8, 128], BF16)
        lam_sb = sb("lam_sb", [P, 4, 64], F32)
        subgs = sb("subgs", [P, 128], F32)
        fngs = gm_bc
        junk_t = sb("junk", [P, 2 * D], BF16)
        junk = junk_t[:, 0:D]
        junk_alt = junk_t[:, D:2 * D]

        add("sp", DMA(cb[:, :], cb_d), writes=["cb"])
        add("sp", DMA(small[:, 0:8], cfm_d), writes=["cfm"])
        add("sp", DMA(small[:, 16:32], gfm_d), writes=["gfm"])
        add("sp", DMA(identf[:, :], cf_d), writes=["identf"])
        add("sp", DMA(ind[:, :], ind_d), writes=["ind"])
        add("sp", DMA(lam_sb[:, :, :], lam_d), writes=["lam_sb"])
        add("sp", DMA(subgs[:, :], subg_d), writes=["subgs"])
        add("pool", MEMSET(small[:, 80:96], 0.0), writes=["xn_ss"])
        add("pool", MEMSET(small[:, 120:136], 0.0), writes=["xn2_ss"])
        add("pool", MEMSET(small[:, 152:168], 0.0), writes=["ssq3"])
        add("pool", MEMSET(small[:, 184:256], 0.0), writes=[("dsm", k_, 3) for k_ in range(2)])

        add("act", ACT(cact, cfm, AF.Silu), reads=["cfm"], writes=["cact"])
        add("dve", CP(crep[:, :, :], cact.unsqueeze(2).to_broadcast([P, 8, 128])), reads=["cact"], writes=["crep"])
        add("dve", TT(lam_sb[:, 0:2, :], lam_sb[:, 0:4:2, :], lam_sb[:, 1:4:2, :], ALU.mult), reads=["lam_sb"], writes=["lam_sb2"])
        add("dve", RED(lamw[:, 0:2], lam_sb[:, 0:2, :]), reads=["lam_sb2"], writes=["lamw"])
        add("act", ACT(lamw[:, 2:4], lamw[:, 0:2], AF.Exp), reads=["lamw"], writes=["lamw2"])
        add("dve", STT(neglam, lamw[:, 3:4], -LAMBDA_INIT, lamw[:, 2:3], ALU.add, ALU.subtract), reads=["lamw2"], writes=["neglam"])
        add("dve", TS(subgs[:, :], subgs[:, :], (1.0 - LAMBDA_INIT) * math.sqrt(128.0), None, ALU.mult), reads=["subgs"], writes=["subgs"])

        for t_ in range(8):
            add("sp", DMA(xs[:, t_ % 8, :], x_d[t_ * 128:(t_ + 1) * 128, :]), writes=[("xs", t_ % 8)])
        add("sp", DMA(cs_sb, cs_d), writes=["cs"])
        adaw_v = adaw_d.rearrange("(k p) n -> p k n", p=P)
        wcount = [0]

        def wslot_load(src_ap, tag):
            s = wcount[0] % 4
            wcount[0] += 1
            add("poolq", DMA(wslot[s][:, :, :], src_ap), writes=[("w", s)])
            return s

        adab_slots = [RF[:, 0:1024].bitcast(F32), RF[:, 1024:2048].bitcast(F32)]
        tmp_mod_e = RF[:, 2048:3072].bitcast(F32)
        tmp_mod2_e = RF[:, 3072:4096].bitcast(F32)
        piece_order = [2, 3, 0, 1, 4, 5, 6, 7, 8, 9, 10, 11]
        modcol = {0: 0, 1: 4, 2: 8, 3: 12, 6: 16, 7: 20, 8: 24, 9: 28}
        def ada_piece(n_, pc, late=None):
            if late is None:
                s = wslot_load(adaw_v[:, :, pc * 512:(pc + 1) * 512], "ada")
                ab = adab_slots[n_ % 2]
                abk = ("adab", n_ % 2)
                add("sp", DMA(ab, adab_d[:, pc * 512:(pc + 1) * 512]), writes=[abk])
                b = n_ % 2
                tmp_mod, tmp_mod2, tk, tk2, aft = tmp_mod_e, tmp_mod2_e, "tmp_mod", "tmp_mod2", []
            else:
                s, ab, abk, b = late["slot"], late["ab"], late["abk"], 7
                tmp_mod, tmp_mod2, tk, tk2, aft = late["tmp"], late["tmp2"], "tmp_modB", "tmp_mod2B", late["after"]
                if late["stage"] == "load":
                    add("poolq", DMA(wslot[s][:, :, :], adaw_v[:, :, pc * 512:(pc + 1) * 512]), writes=[("w", s)])
                    add("sp", DMA(ab, adab_d[:, pc * 512:(pc + 1) * 512]), writes=[abk], after=aft)
                    return
            for kc in range(8):
                add("pe", MM(bank(b), crep[:, kc, :], wslot[s][:, kc, :], start=(kc == 0), stop=(kc == 7)),
                    reads=["crep", ("w", s)], writes=[("ps", b)])
            if pc in (4, 5):
                add("dve", TT(gm_bc[:, (pc - 4) * 512:(pc - 3) * 512], bank(b), ab, ALU.add),
                    reads=[("ps", b), abk], writes=[("gm", pc - 4)])
            elif pc in (10, 11):
                add("dve", TT(gf_bc[:, (pc - 10) * 512:(pc - 9) * 512], bank(b), ab, ALU.add),
                    reads=[("ps", b), abk], writes=[("gf", pc - 10)])
            else:
                add("dve", TT(tmp_mod, bank(b), ab, ALU.add), reads=[("ps", b), abk], writes=[tk], after=aft)
                add("dve", TT(tmp_mod2.rearrange("p (a b) -> p a b", a=4), tmp_mod.rearrange("p (a b) -> p a b", a=4),
                              identf[:, :].unsqueeze(1).to_broadcast([P, 4, 128]), ALU.mult),
                    reads=[tk, "identf"], writes=[tk2], after=aft)
                c0 = modcol[pc]
                add("dve", RED(modT[:, c0:c0 + 4], tmp_mod2.rearrange("p (a b) -> p a b", a=4)),
                    reads=[tk2], writes=[("modT", c0)])
            if pc == 3:
                add("dve", STT(a_m, modT[:, 8:16], 1.0, gfm[:, 0:8], ALU.add, ALU.mult),
                    reads=[("modT", 8), ("modT", 12), "gfm"], writes=["a_m"])
            if pc == 9:
                add("dve", STT(a_f, modT[:, 24:32], 1.0, gfm[:, 8:16], ALU.add, ALU.mult),
                    reads=[("modT", 24), ("modT", 28), "gfm"], writes=["a_f"])

        for n_, pc in enumerate(piece_order[:4]):
            ada_piece(n_, pc)

        mark("p0")
        def norm_phase(src_tile, ssq_, rstd_, xnbuf, aff_a, aff_sh, dstT, src_reads, xn_key, dst_key, banks, junk, extra_reads, xn_after=(), dst_after=(), pre_group=None, groups=None):
            bi = [0]
            for tt in [4 * g_ + j_ for g_ in (groups if groups is not None else range(NU)) for j_ in range(4)]:
                U = tt // 4
                slot = tt % 8
                if pre_group is not None and tt % 4 == 0:
                    pre_group(U)
                add("act", ACT(junk if tt % 2 == 0 else junk_alt, src_tile(tt), AF.Square, accum_out=ssq_[:, tt:tt + 1]),
                    reads=src_reads(tt) + [xn_key + "_ss"], writes=[(xn_key + "_ssq", tt), ("junk", tt % 2)])
                if tt % 4 == 3:
                    g0 = 4 * U
                    mark("n_sq")
                    add("act", ACT(rstd_[:, g0:g0 + 4], ssq_[:, g0:g0 + 4], AF.Ln, bias=D * EPS),
                        reads=[(xn_key + "_ssq", g0 + j) for j in range(4)], writes=[(xn_key + "_ln", U)])
                    add("act", ACT(rstd_[:, g0:g0 + 4], rstd_[:, g0:g0 + 4], AF.Exp, scale=-0.5),
                        reads=[(xn_key + "_ln", U)], writes=[(xn_key + "_rstd", U)])
                    mark("n_ln")
                    for j in range(4):
                        t_ = g0 + j
                        add("dve", TS(xnbuf[:, t_ % 8, :], src_tile(t_), rstd_[:, t_:t_ + 1], 32.0, ALU.mult, ALU.mult),
                            reads=src_reads(t_) + [(xn_key + "_rstd", U)], writes=[(xn_key, t_ % 8)], after=xn_after)
                    mark("n_norm")
                    for kp in range(4):
                        b = banks[bi[0] % len(banks)]
                        bi[0] += 1
                        pb = bankbf(b)
                        for k2 in range(2):
                            kc = kp * 2 + k2
                            for j in range(4):
                                add("pe", TR(pb[:, k2 * 512 + j * 128:k2 * 512 + (j + 1) * 128],
                                             xnbuf[:, (U % 2) * 4 + j, kc * 128:(kc + 1) * 128], ident),
                                    reads=[(xn_key, (U % 2) * 4 + j), "cb"], writes=[("ps", b)])
                        mark("n_tr")
                        for k2 in range(2):
                            kc = kp * 2 + k2
                            dst = dstT[:, kc, U * 512:(U + 1) * 512]
                            src = pb[:, k2 * 512:(k2 + 1) * 512]
                            if kp % 2 == 0:
                                add("act", ACT(dst, src, AF.Identity, scale=aff_a[:, kc:kc + 1], bias=aff_sh[:, kc:kc + 1]),
                                    reads=[("ps", b)] + extra_reads, writes=[(dst_key, kc, U)], after=dst_after)
                                mark("n_ea")
                            else:
                                add("dve", TS(dst, src, aff_a[:, kc:kc + 1], aff_sh[:, kc:kc + 1], ALU.mult, ALU.add),
                                    reads=[("ps", b)] + extra_reads, writes=[(dst_key, kc, U)], after=dst_after)

        win_v = win_d.rearrange("(k p) n -> p k n", p=P)
        pre_slots = {}
        for pcs_ in ((0, 6), (1, 7)):
            pre_slots[pcs_] = (wslot_load(win_v[:, :, pcs_[0] * 512:(pcs_[0] + 1) * 512], "win"),
                               wslot_load(win_v[:, :, pcs_[1] * 512:(pcs_[1] + 1) * 512], "win"))

        def load_x_group(U):
            groups = [] if U == 0 else ([U + 1] if U + 1 < NU else [])
            for g_ in groups:
                for j in range(4):
                    t_ = 4 * g_ + j
                    add("sp", DMA(xs[:, t_ % 8, :], x_d[t_ * 128:(t_ + 1) * 128, :]), writes=[("xs", t_ % 8)])

        norm_phase(lambda tt: xs[:, tt % 8, :], ssq, rstd, xn_bf, a_m, modT[:, 0:8], hT,
                   lambda tt: [("xs", tt % 8)], "xn", "hT", [0, 1, 2, 3], junk, ["a_m", ("modT", 0), ("modT", 4)],
                   pre_group=load_x_group)

        mark("p1")
        t1s = [RF[:, 5120 + i * 1024:5120 + (i + 1) * 1024].bitcast(F32) for i in range(2)]
        t2s = [RF[:, 7168 + i * 512:7168 + (i + 1) * 512] for i in range(0)]
        t2a = [RG[:, 0:512], RG[:, 512:1024], RG[:, 1024:1536], RG[:, 1536:2048]]
        ropec = [0]
        pbk = [4, 5, 6, 7]
        xs_keys = [("xs", i) for i in range(8)]
        for c_ in range(4):
            add("pool", MEMSET(qz[64:128, 2 * c_, :], 0.0), after=xs_keys, writes=[("qzero", 2 * c_)])
            add("dve", MEMSET(qz[0:64, 2 * c_ + 1, :], 0.0), after=xs_keys, writes=[("qzero", 2 * c_ + 1)])
        for (pa, pb_, dest, dkey, scale) in [(0, 6, None, "qz", 0.125), (1, 7, ka, "ka", 1.0)]:
            sa, sb_ = pre_slots[(pa, pb_)]
            for cc in range(4):
                for U in range(NU):
                    r = ropec[0]
                    ropec[0] += 1
                    ba = pbk[(2 * r) % 4]
                    bb = pbk[(2 * r + 1) % 4]
                    for kc in range(8):
                        add("pe", MM(bank(ba), wslot[sa][:, kc, cc * 128:(cc + 1) * 128], hT[:, kc, U * 512:(U + 1) * 512],
                                     start=(kc == 0), stop=(kc == 7)),
                            reads=[("w", sa), ("hT", kc, U)], writes=[("ps", ba)])
                    for kc in range(8):
                        add("pe", MM(bank(bb), wslot[sb_][:, kc, cc * 128:(cc + 1) * 128], hT[:, kc, U * 512:(U + 1) * 512],
                                     start=(kc == 0), stop=(kc == 7)),
                            reads=[("w", sb_), ("hT", kc, U)], writes=[("ps", bb)])
                    t1 = t1s[r % 2]
                    t2 = t2a[r % 2]
                    add("dve", STT(t1, bank(ba), scale, cosT[:, U * 512:(U + 1) * 512], ALU.mult, ALU.mult),
                        reads=[("ps", ba), "cs"], writes=[("t1", r % 2)])
                    add("dve", STT(t2, bank(bb), scale, sinT[:, U * 512:(U + 1) * 512], ALU.mult, ALU.mult),
                        reads=[("ps", bb), "cs"], writes=[("t2", r % 2)])
                    if dest is None:
                        for m_ in range(2):
                            add("pool", TT(qz[m_ * 64:(m_ + 1) * 64, 2 * cc + m_, U * 512:(U + 1) * 512],
                                           t1[m_ * 64:(m_ + 1) * 64, :], t2[m_ * 64:(m_ + 1) * 64, :], ALU.add),
                                reads=[("t1", r % 2), ("t2", r % 2), ("qzero", 2 * cc + m_)], writes=[("qz", 2 * cc + m_, U)])
                    else:
                        add("pool", TT(dest[:, cc, U * 512:(U + 1) * 512], t1, t2, ALU.add),
                            reads=[("t1", r % 2), ("t2", r % 2)], writes=[(dkey, cc, U)])
        add("pool", MEMSET(dvaug[:, :, :, 128:130], 1.0), writes=["dvones"])
        sv_ = wslot_load(win_v[:, :, 2 * 512:3 * 512], "win")
        for tt in range(NT):
            b = pbk[tt % 4]
            for kc in range(8):
                add("pe", MM(bank(b), hT[:, kc, tt * 128:(tt + 1) * 128], wslot[sv_][:, kc, :], start=(kc == 0), stop=(kc == 7)),
                    reads=[("w", sv_), ("hT", kc, tt // 4)], writes=[("ps", b)])
            eng = "act" if tt % 2 == 0 else "dve"
            src = bank(b).rearrange("p (h d) -> p h d", h=4)
            if eng == "act":
                add("act", ACT(dvaug[:, tt, :, 0:128], src, AF.Copy), reads=[("ps", b), "dvones"], writes=[("dv", tt)])
            else:
                add("dve", CP(dvaug[:, tt, :, 0:128], src), reads=[("ps", b), "dvones"], writes=[("dv", tt)])

        if debug:
            add("sp", DMA(dbg["h1"], RB[:, :]), reads=[("hT", kc, U) for kc in range(8) for U in range(NU)], writes=["dbgh1"])
        s_sq = wslot_load(win_v[:, :, 3 * 512:4 * 512], "win")
        s_sk = wslot_load(win_v[:, :, 4 * 512:5 * 512], "win")
        s_sv = wslot_load(win_v[:, :, 5 * 512:6 * 512], "win")

        mark("p1a")
        et = [RF[:, k_ * 512:(k_ + 1) * 512].rearrange("p (m t) -> p m t", m=2) for k_ in range(4)]
        od = [RF[:, 2048 + k_ * 256:2048 + (k_ + 1) * 256].rearrange("p (j f) -> p j f", j=2) for k_ in range(2)]
        tmpd = [RF[:, 3072:3328].bitcast(F32), RF[:, 3328:3584].bitcast(F32)]
        od32 = [RF[:, 3584 + i * 256:3840 + i * 256].bitcast(F32) for i in range(4)]
        phase_rf = ["tmp_mod", "tmp_mod2", ("adab", 0), ("adab", 1)]
        NSL = 4
        DIST = 3
        djobs = []
        pend = []
        rnd = 0
        for h in range(4):
            for R in range(8):
                for i in range(2 * R + 2):
                    djobs.append(("t", h, R, i, rnd % 2))
                    for p_ in pend:
                        p_[0] -= 1
                    while pend and pend[0][0] <= 0:
                        djobs.append(pend.pop(0)[1])
                pend.append([5, ("f3", h, R, 0, rnd % 2)])
                rnd += 1
        for _ in range(DIST + 2):
            djobs.append(("nop", 0, 0, 0, 0))
        for p_ in pend:
            djobs.append(p_[1])

        def d_s1(job, sl, first):
            kind, h, R, i, fs = job
            if kind == "nop":
                return
            if kind == "f3":
                sm = dsm[:, fs * 16:fs * 16 + 16]
                for j in range(2):
                    add("dve", (lambda o_, i_, a_: (lambda g: g.scalar_tensor_tensor(o_, i_, 1.0, i_, ALU.mult, ALU.mult, accum_out=a_)))(
                        junk_t[:, (fs * 2 + j) * 128:(fs * 2 + j + 1) * 128], od32[fs * 2 + j], sm[:, 8 + j:9 + j]),
                        reads=[("od32", fs * 2 + j), ("dsm", fs, 3)], writes=[("dsm", fs, 4, j), ("junk", 0)])
                add("act", ACT(sm[:, 12:14], sm[:, 8:10], AF.Ln, bias=128.0 * EPS), reads=[("dsm", fs, 4, j) for j in range(2)], writes=[("dsm", fs, 5)])
                add("act", ACT(sm[:, 12:14], sm[:, 12:14], AF.Exp, scale=-0.5), reads=[("dsm", fs, 5)], writes=[("dsm", fs, 6)])
                for j in range(2):
                    add("dve", STT(od[fs][:, j, :], od32[fs * 2 + j], sm[:, 12 + j:13 + j], subgs[:, :], ALU.mult, ALU.mult),
                        reads=[("od32", fs * 2 + j), ("dsm", fs, 6), "subgs"], writes=[("od", fs, j)])
                add("dve", MEMSET(sm[:, 8:10], 0.0), reads=[("dsm", fs, 5)], writes=[("dsm", fs, 3)])
                return
            t0 = max(R * 256, i * 128)
            w = (R + 1) * 256 - t0
            for m in range(2):
                add("pe", MM(bank(sl)[:, m * 256:m * 256 + w], ka[:, h, i * 128:(i + 1) * 128], qz[:, 2 * h + m, t0:t0 + w]),
                    reads=[("ka", h, i // 4), ("qzero", 2 * h + m), ("qz", 2 * h + m, t0 // 512)], writes=[("ps", sl)])
            scv = bank(sl).rearrange("p (m t) -> p m t", m=2)
            add("act", ACT(et[sl][:, :, 0:w], scv[:, :, 0:w], AF.Exp),
                reads=[("ps", sl)], writes=[("et", sl)], after=(phase_rf + [("t1", 0), ("t1", 1)] if first else []))
            if i >= 2 * R:
                add("pool", MEMSET(et[sl][64:128, :, 0:64], 0.0), reads=[], writes=[("et", sl)])

        def d_s2(job, sl):
            kind, h, R, i, fs = job
            if kind == "nop":
                return
            if kind == "f3":
                pb = bankbf(sl)
                for j in range(2):
                    add("pe", TR(pb[:, j * 128:(j + 1) * 128], od[fs][:, j, :], ident),
                        reads=[("od", fs, j), "cb"], writes=[("ps", sl)])
                add("dve", CP(OT[:, h, R * 256:(R + 1) * 256], pb[:, 0:256]),
                    reads=[("ps", sl)], after=["cs"] + [("xn", s_) for s_ in range(8)], writes=[("OT", h, R)])
                return
            t0 = max(R * 256, i * 128)
            for T in range(max(2 * R, i), 2 * R + 2):
                bacc = 4 + 2 * fs + (T % 2)
                c0 = T * 128 - t0
                for m in range(2):
                    add("pe", MM(bank(bacc)[:, m * 132:m * 132 + 129], et[sl][:, m, c0:c0 + 128], dvaug[:, i, h, 0:129],
                                 start=(i == 0 and m == 0), stop=(i == T), skip=True),
                        reads=[("et", sl), ("dv", i)], writes=[("ps", bacc)])
            if i != 2 * R + 1:
                return
            sm = dsm[:, fs * 16:fs * 16 + 16]
            ab0 = 4 + 2 * fs
            accs = [("ps", ab0), ("ps", ab0 + 1)]
            add("dve", RECIP(sm[:, 0:2], PS[:, ab0 * 512 + 128:(ab0 + 2) * 512:512]), reads=accs, writes=[("dsm", fs, 0)])
            add("dve", RECIP(sm[:, 4:6], PS[:, ab0 * 512 + 260:(ab0 + 2) * 512:512]), reads=accs, writes=[("dsm", fs, 1)])
            add("dve", TS(sm[:, 4:6], sm[:, 4:6], neglam, None, ALU.mult), reads=[("dsm", fs, 1), "neglam"], writes=[("dsm", fs, 1)])
            for j in range(2):
                bacc = ab0 + j
                acc0 = bank(bacc)[:, 0:128]
                acc1 = bank(bacc)[:, 132:260]
                o32 = od32[fs * 2 + j]
                add("dve", TS(tmpd[j], acc1, sm[:, 4 + j:5 + j], None, ALU.mult), reads=[("ps", bacc), ("dsm", fs, 1)], writes=[("tmpd", j)])
                add("dve", STT(o32, acc0, sm[:, j:j + 1], tmpd[j], ALU.mult, ALU.add),
                    reads=[("ps", bacc), ("dsm", fs, 0), ("tmpd", j)], writes=[("od32", fs * 2 + j)])

        hist = []
        for n_, job in enumerate(djobs + [None] * DIST):
            if job is not None:
                d_s1(job, n_ % NSL, n_ == 0)
            hist.append(job)
            if n_ >= DIST and hist[n_ - DIST] is not None:
                d_s2(hist[n_ - DIST], (n_ - DIST) % NSL)

        mark("p2a")
        pc_ = [0]
        dv_all = [("dv", t) for t in range(16)] + ["dvones"]
        for (sw, dest, dkey, scale) in [(s_sq, None, "qz", 0.125), (s_sk, ka, "ka", 1.0)]:
            for cc in range(4):
                for U in range(NU):
                    b = pc_[0] % 8
                    pc_[0] += 1
                    for kc in range(8):
                        add("pe", MM(bank(b), wslot[sw][:, kc, cc * 128:(cc + 1) * 128], hT[:, kc, U * 512:(U + 1) * 512],
                                     start=(kc == 0), stop=(kc == 7)),
                            reads=[("w", sw), ("hT", kc, U)], writes=[("ps", b)])
                    if dest is None:
                        for m_ in range(2):
                            dst_ = qz[m_ * 64:(m_ + 1) * 64, 2 * cc + m_, U * 512:(U + 1) * 512]
                            src_ = bank(b)[m_ * 64:(m_ + 1) * 64, :]
                            if pc_[0] % 2 == 0:
                                add("act", ACT(dst_, src_, AF.Identity, scale=scale), reads=[("ps", b)], writes=[("qz", 2 * cc + m_, U)])
                            else:
                                add("dve", TS(dst_, src_, scale, None, ALU.mult), reads=[("ps", b)], writes=[("qz", 2 * cc + m_, U)])
                    elif pc_[0] % 2 == 0:
                        add("act", ACT(dest[:, cc, U * 512:(U + 1) * 512], bank(b), AF.Identity, scale=scale),
                            reads=[("ps", b)], writes=[(dkey, cc, U)])
                    else:
                        add("dve", TS(dest[:, cc, U * 512:(U + 1) * 512], bank(b), scale, None, ALU.mult),
                            reads=[("ps", b)], writes=[(dkey, cc, U)])
        for tt in range(NT):
            b = pc_[0] % 8
            pc_[0] += 1
            for kc in range(8):
                add("pe", MM(bank(b), hT[:, kc, tt * 128:(tt + 1) * 128], wslot[s_sv][:, kc, :], start=(kc == 0), stop=(kc == 7)),
                    reads=[("w", s_sv), ("hT", kc, tt // 4)], writes=[("ps", b)])
            if tt % 2 == 0:
                add("act", ACT(svb[:, tt, :], bank(b), AF.Copy), reads=[("ps", b)], after=dv_all, writes=[("sv", tt)])
            else:
                add("dve", CP(svb[:, tt, :], bank(b)), reads=[("ps", b)], after=dv_all, writes=[("sv", tt)])

        stc = [0]

        def wout_prep(kcs, fin_):
            for kc in kcs:
                ss_ = stc[0] % 2
                stc[0] += 1
                add("sp", DMA(stage[ss_], wout_d[kc * 128:(kc + 1) * 128, :]), writes=[("stage", ss_), ("t2", 0), ("t2", 1)])
                add("pool", TT(woutg[:, kc, :], stage[ss_], gm_bc[:, :], ALU.mult),
                    reads=[("stage", ss_), ("gm", 0), ("gm", 1)], writes=[("w", 0), ("w", 1)] if kc == 0 else [("woutg", kc)])
            if fin_:
                add("sp", DMA(fngs[:, :], fng_d), writes=["fngs"], after=[("gm", 0), ("gm", 1)])
                add("pool", TS(fngs[:, :], fngs[:, :], 32.0, None, ALU.mult), reads=["fngs"], writes=["fngs"])

        hT_keys_early = [("hT", kc, U) for kc in range(8) for U in range(NU)]
        for tt in range(12, 16):
            add("sp", DMA(xr(tt), x_d[tt * 128:(tt + 1) * 128, :]), writes=[("xr", tt)], after=hT_keys_early)
        mark("p1b")
        etmp = [RF[:, 4096:5120].bitcast(F32), RF[:, 5120:6144].bitcast(F32)]
        at = [RF[:, 6144:6656], RF[:, 6656:7168], RF[:, 7168:7680]]
        negr = [RF[:, 7680:8192], RF[:, 0:512]]
        hT_keys = [("hT", kc, U) for kc in range(8) for U in range(NU)]
        units = [(U, h) for U in range(NU) for h in range(8)]
        BB = 2
        sbc = {"za": 0, "zb": 0, "at": 0}

        def geom(U, i):
            t0 = max(U * 512, i * 128)
            w = (U + 1) * 512 - t0
            return t0, w, t0 - U * 512

        def qk_reads(cc, U, i, t0, h_):
            return [("ka", cc, i // 4)] + [("qz", h_, uu) for uu in range(t0 // 512, U + 1)]

        def A1(un, i, st):
            U, h = units[un]
            cc, r0 = h // 2, (h % 2) * 64
            t0, w, c0 = geom(U, i)
            b = sbc["za"] % 2
            sbc["za"] += 1
            st["zab"] = b
            add("pe", MM(bank(b)[:, 0:w], ka[:, cc, i * 128:(i + 1) * 128], qz[:, h, t0:t0 + w]),
                reads=qk_reads(cc, U, i, t0, h), writes=[("ps", b)])
            add("act", ACT(bank(b)[:, 0:w], bank(b)[:, 0:w], AF.Exp), reads=[("ps", b)], writes=[("ps", b)])
            add("act", ACT(lneg[:, i, 0:w], bank(b)[:, 0:w], AF.Ln, bias=1.0), reads=[("ps", b)],
                after=(hT_keys if un < 8 else []), writes=[("lneg", i)])
            if i >= 4 * U:
                add("dve", TT(lneg[:, i, 0:128], lneg[:, i, 0:128], masku, ALU.mult), reads=[("lneg", i), "cb"], writes=[("lneg", i)])

        def A2(un, i):
            U, h = units[un]
            nI = 4 * U + 4
            t0, w, c0 = geom(U, i)
            add("pe", MM(bank(BB)[:, c0:c0 + w], wsel_i(i), lneg[:, i, 0:w], start=(i == 0), stop=(i == nI - 1)),
                reads=[("lneg", i), "cb"], writes=[("ps", BB)])
            if i == nI - 1:
                add("dve", CP(negr[un % 2][:, :], bank(BB)[:, :]), reads=[("ps", BB)], writes=[("negr", un % 2)],
                    after=([("et", 0), ("et", 1), ("et", 2), ("et", 3)] if un < 2 else []))

        def B1(un, i, st):
            U, h = units[un]
            cc, r0 = h // 2, (h % 2) * 64
            nI = 4 * U + 4
            t0, w, c0 = geom(U, i)
            b = 3 + (sbc["zb"] % 2)
            sbc["zb"] += 1
            add("pe", MM(bank(b)[:, 0:w], ka[:, cc, i * 128:(i + 1) * 128], qz[:, h, t0:t0 + w], start=True, stop=False),
                reads=qk_reads(cc, U, i, t0, h), writes=[("ps", b)])
            if i < nI - 1:
                add("pe", MM(bank(b)[:, 0:w], indv[:, i, :], negr[un % 2][:, c0:c0 + w], start=False, stop=False),
                    reads=[("negr", un % 2), "ind"], writes=[("ps", b)])
            add("pe", MM(bank(b)[:, 0:w], trineg, lneg[:, i, 0:w], start=False, stop=True),
                reads=[("lneg", i), "cb"], writes=[("ps", b)])
            k = sbc["at"] % 3
            sbc["at"] += 1
            st[("at", i)] = k
            add("act", ACT(at[k][:, 0:w], bank(b)[:, 0:w], AF.Exp), reads=[("ps", b)], writes=[("at", k)],
                after=([("t1", 0), ("t1", 1)] if un == 0 else []))
            if i >= 4 * U:
                add("dve", TT(at[k][:, 0:128], at[k][:, 0:128], masku, ALU.mult), reads=[("at", k), "cb"], writes=[("at", k)])

        def B2(un, i, st):
            U, h = units[un]
            cc, r0 = h // 2, (h % 2) * 64
            nI = 4 * U + 4
            t0, w, c0 = geom(U, i)
            ob = 5 + (un % 2)
            k = st[("at", i)]
            add("pe", MM(bank(ob)[:, c0:c0 + w], svb[:, i, cc * 128:(cc + 1) * 128], at[k][:, 0:w], start=(i == 0), stop=(i == nI - 1)),
                reads=[("at", k), ("sv", i)], writes=[("ps", ob)])
            if i == nI - 1:
                if False:
                    add("act", ACT(OT[r0:r0 + 64, 4 + cc, U * 512:(U + 1) * 512], bank(ob)[r0:r0 + 64, :], AF.Copy),
                        reads=[("ps", ob)], writes=[("OT", 4 + cc, U, h % 2)])
                else:
                    add("dve", CP(OT[r0:r0 + 64, 4 + cc, U * 512:(U + 1) * 512], bank(ob)[r0:r0 + 64, :]),
                        reads=[("ps", ob)], writes=[("OT", 4 + cc, U, h % 2)])

        nI0 = 4 * units[0][0] + 4
        stA = {}
        for t in range(nI0 + 1):
            if t < nI0:
                A1(0, t, stA)
            if t >= 1:
                A2(0, t - 1)
        late_pcs = piece_order[4:]
        diff_scratch = [("et", k_) for k_ in range(4)] + [("od", f_, j_) for f_ in range(2) for j_ in range(2)] + \
                       [("tmpd", 0), ("tmpd", 1)] + [("od32", j_) for j_ in range(4)]
        late_ab = [RF[:, 512:1536].bitcast(F32), RF[:, 4096:5120].bitcast(F32)]

        def late_cfg(k_, stage_):
            return dict(slot=2 + (k_ % 2), ab=late_ab[k_ % 2], abk=("adab2", k_ % 2), tmp=RF[:, 2048:3072].bitcast(F32),
                        tmp2=RF[:, 3072:4096].bitcast(F32), after=(diff_scratch if k_ < 2 else []), stage=stage_)

        for un in range(len(units)):
            if un % 2 == 0 and 10 <= un <= 10 + 2 * (len(late_pcs) - 1):
                k_ = (un - 10) // 2
                ada_piece(4 + k_, late_pcs[k_], late=late_cfg(k_, "compute"))
            if un % 2 == 0 and 8 <= un <= 8 + 2 * (len(late_pcs) - 1):
                k_ = (un - 8) // 2
                ada_piece(4 + k_, late_pcs[k_], late=late_cfg(k_, "load"))
            if 17 <= un <= 24:
                wout_prep([un - 17], un == 24)
            nIu = 4 * units[un][0] + 4
            nIn = 4 * units[un + 1][0] + 4 if un + 1 < len(units) else 0
            stB = {}
            for t in range(max(nIu, nIn) + 1):
                if t < nIu:
                    B1(un, t, stB)
                if t < nIn:
                    A1(un + 1, t, stA)
                if 1 <= t <= nIu:
                    B2(un, t - 1, stB)
                if 1 <= t <= nIn:
                    A2(un + 1, t - 1)


        mark("p2b")
        qk_keys = [("qz", c, U) for c in range(8) for U in range(4)] + [("qzero", c) for c in range(8)]
        ka_keys = [("ka", c, U) for c in range(4) for U in range(4)]
        rb_keys = [("lneg", i) for i in range(16)] + hT_keys
        OT_all = [("OT", c, R_) for c in range(4) for R_ in range(8)] + [("OT", 4 + c, U, k) for c in range(4) for U in range(4) for k in range(2)]
        wg_keys = [("w", 0), ("w", 1)] + [("woutg", kc) for kc in range(1, 8)]
        for tt in range(12):
            add("sp", DMA(xr(tt), x_d[tt * 128:(tt + 1) * 128, :]),
                writes=[("xr", tt)], after=(["dbgqk"] if debug else []) + (qk_keys if tt < 8 else rb_keys))
        p3 = [0]

        def p3_group(U_):
          for tt in range(4 * U_, 4 * U_ + 4):
            for nh in range(2):
                b = p3[0] % 8
                p3[0] += 1
                for fc in range(8):
                    rk = [("OT", fc, tt // 2)] if fc < 4 else [("OT", fc, tt // 4, 0), ("OT", fc, tt // 4, 1)]
                    add("pe", MM(bank(b), OT[:, fc, tt * 128:(tt + 1) * 128], woutg[:, fc, nh * 512:(nh + 1) * 512], start=(fc == 0), stop=(fc == 7)),
                        reads=rk + wg_keys, writes=[("ps", b)])
                add("dve", TT(xr(tt)[:, nh * 512:(nh + 1) * 512], bank(b), xr(tt)[:, nh * 512:(nh + 1) * 512], ALU.add),
                    reads=[("ps", b), ("xr", tt)], writes=[("xr", tt)])
        mark("p3")
        xn2 = RF[:, :].rearrange("p (j f) -> p j f", j=8)
        h2T = RC[:, :].rearrange("p (c t) -> p c t", c=8)
        junkb = junk[:, :]
        rf_keys = [("at", 0), ("at", 1), ("at", 2), ("negr", 0), ("negr", 1), ("et", 0), ("et", 1), ("et", 2), ("et", 3),
                   ("tmpd", 0), ("tmpd", 1)] + [("od32", j) for j in range(4)]
        def n2_group(U_):
            ot_blk = [("OT", c, R_) for c in range(4) for R_ in (2 * U_, 2 * U_ + 1)] + \
                     [("OT", 4 + c, U_, k) for c in range(4) for k in range(2)]
            norm_phase(lambda tt: xr(tt), ssq2, rstd2, xn2, a_f, modT[:, 16:24], h2T,
                       lambda tt: [("xr", tt)], "xn2", "h2T", [0, 1, 2, 3], junkb,
                       ["a_f", ("modT", 16), ("modT", 20)],
                       xn_after=rf_keys + [("od", 0, j) for j in range(2)] + [("od", 1, j) for j in range(2)],
                       dst_after=ot_blk, groups=[U_])

        p3_group(3)
        p3_group(0)
        n2_group(3)
        p3_group(1)
        n2_group(0)
        p3_group(2)
        n2_group(1)
        n2_group(2)

        w1_v = w1_d.rearrange("(k p) n -> p k n", p=P)
        rbuf = [RF[:, i * 1024:(i + 1) * 1024].bitcast(F32) for i in range(4)]
        xn2_keys = [("xn2", s_) for s_ in range(8)]
        rc_ = [0]
        p4 = [0]
        for fb in range(4):
            ws1 = []
            for half in range(2):
                s = half
                add("poolq", DMA(wslot[s][:, :, :], w1_v[:, :, fb * 1024 + half * 512: fb * 1024 + (half + 1) * 512]),
                    reads=wg_keys if fb == 0 else [], writes=[("w", s)])
                ws1.append(s)
            for half in range(2):
                for fc in range(4):
                    ss_ = stc[0] % 2
                    stc[0] += 1
                    r_ = fb * 1024 + half * 512 + fc * 128
                    add("sp", DMA(stage[ss_], w2_d[r_:r_ + 128, :]), writes=[("stage", ss_)])
                    add("pool", TT(w2slot[half][:, fc, :], stage[ss_], gf_bc[:, :], ALU.mult),
                        reads=[("stage", ss_), ("gf", 0), ("gf", 1)], writes=[("w", 2 + half)] if fc == 0 else [("w2g", half, fc)])
            for half in range(2):
                for fc in range(4):
                    for U in range(NU):
                        b = p4[0] % 8
                        p4[0] += 1
                        for kc in range(8):
                            add("pe", MM(bank(b), wslot[ws1[half]][:, kc, fc * 128:(fc + 1) * 128], h2T[:, kc, U * 512:(U + 1) * 512],
                                         start=(kc == 0), stop=(kc == 7)),
                                reads=[("w", ws1[half]), ("h2T", kc, U)], writes=[("ps", b)])
                        rb_ = rc_[0] % 4
                        rc_[0] += 1
                        add("act", ACT(rbuf[rb_], bank(b), AF.Relu), reads=[("ps", b)], after=(xn2_keys if rc_[0] <= 4 else []), writes=[("rbuf", rb_)])
                        eng = "dve" if rc_[0] % 2 == 0 else "pool"
                        add(eng, TT(uT[:, half * 4 + fc, U * 512:(U + 1) * 512], rbuf[rb_], rbuf[rb_], ALU.mult),
                            reads=[("rbuf", rb_)], after=(ka_keys + [("dv", t) for t in range(16)] + [("sv", t) for t in range(16)] if fb == 0 else []),
                            writes=[("uT", half * 4 + fc, U)])
            for tt in range(NT):
                for nh in range(2):
                    b = p4[0] % 8
                    p4[0] += 1
                    for j in range(8):
                        half, fc = j // 4, j % 4
                        add("pe", MM(bank(b), uT[:, j, tt * 128:(tt + 1) * 128], w2slot[half][:, fc, nh * 512:(nh + 1) * 512],
                                     start=(j == 0), stop=(j == 7)),
                            reads=[("uT", j, tt // 4), ("w", 2 + half)] + [("w2g", half, f_) for f_ in range(1, 4)], writes=[("ps", b)])
                    add("dve", TT(xr(tt)[:, nh * 512:(nh + 1) * 512], bank(b), xr(tt)[:, nh * 512:(nh + 1) * 512], ALU.add),
                        reads=[("ps", b), ("xr", tt)], writes=[("xr", tt)])
                if fb == 3:
                    add("act", ACT(junk if tt % 2 == 0 else junk_alt, xr(tt), AF.Square, accum_out=ssq3[:, tt:tt + 1]),
                        reads=[("xr", tt), "ssq3"], writes=[("ssq3", tt), ("junk", tt % 2)])
                    if tt % 4 == 3:
                        g0 = tt - 3
                        add("act", ACT(rstd3[:, g0:g0 + 4], ssq3[:, g0:g0 + 4], AF.Ln, bias=D * EPS), reads=[("ssq3", g0 + j) for j in range(4)], writes=[("ln3", g0)])
                        add("act", ACT(rstd3[:, g0:g0 + 4], rstd3[:, g0:g0 + 4], AF.Exp, scale=-0.5), reads=[("ln3", g0)], writes=[("rstd3", g0)])
                        for t_ in range(g0, g0 + 4):
                            add("dve", STT(xr(t_), xr(t_), rstd3[:, t_:t_ + 1], fngs[:, :], ALU.mult, ALU.mult),
                                reads=[("xr", t_), ("rstd3", g0), "fngs"], writes=[("xr", t_)])
                            add("sp", DMA(out_d[t_ * 128:(t_ + 1) * 128, :], xr(t_)), reads=[("xr", t_)], writes=[("out", t_)])

        mark("p4")
        if debug:
            add("sp", DMA(dbg["hT"], RC[:, :]), reads=[("h2T", kc, U) for kc in range(8) for U in range(4)], writes=["dbghT"])
        _cut[0] = False
        Sd.emit(nc, final_wait_ops=list(Sd.dma_ops["sp"]))
    return nc


def _rope_tables_host():
    try:
        import jax
        import jax.numpy as jnp
        cpu = jax.devices("cpu")[0]
        with jax.default_device(cpu):
            inv = 1.0 / (10000.0 ** (jnp.arange(0, 64, 2, dtype=jnp.float32) / 64))
            ang = jnp.arange(S, dtype=jnp.float32)[:, None] * inv[None, :]
            ang = jnp.concatenate([ang, ang], axis=-1)
            cos = np.asarray(jnp.cos(ang), dtype=np.float32)
            sin = np.asarray(jnp.sin(ang), dtype=np.float32)
        if cos.shape == (S, 64) and np.isfinite(cos).all() and np.isfinite(sin).all():
            return cos, sin
    except Exception:
        pass
    inv = (1.0 / (np.float32(10000.0) ** (np.arange(0, 64, 2, dtype=np.float32) / np.float32(64)))).astype(np.float32)
    ang = np.arange(S, dtype=np.float32)[:, None] * inv[None, :]
    ang = np.concatenate([ang, ang], axis=-1)
    return np.cos(ang).astype(np.float32), np.sin(ang).astype(np.float32)


def _consts():
    bf = ml_dtypes.bfloat16
    j = np.arange(128)
    cbm = np.zeros((128, 528), np.float32)
    cbm[:, 0:128] = np.eye(128)
    cbm[:, 128:256] = -(j[:, None] >= j[None, :]).astype(np.float32)
    cbm[:, 256:384] = (j[None, :] > j[:, None]).astype(np.float32)
    cbm[:, 384:400] = 1.0
    ind = np.zeros((128, 16, 128), np.float32)
    for i in range(16):
        ind[i, i, :] = -1.0
    cf = np.eye(128, dtype=np.float32)
    cos, sin = _rope_tables_host()
    sgn = np.concatenate([-np.ones(32, np.float32), np.ones(32, np.float32)])
    cosT = np.tile(cos.T, (2, 1))
    sinT = np.tile((sin * sgn[None, :]).T, (2, 1))
    cs = np.concatenate([cosT, sinT], axis=1).astype(np.float32)
    return cbm.astype(bf), ind.reshape(128, 2048).astype(bf), cf, np.ascontiguousarray(cs)


def _prep_inputs(x, c, ada_w, ada_b, mix_norm_g, w_in, lambda_q1, lambda_k1, lambda_q2, lambda_k2,
                 diff_subln_g, w_out, ffn_norm_g, w_ff1, w_ff2, final_norm_g):
    f = np.float32
    x = np.asarray(x, f)
    c = np.asarray(c, f)
    w_in0 = np.asarray(w_in, f)[0]
    perm = np.arange(512).reshape(8, 2, 32)[:, ::-1, :].reshape(512)
    w_in_ext = np.concatenate([w_in0, w_in0[:, 0:512][:, perm], w_in0[:, 512:1024][:, perm]], axis=1)
    cbm, ind, cf, cs = _consts()
    shared = {
        "ada_w": np.ascontiguousarray(np.asarray(ada_w, f)[0]),
        "ada_b": np.ascontiguousarray(np.broadcast_to(np.asarray(ada_b, f)[0][None, :], (P, 6 * D))),
        "gfm": np.ascontiguousarray(np.concatenate([np.asarray(mix_norm_g, f)[0].reshape(8, P).T,
                                                    np.asarray(ffn_norm_g, f)[0].reshape(8, P).T], axis=1)),
        "w_in": np.ascontiguousarray(w_in_ext),
        "lam": np.ascontiguousarray(np.broadcast_to(np.stack([np.asarray(v, f)[0] for v in
                                    (lambda_q1, lambda_k1, lambda_q2, lambda_k2)])[None], (P, 4, 64))),
        "subg": np.ascontiguousarray(np.broadcast_to(np.asarray(diff_subln_g, f)[0][None, :], (P, 128))),
        "w_out": np.ascontiguousarray(np.asarray(w_out, f)[0]),
        "w_ff1": np.ascontiguousarray(np.asarray(w_ff1, f)[0]),
        "w_ff2": np.ascontiguousarray(np.asarray(w_ff2, f)[0]),
        "fng": np.ascontiguousarray(np.broadcast_to(np.asarray(final_norm_g, f)[None, :], (P, D))),
        "cb": cbm, "ind": ind, "cf": cf, "cs": cs,
    }
    in_maps = []
    for b in range(x.shape[0]):
        m = dict(shared)
        m["x"] = np.ascontiguousarray(x[b])
        m["cfm"] = np.ascontiguousarray(c[b].reshape(8, P).T)
        in_maps.append(m)
    return in_maps


_NC_CACHE = {}


def kernel(**inputs):
    in_maps = _prep_inputs(**inputs)
    if "nc" not in _NC_CACHE:
        _NC_CACHE["nc"] = build_nc(False)
    nc = _NC_CACHE["nc"]
    n = len(in_maps)
    res = run_bass_kernel_spmd(nc, in_maps, core_ids=list(range(n)))
    return np.stack([np.asarray(r["out"], np.float32) for r in res.results], axis=0)
```

```python
import contextlib
import math
import numpy as np
import ml_dtypes
import concourse.bass as bass
import concourse.mybir as mybir
from concourse.bass_utils import run_bass_kernel_spmd

F32 = mybir.dt.float32
BF16 = mybir.dt.bfloat16
AF = mybir.ActivationFunctionType
ALU = mybir.AluOpType
AX = mybir.AxisListType

P = 128
S = 2048
D = 1024
NT = 16
NU = 4
DFF = 4096
EPS = 1e-6
LAMBDA_INIT = 0.8 - 0.6 * math.exp(-0.3 * 0)

COMPUTE = ("pe", "act", "dve", "pool")
DMAQ = ("sp", "poolq")
STREAM_OF = {"pe": "pe", "act": "act", "dve": "dve", "pool": "pool", "sp": "sp", "poolq": "pool"}
NDMASEM = 8


class Op:
    __slots__ = ("eng", "fn", "deps", "signal", "sem", "val", "inc", "idx", "prewait")

    def __init__(self, eng, fn):
        self.eng = eng
        self.fn = fn
        self.deps = []
        self.signal = False
        self.sem = None
        self.val = None
        self.inc = 1
        self.idx = 0
        self.prewait = None


class Sched:
    def __init__(self):
        self.streams = {s: [] for s in ("pe", "act", "dve", "pool", "sp")}
        self.last_w = {}
        self.readers = {}
        self.dma_ops = {q: [] for q in DMAQ}

    def add(self, eng, fn, reads=(), writes=(), after=()):
        op = Op(eng, fn)
        raw = set()
        other = set()
        for k in reads:
            w = self.last_w.get(k)
            if w is not None:
                raw.add(w)
            if isinstance(k, tuple) and k[0] == "ps":
                for r in self.readers.get(k, ()):
                    if r.eng != eng:
                        other.add(r)
        for k in list(writes) + list(after):
            w = self.last_w.get(k)
            if w is not None:
                other.add(w)
            for r in self.readers.get(k, ()):
                other.add(r)
        deps = set()
        for d in raw:
            if d.eng == "pe" and eng == "pe":
                continue
            deps.add(d)
        for d in other:
            if d.eng == "pe" and eng == "pe":
                continue
            deps.add(d)
        for d in deps:
            d.signal = True
        op.deps = list(deps)
        if eng in DMAQ:
            op.signal = True
            k = len(self.dma_ops[eng])
            op.idx = k
            if k >= NDMASEM:
                op.prewait = self.dma_ops[eng][k - NDMASEM]
            self.dma_ops[eng].append(op)
        for k in reads:
            self.readers.setdefault(k, []).append(op)
        for k in writes:
            self.last_w[k] = op
            self.readers[k] = []
        self.streams[STREAM_OF[eng]].append(op)
        return op

    def emit(self, nc, final_wait_ops=()):
        with contextlib.ExitStack() as st:
            sems = {}
            for e in COMPUTE:
                sems[e] = st.enter_context(nc.semaphore("s_" + e))
            for q in DMAQ:
                sems[q] = [st.enter_context(nc.semaphore("d_%s%d" % (q, i))) for i in range(NDMASEM)]
            cnt = {e: 0 for e in COMPUTE}
            for ops in self.streams.values():
                for op in ops:
                    if op.eng in COMPUTE:
                        if op.signal:
                            cnt[op.eng] += 1
                            op.sem = sems[op.eng]
                            op.val = cnt[op.eng]
                            op.inc = 1
                    else:
                        op.sem = sems[op.eng][op.idx % NDMASEM]
                        op.val = 16 * (op.idx // NDMASEM + 1)
                        op.inc = 16
            block = st.enter_context(nc.Block())
            engines = {"pe": "tensor", "act": "scalar", "dve": "vector", "pool": "gpsimd", "sp": "sync"}

            def make(stream):
                ops = self.streams[stream]

                def body(eng):
                    waited = {}
                    for op in ops:
                        dl = list(op.deps)
                        if op.prewait is not None:
                            dl.append(op.prewait)
                        mx = {}
                        for d in dl:
                            key = id(d.sem)
                            if waited.get(key, 0) >= d.val:
                                continue
                            if key not in mx or mx[key][1] < d.val:
                                mx[key] = (d.sem, d.val)
                        wl = list(mx.items())
                        for key, (sm, v) in wl[:-1]:
                            waited[key] = v
                            eng.wait_ge(sm, v)
                        ins = op.fn(eng)
                        if wl:
                            key, (sm, v) = wl[-1]
                            waited[key] = v
                            ins._wait_ge(sm, v)
                        if op.signal:
                            ins.then_inc(op.sem, op.inc)
                    if stream == "sp":
                        for d in final_wait_ops:
                            eng.wait_ge(d.sem, d.val)

                return body

            for stream, attr in engines.items():
                getattr(block, attr)(make(stream))


def MM(out, lhsT, rhs, start=True, stop=True, skip=False):
    if skip:
        return lambda g: g.matmul(out, lhsT, rhs, start=start, stop=stop, skip_group_check=True)
    return lambda g: g.matmul(out, lhsT, rhs, start=start, stop=stop)


def TR(out, in_, ident):
    return lambda g: g.transpose(out, in_, ident)


def ACT(out, in_, func, **kw):
    return lambda g: g.activation(out=out, in_=in_, func=func, **kw)


def TS(out, in0, s1, s2, op0, op1=None):
    if op1 is None:
        return lambda g: g.tensor_scalar(out, in0, s1, None, op0)
    return lambda g: g.tensor_scalar(out, in0, s1, s2, op0, op1)


def TT(out, in0, in1, op):
    return lambda g: g.tensor_tensor(out, in0, in1, op)


def STT(out, in0, scalar, in1, op0, op1):
    return lambda g: g.scalar_tensor_tensor(out, in0, scalar, in1, op0, op1)


def CP(out, in_):
    return lambda g: g.tensor_copy(out, in_)


def MEMSET(ap, v):
    return lambda g: g.memset(ap, v)


def DMA(out, in_):
    return lambda g: g.dma_start(out=out, in_=in_)


def RED(out, in_, op=ALU.add):
    return lambda g: g.tensor_reduce(out, in_, AX.X, op)


def RECIP(out, in_):
    return lambda g: g.reciprocal(out, in_)


def build_nc(debug=False):
    nc = bass.Bass("TRN2", target_bir_lowering=False)
    dt = nc.dram_tensor
    x_d = dt("x", [S, D], F32, kind="ExternalInput").ap()
    cfm_d = dt("cfm", [P, 8], F32, kind="ExternalInput").ap()
    adaw_d = dt("ada_w", [D, 6 * D], F32, kind="ExternalInput").ap()
    adab_d = dt("ada_b", [P, 6 * D], F32, kind="ExternalInput").ap()
    gfm_d = dt("gfm", [P, 16], F32, kind="ExternalInput").ap()
    win_d = dt("w_in", [D, 4096], F32, kind="ExternalInput").ap()
    lam_d = dt("lam", [P, 4, 64], F32, kind="ExternalInput").ap()
    subg_d = dt("subg", [P, 128], F32, kind="ExternalInput").ap()
    wout_d = dt("w_out", [D, D], F32, kind="ExternalInput").ap()
    w1_d = dt("w_ff1", [D, DFF], F32, kind="ExternalInput").ap()
    w2_d = dt("w_ff2", [DFF, D], F32, kind="ExternalInput").ap()
    fng_d = dt("fng", [P, D], F32, kind="ExternalInput").ap()
    cb_d = dt("cb", [P, 528], BF16, kind="ExternalInput").ap()
    ind_d = dt("ind", [P, 2048], BF16, kind="ExternalInput").ap()
    cf_d = dt("cf", [P, 128], F32, kind="ExternalInput").ap()
    cs_d = dt("cs", [P, 4096], F32, kind="ExternalInput").ap()
    out_d = dt("out", [S, D], F32, kind="ExternalOutput").ap()
    dbg = {}
    if debug:
        dbg["hT"] = dt("dbg_hT", [P, 8 * S], BF16, kind="ExternalOutput").ap()
        dbg["qk"] = dt("dbg_qk", [P, 8 * S], BF16, kind="ExternalOutput").ap()
        dbg["V"] = dt("dbg_V", [P, 16512], BF16, kind="ExternalOutput").ap()
        dbg["OT"] = dt("dbg_OT", [P, 8 * S], BF16, kind="ExternalOutput").ap()
        dbg["x1"] = dt("dbg_x1", [P, 16 * D], F32, kind="ExternalOutput").ap()
        dbg["mod"] = dt("dbg_mod", [P, 64], F32, kind="ExternalOutput").ap()
        dbg["h1"] = dt("dbg_h1", [P, 8 * S], BF16, kind="ExternalOutput").ap()
        dbg["qk0"] = dt("dbg_qk0", [P, 8 * S], BF16, kind="ExternalOutput").ap()

    Sd = Sched()
    import os as _os
    _stop = _os.environ.get("KSTOP", "")
    _cut = [False]

    def add(*a, **k):
        if _cut[0]:
            return None
        return Sd.add(*a, **k)

    def mark(name):
        if name == _stop:
            _cut[0] = True
    with contextlib.ExitStack() as st:
        sb = lambda name, shape, dtype: st.enter_context(nc.sbuf_tensor(name, shape, dtype))
        RA = sb("RA", [P, 16384], BF16)
        RDK = sb("RDK", [P, 8192 + 8320], BF16)
        RB = sb("RB", [P, 16384], BF16)
        RC = sb("RC", [P, 16384], BF16)
        RE = sb("RE", [P, 16384], BF16)
        RF = sb("RF", [P, 8192], BF16)
        RG = sb("RG", [P, 2048], F32)
        cb = sb("cbs", [P, 528], BF16)
        ind = sb("inds", [P, 2048], BF16)
        identf = sb("identf", [P, 128], F32)
        gm_bc = sb("gm_bc", [P, D], F32)
        gf_bc = sb("gf_bc", [P, D], F32)
        small = sb("small", [P, 256], F32)
        PS = st.enter_context(nc.psum_tensor("PS", [P, 4096], F32))

        ident = cb[:, 0:128]
        trineg = cb[:, 128:256]
        masku = cb[:, 256:384]
        vsel = cb[:, 384:528]

        def wsel_i(i):
            return vsel[:, 16 - i:16 - i + 128]
        indv = ind[:, :].rearrange("p (i s) -> p i s", i=16)

        def bank(b, n=1):
            return PS[:, b * 512:(b + n) * 512]

        def bankbf(b):
            return PS[:, b * 512:(b + 1) * 512].bitcast(BF16)

        qz = RA[:, :].rearrange("p (c t) -> p c t", c=8)
        ka = RDK[:, 0:8192].rearrange("p (c t) -> p c t", c=4)
        hT = RB[:, :].rearrange("p (c t) -> p c t", c=8)
        OT = RC[:, :].rearrange("p (c t) -> p c t", c=8)
        xn_bf = RC[:, 0:8192].rearrange("p (j f) -> p j f", j=8)
        cs_sb = RC[:, 8192:16384].bitcast(F32)
        cosT = cs_sb[:, 0:2048]
        sinT = cs_sb[:, 2048:4096]
        xs = RA[:, :].bitcast(F32).rearrange("p (j f) -> p j f", j=8)
        dvaug = RDK[:, 8192:16512].rearrange("p (t h d) -> p t h d", t=16, h=4)
        svb = RDK[:, 8192:16384].rearrange("p (t f) -> p t f", t=16)
        uT = RDK[:, 0:16384].rearrange("p (c t) -> p c t", c=8)
        wslot = [RE[:, s * 4096:(s + 1) * 4096].rearrange("p (k n) -> p k n", k=8) for s in range(4)]
        woutg = RE[:, 0:8192].rearrange("p (k n) -> p k n", k=8)
        w2slot = [RE[:, 8192 + s * 4096:8192 + (s + 1) * 4096].rearrange("p (k n) -> p k n", k=4) for s in range(2)]
        xr_lo = RA[:, :].bitcast(F32).rearrange("p (j f) -> p j f", j=8)
        xr_hi = RB[:, :].bitcast(F32).rearrange("p (j f) -> p j f", j=8)

        def xr(tt):
            return xr_lo[:, tt, :] if tt < 8 else xr_hi[:, tt - 8, :]

        lneg = RB[:, 0:8192].rearrange("p (i t) -> p i t", i=16)
        stage = [RG[:, 0:1024], RG[:, 1024:2048]]
        cfm = small[:, 0:8]
        cact = small[:, 8:16]
        gfm = small[:, 16:32]
        modT = small[:, 32:64]
        a_m = small[:, 64:72]
        a_f = small[:, 72:80]
        ssq = small[:, 80:96]
        rstd = small[:, 96:112]
        lamw = small[:, 112:116]
        neglam = small[:, 116:117]
        ssq2 = small[:, 120:136]
        rstd2 = small[:, 136:152]
        ssq3 = small[:, 152:168]
        rstd3 = small[:, 168:184]
        dsm = small[:, 184:256]
        crep = sb("crep", [P, 8, 128], BF16)
        lam_sb = sb("lam_sb", [P, 4, 64], F32)
        subgs = sb("subgs", [P, 128], F32)
        fngs = gm_bc
        junk_t = sb("junk", [P, 2 * D], BF16)
        junk = junk_t[:, 0:D]
        junk_alt = junk_t[:, D:2 * D]

        add("sp", DMA(cb[:, :], cb_d), writes=["cb"])
        add("sp", DMA(small[:, 0:8], cfm_d), writes=["cfm"])
        add("sp", DMA(small[:, 16:32], gfm_d), writes=["gfm"])
        add("sp", DMA(identf[:, :], cf_d), writes=["identf"])
        add("sp", DMA(ind[:, :], ind_d), writes=["ind"])
        add("sp", DMA(lam_sb[:, :, :], lam_d), writes=["lam_sb"])
        add("sp", DMA(subgs[:, :], subg_d), writes=["subgs"])
        add("pool", MEMSET(small[:, 80:96], 0.0), writes=["xn_ss"])
        add("pool", MEMSET(small[:, 120:136], 0.0), writes=["xn2_ss"])
        add("pool", MEMSET(small[:, 152:168], 0.0), writes=["ssq3"])
        add("pool", MEMSET(small[:, 184:256], 0.0), writes=[("dsm", k_, 3) for k_ in range(2)])

        add("act", ACT(cact, cfm, AF.Silu), reads=["cfm"], writes=["cact"])
        add("dve", CP(crep[:, :, :], cact.unsqueeze(2).to_broadcast([P, 8, 128])), reads=["cact"], writes=["crep"])
        add("dve", TT(lam_sb[:, 0:2, :], lam_sb[:, 0:4:2, :], lam_sb[:, 1:4:2, :], ALU.mult), reads=["lam_sb"], writes=["lam_sb2"])
        add("dve", RED(lamw[:, 0:2], lam_sb[:, 0:2, :]), reads=["lam_sb2"], writes=["lamw"])
        add("act", ACT(lamw[:, 2:4], lamw[:, 0:2], AF.Exp), reads=["lamw"], writes=["lamw2"])
        add("dve", STT(neglam, lamw[:, 3:4], -LAMBDA_INIT, lamw[:, 2:3], ALU.add, ALU.subtract), reads=["lamw2"], writes=["neglam"])
        add("dve", TS(subgs[:, :], subgs[:, :], (1.0 - LAMBDA_INIT) * math.sqrt(128.0), None, ALU.mult), reads=["subgs"], writes=["subgs"])

        for t_ in range(8):
            add("sp", DMA(xs[:, t_ % 8, :], x_d[t_ * 128:(t_ + 1) * 128, :]), writes=[("xs", t_ % 8)])
        add("sp", DMA(cs_sb, cs_d), writes=["cs"])
        adaw_v = adaw_d.rearrange("(k p) n -> p k n", p=P)
        wcount = [0]

        def wslot_load(src_ap, tag):
            s = wcount[0] % 4
            wcount[0] += 1
            add("poolq", DMA(wslot[s][:, :, :], src_ap), writes=[("w", s)])
            return s

        adab_slots = [RF[:, 0:1024].bitcast(F32), RF[:, 1024:2048].bitcast(F32)]
        tmp_mod_e = RF[:, 2048:3072].bitcast(F32)
        tmp_mod2_e = RF[:, 3072:4096].bitcast(F32)
        piece_order = [2, 3, 0, 1, 4, 5, 6, 7, 8, 9, 10, 11]
        modcol = {0: 0, 1: 4, 2: 8, 3: 12, 6: 16, 7: 20, 8: 24, 9: 28}
        def ada_piece(n_, pc, late=None):
            if late is None:
                s = wslot_load(adaw_v[:, :, pc * 512:(pc + 1) * 512], "ada")
                ab = adab_slots[n_ % 2]
                abk = ("adab", n_ % 2)
                add("sp", DMA(ab, adab_d[:, pc * 512:(pc + 1) * 512]), writes=[abk])
                b = n_ % 2
                tmp_mod, tmp_mod2, tk, tk2, aft = tmp_mod_e, tmp_mod2_e, "tmp_mod", "tmp_mod2", []
            else:
                s, ab, abk, b = late["slot"], late["ab"], late["abk"], 7
                tmp_mod, tmp_mod2, tk, tk2, aft = late["tmp"], late["tmp2"], "tmp_modB", "tmp_mod2B", late["after"]
                if late["stage"] == "load":
                    add("poolq", DMA(wslot[s][:, :, :], adaw_v[:, :, pc * 512:(pc + 1) * 512]), writes=[("w", s)])
                    add("sp", DMA(ab, adab_d[:, pc * 512:(pc + 1) * 512]), writes=[abk], after=aft)
                    return
            for kc in range(8):
                add("pe", MM(bank(b), crep[:, kc, :], wslot[s][:, kc, :], start=(kc == 0), stop=(kc == 7)),
                    reads=["crep", ("w", s)], writes=[("ps", b)])
            if pc in (4, 5):
                add("dve", TT(gm_bc[:, (pc - 4) * 512:(pc - 3) * 512], bank(b), ab, ALU.add),
                    reads=[("ps", b), abk], writes=[("gm", pc - 4)])
            elif pc in (10, 11):
                add("dve", TT(gf_bc[:, (pc - 10) * 512:(pc - 9) * 512], bank(b), ab, ALU.add),
                    reads=[("ps", b), abk], writes=[("gf", pc - 10)])
            else:
                add("dve", TT(tmp_mod, bank(b), ab, ALU.add), reads=[("ps", b), abk], writes=[tk], after=aft)
                add("dve", TT(tmp_mod2.rearrange("p (a b) -> p a b", a=4), tmp_mod.rearrange("p (a b) -> p a b", a=4),
                              identf[:, :].unsqueeze(1).to_broadcast([P, 4, 128]), ALU.mult),
                    reads=[tk, "identf"], writes=[tk2], after=aft)
                c0 = modcol[pc]
                add("dve", RED(modT[:, c0:c0 + 4], tmp_mod2.rearrange("p (a b) -> p a b", a=4)),
                    reads=[tk2], writes=[("modT", c0)])
            if pc == 3:
                add("dve", STT(a_m, modT[:, 8:16], 1.0, gfm[:, 0:8], ALU.add, ALU.mult),
                    reads=[("modT", 8), ("modT", 12), "gfm"], writes=["a_m"])
            if pc == 9:
                add("dve", STT(a_f, modT[:, 24:32], 1.0, gfm[:, 8:16], ALU.add, ALU.mult),
                    reads=[("modT", 24), ("modT", 28), "gfm"], writes=["a_f"])

        for n_, pc in enumerate(piece_order[:4]):
            ada_piece(n_, pc)

        mark("p0")
        def norm_phase(src_tile, ssq_, rstd_, xnbuf, aff_a, aff_sh, dstT, src_reads, xn_key, dst_key, banks, junk, extra_reads, xn_after=(), dst_after=(), pre_group=None, groups=None):
            bi = [0]
            for tt in [4 * g_ + j_ for g_ in (groups if groups is not None else range(NU)) for j_ in range(4)]:
                U = tt // 4
                slot = tt % 8
                if pre_group is not None and tt % 4 == 0:
                    pre_group(U)
                add("act", ACT(junk if tt % 2 == 0 else junk_alt, src_tile(tt), AF.Square, accum_out=ssq_[:, tt:tt + 1]),
                    reads=src_reads(tt) + [xn_key + "_ss"], writes=[(xn_key + "_ssq", tt), ("junk", tt % 2)])
                if tt % 4 == 3:
                    g0 = 4 * U
                    mark("n_sq")
                    add("act", ACT(rstd_[:, g0:g0 + 4], ssq_[:, g0:g0 + 4], AF.Ln, bias=D * EPS),
                        reads=[(xn_key + "_ssq", g0 + j) for j in range(4)], writes=[(xn_key + "_ln", U)])
                    add("act", ACT(rstd_[:, g0:g0 + 4], rstd_[:, g0:g0 + 4], AF.Exp, scale=-0.5),
                        reads=[(xn_key + "_ln", U)], writes=[(xn_key + "_rstd", U)])
                    mark("n_ln")
                    for j in range(4):
                        t_ = g0 + j
                        add("dve", TS(xnbuf[:, t_ % 8, :], src_tile(t_), rstd_[:, t_:t_ + 1], 32.0, ALU.mult, ALU.mult),
                            reads=src_reads(t_) + [(xn_key + "_rstd", U)], writes=[(xn_key, t_ % 8)], after=xn_after)
                    mark("n_norm")
                    for kp in range(4):
                        b = banks[bi[0] % len(banks)]
                        bi[0] += 1
                        pb = bankbf(b)
                        for k2 in range(2):
                            kc = kp * 2 + k2
                            for j in range(4):
                                add("pe", TR(pb[:, k2 * 512 + j * 128:k2 * 512 + (j + 1) * 128],
                                             xnbuf[:, (U % 2) * 4 + j, kc * 128:(kc + 1) * 128], ident),
                                    reads=[(xn_key, (U % 2) * 4 + j), "cb"], writes=[("ps", b)])
                        mark("n_tr")
                        for k2 in range(2):
                            kc = kp * 2 + k2
                            dst = dstT[:, kc, U * 512:(U + 1) * 512]
                            src = pb[:, k2 * 512:(k2 + 1) * 512]
                            if kp % 2 == 0:
                                add("act", ACT(dst, src, AF.Identity, scale=aff_a[:, kc:kc + 1], bias=aff_sh[:, kc:kc + 1]),
                                    reads=[("ps", b)] + extra_reads, writes=[(dst_key, kc, U)], after=dst_after)
                                mark("n_ea")
                            else:
                                add("dve", TS(dst, src, aff_a[:, kc:kc + 1], aff_sh[:, kc:kc + 1], ALU.mult, ALU.add),
                                    reads=[("ps", b)] + extra_reads, writes=[(dst_key, kc, U)], after=dst_after)

        win_v = win_d.rearrange("(k p) n -> p k n", p=P)
        pre_slots = {}
        for pcs_ in ((0, 6), (1, 7)):
            pre_slots[pcs_] = (wslot_load(win_v[:, :, pcs_[0] * 512:(pcs_[0] + 1) * 512], "win"),
                               wslot_load(win_v[:, :, pcs_[1] * 512:(pcs_[1] + 1) * 512], "win"))

        def load_x_group(U):
            groups = [] if U == 0 else ([U + 1] if U + 1 < NU else [])
            for g_ in groups:
                for j in range(4):
                    t_ = 4 * g_ + j
                    add("sp", DMA(xs[:, t_ % 8, :], x_d[t_ * 128:(t_ + 1) * 128, :]), writes=[("xs", t_ % 8)])

        norm_phase(lambda tt: xs[:, tt % 8, :], ssq, rstd, xn_bf, a_m, modT[:, 0:8], hT,
                   lambda tt: [("xs", tt % 8)], "xn", "hT", [0, 1, 2, 3], junk, ["a_m", ("modT", 0), ("modT", 4)],
                   pre_group=load_x_group)

        mark("p1")
        t1s = [RF[:, 5120 + i * 1024:5120 + (i + 1) * 1024].bitcast(F32) for i in range(2)]
        t2s = [RF[:, 7168 + i * 512:7168 + (i + 1) * 512] for i in range(0)]
        t2a = [RG[:, 0:512], RG[:, 512:1024], RG[:, 1024:1536], RG[:, 1536:2048]]
        ropec = [0]
        pbk = [4, 5, 6, 7]
        xs_keys = [("xs", i) for i in range(8)]
        for c_ in range(4):
            add("pool", MEMSET(qz[64:128, 2 * c_, :], 0.0), after=xs_keys, writes=[("qzero", 2 * c_)])
            add("dve", MEMSET(qz[0:64, 2 * c_ + 1, :], 0.0), after=xs_keys, writes=[("qzero", 2 * c_ + 1)])
        for (pa, pb_, dest, dkey, scale) in [(0, 6, None, "qz", 0.125), (1, 7, ka, "ka", 1.0)]:
            sa, sb_ = pre_slots[(pa, pb_)]
            for cc in range(4):
                for U in range(NU):
                    r = ropec[0]
                    ropec[0] += 1
                    ba = pbk[(2 * r) % 4]
                    bb = pbk[(2 * r + 1) % 4]
                    for kc in range(8):
                        add("pe", MM(bank(ba), wslot[sa][:, kc, cc * 128:(cc + 1) * 128], hT[:, kc, U * 512:(U + 1) * 512],
                                     start=(kc == 0), stop=(kc == 7)),
                            reads=[("w", sa), ("hT", kc, U)], writes=[("ps", ba)])
                    for kc in range(8):
                        add("pe", MM(bank(bb), wslot[sb_][:, kc, cc * 128:(cc + 1) * 128], hT[:, kc, U * 512:(U + 1) * 512],
                                     start=(kc == 0), stop=(kc == 7)),
                            reads=[("w", sb_), ("hT", kc, U)], writes=[("ps", bb)])
                    t1 = t1s[r % 2]
                    t2 = t2a[r % 2]
                    add("dve", STT(t1, bank(ba), scale, cosT[:, U * 512:(U + 1) * 512], ALU.mult, ALU.mult),
                        reads=[("ps", ba), "cs"], writes=[("t1", r % 2)])
                    add("dve", STT(t2, bank(bb), scale, sinT[:, U * 512:(U + 1) * 512], ALU.mult, ALU.mult),
                        reads=[("ps", bb), "cs"], writes=[("t2", r % 2)])
                    if dest is None:
                        for m_ in range(2):
                            add("dve", TT(qz[m_ * 64:(m_ + 1) * 64, 2 * cc + m_, U * 512:(U + 1) * 512],
                                           t1[m_ * 64:(m_ + 1) * 64, :], t2[m_ * 64:(m_ + 1) * 64, :], ALU.add),
                                reads=[("t1", r % 2), ("t2", r % 2), ("qzero", 2 * cc + m_)], writes=[("qz", 2 * cc + m_, U)])
                    else:
                        add("pool", TT(dest[:, cc, U * 512:(U + 1) * 512], t1, t2, ALU.add),
                            reads=[("t1", r % 2), ("t2", r % 2)], writes=[(dkey, cc, U)])
            if dest is None:
                sv_ = wslot_load(win_v[:, :, 2 * 512:3 * 512], "win")
                s_sq = wslot_load(win_v[:, :, 3 * 512:4 * 512], "win")
            else:
                s_sk = wslot_load(win_v[:, :, 4 * 512:5 * 512], "win")
                s_sv = wslot_load(win_v[:, :, 5 * 512:6 * 512], "win")
        add("pool", MEMSET(dvaug[:, :, :, 128:130], 1.0), writes=["dvones"])
        for tt in range(NT):
            b = pbk[tt % 4]
            for kc in range(8):
                add("pe", MM(bank(b), hT[:, kc, tt * 128:(tt + 1) * 128], wslot[sv_][:, kc, :], start=(kc == 0), stop=(kc == 7)),
                    reads=[("w", sv_), ("hT", kc, tt // 4)], writes=[("ps", b)])
            eng = "act" if tt % 2 == 0 else "dve"
            src = bank(b).rearrange("p (h d) -> p h d", h=4)
            if eng == "act":
                add("act", ACT(dvaug[:, tt, :, 0:128], src, AF.Copy), reads=[("ps", b), "dvones"], writes=[("dv", tt)])
            else:
                add("dve", CP(dvaug[:, tt, :, 0:128], src), reads=[("ps", b), "dvones"], writes=[("dv", tt)])

        if debug:
            add("sp", DMA(dbg["h1"], RB[:, :]), reads=[("hT", kc, U) for kc in range(8) for U in range(NU)], writes=["dbgh1"])

        mark("p1a")
        et = [RF[:, k_ * 512:(k_ + 1) * 512].rearrange("p (m t) -> p m t", m=2) for k_ in range(4)]
        od = [RF[:, 2048 + k_ * 256:2048 + (k_ + 1) * 256].rearrange("p (j f) -> p j f", j=2) for k_ in range(2)]
        tmpd = [RF[:, 3072:3328].bitcast(F32), RF[:, 3328:3584].bitcast(F32)]
        od32 = [RF[:, 3584 + i * 256:3840 + i * 256].bitcast(F32) for i in range(4)]
        phase_rf = ["tmp_mod", "tmp_mod2", ("adab", 0), ("adab", 1)]
        NSL = 4
        DIST = 3
        djobs = []
        pend = []
        rnd = 0
        for h in range(4):
            for R in range(8):
                for i in range(2 * R + 2):
                    djobs.append(("t", h, R, i, rnd % 2))
                    for p_ in pend:
                        p_[0] -= 1
                    while pend and pend[0][0] <= 0:
                        djobs.append(pend.pop(0)[1])
                pend.append([5, ("f3", h, R, 0, rnd % 2)])
                rnd += 1
        for _ in range(DIST + 2):
            djobs.append(("nop", 0, 0, 0, 0))
        for p_ in pend:
            djobs.append(p_[1])

        def d_s1(job, sl, first):
            kind, h, R, i, fs = job
            if kind == "nop":
                return
            if kind == "f3":
                sm = dsm[:, fs * 16:fs * 16 + 16]
                for j in range(2):
                    add("dve", (lambda o_, i_, a_: (lambda g: g.scalar_tensor_tensor(o_, i_, 1.0, i_, ALU.mult, ALU.mult, accum_out=a_)))(
                        junk_t[:, (fs * 2 + j) * 128:(fs * 2 + j + 1) * 128], od32[fs * 2 + j], sm[:, 8 + j:9 + j]),
                        reads=[("od32", fs * 2 + j), ("dsm", fs, 3)], writes=[("dsm", fs, 4, j), ("junk", 0)])
                add("act", ACT(sm[:, 12:14], sm[:, 8:10], AF.Ln, bias=128.0 * EPS), reads=[("dsm", fs, 4, j) for j in range(2)], writes=[("dsm", fs, 5)])
                add("act", ACT(sm[:, 12:14], sm[:, 12:14], AF.Exp, scale=-0.5), reads=[("dsm", fs, 5)], writes=[("dsm", fs, 6)])
                for j in range(2):
                    add("dve", STT(od[fs][:, j, :], od32[fs * 2 + j], sm[:, 12 + j:13 + j], subgs[:, :], ALU.mult, ALU.mult),
                        reads=[("od32", fs * 2 + j), ("dsm", fs, 6), "subgs"], writes=[("od", fs, j)])
                add("dve", MEMSET(sm[:, 8:10], 0.0), reads=[("dsm", fs, 5)], writes=[("dsm", fs, 3)])
                return
            t0 = max(R * 256, i * 128)
            w = (R + 1) * 256 - t0
            for m in range(2):
                add("pe", MM(bank(sl)[:, m * 256:m * 256 + w], ka[:, h, i * 128:(i + 1) * 128], qz[:, 2 * h + m, t0:t0 + w]),
                    reads=[("ka", h, i // 4), ("qzero", 2 * h + m), ("qz", 2 * h + m, t0 // 512)], writes=[("ps", sl)])
            scv = bank(sl).rearrange("p (m t) -> p m t", m=2)
            add("act", ACT(et[sl][:, :, 0:w], scv[:, :, 0:w], AF.Exp),
                reads=[("ps", sl)], writes=[("et", sl)], after=(phase_rf + [("t1", 0), ("t1", 1)] if first else []))
            if i >= 2 * R:
                add("pool", MEMSET(et[sl][64:128, :, 0:64], 0.0), reads=[], writes=[("et", sl)])

        def d_s2(job, sl):
            kind, h, R, i, fs = job
            if kind == "nop":
                return
            if kind == "f3":
                pb = bankbf(sl)
                for j in range(2):
                    add("pe", TR(pb[:, j * 128:(j + 1) * 128], od[fs][:, j, :], ident),
                        reads=[("od", fs, j), "cb"], writes=[("ps", sl)])
                add("dve", CP(OT[:, h, R * 256:(R + 1) * 256], pb[:, 0:256]),
                    reads=[("ps", sl)], after=["cs"] + [("xn", s_) for s_ in range(8)], writes=[("OT", h, R)])
                return
            t0 = max(R * 256, i * 128)
            for T in range(max(2 * R, i), 2 * R + 2):
                bacc = 4 + 2 * fs + (T % 2)
                c0 = T * 128 - t0
                for m in range(2):
                    add("pe", MM(bank(bacc)[:, m * 132:m * 132 + 129], et[sl][:, m, c0:c0 + 128], dvaug[:, i, h, 0:129],
                                 start=(i == 0 and m == 0), stop=(i == T), skip=True),
                        reads=[("et", sl), ("dv", i)], writes=[("ps", bacc)])
            if i != 2 * R + 1:
                return
            sm = dsm[:, fs * 16:fs * 16 + 16]
            ab0 = 4 + 2 * fs
            accs = [("ps", ab0), ("ps", ab0 + 1)]
            add("dve", RECIP(sm[:, 0:2], PS[:, ab0 * 512 + 128:(ab0 + 2) * 512:512]), reads=accs, writes=[("dsm", fs, 0)])
            add("dve", RECIP(sm[:, 4:6], PS[:, ab0 * 512 + 260:(ab0 + 2) * 512:512]), reads=accs, writes=[("dsm", fs, 1)])
            add("dve", TS(sm[:, 4:6], sm[:, 4:6], neglam, None, ALU.mult), reads=[("dsm", fs, 1), "neglam"], writes=[("dsm", fs, 1)])
            for j in range(2):
                bacc = ab0 + j
                acc0 = bank(bacc)[:, 0:128]
                acc1 = bank(bacc)[:, 132:260]
                o32 = od32[fs * 2 + j]
                add("dve", TS(tmpd[j], acc1, sm[:, 4 + j:5 + j], None, ALU.mult), reads=[("ps", bacc), ("dsm", fs, 1)], writes=[("tmpd", j)])
                add("dve", STT(o32, acc0, sm[:, j:j + 1], tmpd[j], ALU.mult, ALU.add),
                    reads=[("ps", bacc), ("dsm", fs, 0), ("tmpd", j)], writes=[("od32", fs * 2 + j)])

        hist = []
        for n_, job in enumerate(djobs + [None] * DIST):
            if job is not None:
                d_s1(job, n_ % NSL, n_ == 0)
            hist.append(job)
            if n_ >= DIST and hist[n_ - DIST] is not None:
                d_s2(hist[n_ - DIST], (n_ - DIST) % NSL)

        mark("p2a")
        pc_ = [0]
        dv_all = [("dv", t) for t in range(16)] + ["dvones"]
        for (sw, dest, dkey, scale) in [(s_sq, None, "qz", 0.125), (s_sk, ka, "ka", 1.0)]:
            for cc in range(4):
                for U in range(NU):
                    b = pc_[0] % 8
                    pc_[0] += 1
                    for kc in range(8):
                        add("pe", MM(bank(b), wslot[sw][:, kc, cc * 128:(cc + 1) * 128], hT[:, kc, U * 512:(U + 1) * 512],
                                     start=(kc == 0), stop=(kc == 7)),
                            reads=[("w", sw), ("hT", kc, U)], writes=[("ps", b)])
                    if dest is None:
                        for m_ in range(2):
                            dst_ = qz[m_ * 64:(m_ + 1) * 64, 2 * cc + m_, U * 512:(U + 1) * 512]
                            src_ = bank(b)[m_ * 64:(m_ + 1) * 64, :]
                            if pc_[0] % 2 == 0:
                                add("act", ACT(dst_, src_, AF.Identity, scale=scale), reads=[("ps", b)], writes=[("qz", 2 * cc + m_, U)])
                            else:
                                add("dve", TS(dst_, src_, scale, None, ALU.mult), reads=[("ps", b)], writes=[("qz", 2 * cc + m_, U)])
                    elif pc_[0] % 2 == 0:
                        add("act", ACT(dest[:, cc, U * 512:(U + 1) * 512], bank(b), AF.Identity, scale=scale),
                            reads=[("ps", b)], writes=[(dkey, cc, U)])
                    else:
                        add("dve", TS(dest[:, cc, U * 512:(U + 1) * 512], bank(b), scale, None, ALU.mult),
                            reads=[("ps", b)], writes=[(dkey, cc, U)])
        for tt in range(NT):
            b = pc_[0] % 8
            pc_[0] += 1
            for kc in range(8):
                add("pe", MM(bank(b), hT[:, kc, tt * 128:(tt + 1) * 128], wslot[s_sv][:, kc, :], start=(kc == 0), stop=(kc == 7)),
                    reads=[("w", s_sv), ("hT", kc, tt // 4)], writes=[("ps", b)])
            if tt % 2 == 0:
                add("act", ACT(svb[:, tt, :], bank(b), AF.Copy), reads=[("ps", b)], after=dv_all, writes=[("sv", tt)])
            else:
                add("dve", CP(svb[:, tt, :], bank(b)), reads=[("ps", b)], after=dv_all, writes=[("sv", tt)])

        stc = [0]

        def wout_prep(kcs, fin_):
            for kc in kcs:
                ss_ = stc[0] % 2
                stc[0] += 1
                add("sp", DMA(stage[ss_], wout_d[kc * 128:(kc + 1) * 128, :]), writes=[("stage", ss_), ("t2", 0), ("t2", 1)])
                add("pool", TT(woutg[:, kc, :], stage[ss_], gm_bc[:, :], ALU.mult),
                    reads=[("stage", ss_), ("gm", 0), ("gm", 1)], writes=[("w", 0), ("w", 1)] if kc == 0 else [("woutg", kc)])
            if fin_:
                add("sp", DMA(fngs[:, :], fng_d), writes=["fngs"], after=[("gm", 0), ("gm", 1)])
                add("pool", TS(fngs[:, :], fngs[:, :], 32.0, None, ALU.mult), reads=["fngs"], writes=["fngs"])

        mark("p1b")
        etmp = [RF[:, 4096:5120].bitcast(F32), RF[:, 5120:6144].bitcast(F32)]
        at = [RF[:, 6144:6656], RF[:, 6656:7168], RF[:, 7168:7680]]
        negr = [RF[:, 7680:8192], RF[:, 0:512]]
        hT_keys = [("hT", kc, U) for kc in range(8) for U in range(NU)]
        units = [(U, h) for U in range(NU) for h in range(8)]
        BB = 2
        sbc = {"za": 0, "zb": 0, "at": 0}

        def geom(U, i):
            t0 = max(U * 512, i * 128)
            w = (U + 1) * 512 - t0
            return t0, w, t0 - U * 512

        def qk_reads(cc, U, i, t0, h_):
            return [("ka", cc, i // 4)] + [("qz", h_, uu) for uu in range(t0 // 512, U + 1)]

        def A1(un, i, st):
            U, h = units[un]
            cc, r0 = h // 2, (h % 2) * 64
            t0, w, c0 = geom(U, i)
            b = sbc["za"] % 2
            sbc["za"] += 1
            st["zab"] = b
            add("pe", MM(bank(b)[:, 0:w], ka[:, cc, i * 128:(i + 1) * 128], qz[:, h, t0:t0 + w]),
                reads=qk_reads(cc, U, i, t0, h), writes=[("ps", b)])
            add("act", ACT(bank(b)[:, 0:w], bank(b)[:, 0:w], AF.Exp), reads=[("ps", b)], writes=[("ps", b)])
            add("act", ACT(lneg[:, i, 0:w], bank(b)[:, 0:w], AF.Ln, bias=1.0), reads=[("ps", b)],
                after=(hT_keys if un < 8 else []), writes=[("lneg", i)])
            if i >= 4 * U:
                add("dve", TT(lneg[:, i, 0:128], lneg[:, i, 0:128], masku, ALU.mult), reads=[("lneg", i), "cb"], writes=[("lneg", i)])

        def A2(un, i):
            U, h = units[un]
            nI = 4 * U + 4
            t0, w, c0 = geom(U, i)
            add("pe", MM(bank(BB)[:, c0:c0 + w], wsel_i(i), lneg[:, i, 0:w], start=(i == 0), stop=(i == nI - 1)),
                reads=[("lneg", i), "cb"], writes=[("ps", BB)])
            if i == nI - 1:
                add("dve", CP(negr[un % 2][:, :], bank(BB)[:, :]), reads=[("ps", BB)], writes=[("negr", un % 2)],
                    after=([("et", 0), ("et", 1), ("et", 2), ("et", 3)] if un < 2 else []))

        def B1(un, i, st):
            U, h = units[un]
            cc, r0 = h // 2, (h % 2) * 64
            nI = 4 * U + 4
            t0, w, c0 = geom(U, i)
            b = 3 + (sbc["zb"] % 2)
            sbc["zb"] += 1
            add("pe", MM(bank(b)[:, 0:w], ka[:, cc, i * 128:(i + 1) * 128], qz[:, h, t0:t0 + w], start=True, stop=False),
                reads=qk_reads(cc, U, i, t0, h), writes=[("ps", b)])
            if i < nI - 1:
                add("pe", MM(bank(b)[:, 0:w], indv[:, i, :], negr[un % 2][:, c0:c0 + w], start=False, stop=False),
                    reads=[("negr", un % 2), "ind"], writes=[("ps", b)])
            add("pe", MM(bank(b)[:, 0:w], trineg, lneg[:, i, 0:w], start=False, stop=True),
                reads=[("lneg", i), "cb"], writes=[("ps", b)])
            k = sbc["at"] % 3
            sbc["at"] += 1
            st[("at", i)] = k
            add("act", ACT(at[k][:, 0:w], bank(b)[:, 0:w], AF.Exp), reads=[("ps", b)], writes=[("at", k)],
                after=([("t1", 0), ("t1", 1)] if un == 0 else []))
            if i >= 4 * U:
                add("dve", TT(at[k][:, 0:128], at[k][:, 0:128], masku, ALU.mult), reads=[("at", k), "cb"], writes=[("at", k)])

        def B2(un, i, st):
            U, h = units[un]
            cc, r0 = h // 2, (h % 2) * 64
            nI = 4 * U + 4
            t0, w, c0 = geom(U, i)
            ob = 5 + (un % 2)
            k = st[("at", i)]
            add("pe", MM(bank(ob)[:, c0:c0 + w], svb[:, i, cc * 128:(cc + 1) * 128], at[k][:, 0:w], start=(i == 0), stop=(i == nI - 1)),
                reads=[("at", k), ("sv", i)], writes=[("ps", ob)])
            if i == nI - 1:
                if False:
                    add("act", ACT(OT[r0:r0 + 64, 4 + cc, U * 512:(U + 1) * 512], bank(ob)[r0:r0 + 64, :], AF.Copy),
                        reads=[("ps", ob)], writes=[("OT", 4 + cc, U, h % 2)])
                else:
                    add("dve", CP(OT[r0:r0 + 64, 4 + cc, U * 512:(U + 1) * 512], bank(ob)[r0:r0 + 64, :]),
                        reads=[("ps", ob)], writes=[("OT", 4 + cc, U, h % 2)])

        nI0 = 4 * units[0][0] + 4
        stA = {}
        for t in range(nI0 + 1):
            if t < nI0:
                A1(0, t, stA)
            if t >= 1:
                A2(0, t - 1)
        late_pcs = piece_order[4:]
        diff_scratch = [("et", k_) for k_ in range(4)] + [("od", f_, j_) for f_ in range(2) for j_ in range(2)] + \
                       [("tmpd", 0), ("tmpd", 1)] + [("od32", j_) for j_ in range(4)]
        late_ab = [RF[:, 512:1536].bitcast(F32), RF[:, 4096:5120].bitcast(F32)]

        def late_cfg(k_, stage_):
            return dict(slot=2 + (k_ % 2), ab=late_ab[k_ % 2], abk=("adab2", k_ % 2), tmp=RF[:, 2048:3072].bitcast(F32),
                        tmp2=RF[:, 3072:4096].bitcast(F32), after=(diff_scratch if k_ < 2 else []), stage=stage_)

        for un in range(len(units)):
            if un % 2 == 0 and 10 <= un <= 10 + 2 * (len(late_pcs) - 1):
                k_ = (un - 10) // 2
                ada_piece(4 + k_, late_pcs[k_], late=late_cfg(k_, "compute"))
            if un % 2 == 0 and 8 <= un <= 8 + 2 * (len(late_pcs) - 1):
                k_ = (un - 8) // 2
                ada_piece(4 + k_, late_pcs[k_], late=late_cfg(k_, "load"))
            if 17 <= un <= 24:
                wout_prep([un - 17], un == 24)
            nIu = 4 * units[un][0] + 4
            nIn = 4 * units[un + 1][0] + 4 if un + 1 < len(units) else 0
            stB = {}
            for t in range(max(nIu, nIn) + 1):
                if t < nIu:
                    B1(un, t, stB)
                if t < nIn:
                    A1(un + 1, t, stA)
                if 1 <= t <= nIu:
                    B2(un, t - 1, stB)
                if 1 <= t <= nIn:
                    A2(un + 1, t - 1)


        mark("p2b")
        qk_keys = [("qz", c, U) for c in range(8) for U in range(4)] + [("qzero", c) for c in range(8)]
        ka_keys = [("ka", c, U) for c in range(4) for U in range(4)]
        rb_keys = [("lneg", i) for i in range(16)] + hT_keys
        OT_all = [("OT", c, R_) for c in range(4) for R_ in range(8)] + [("OT", 4 + c, U, k) for c in range(4) for U in range(4) for k in range(2)]
        wg_keys = [("w", 0), ("w", 1)] + [("woutg", kc) for kc in range(1, 8)]
        for tt in range(NT):
            add("sp", DMA(xr(tt), x_d[tt * 128:(tt + 1) * 128, :]),
                writes=[("xr", tt)], after=(["dbgqk"] if debug else []) + (qk_keys if tt < 8 else rb_keys))
        p3 = [0]

        def p3_group(U_):
          for tt in range(4 * U_, 4 * U_ + 4):
            for nh in range(2):
                b = p3[0] % 8
                p3[0] += 1
                for fc in range(8):
                    rk = [("OT", fc, tt // 2)] if fc < 4 else [("OT", fc, tt // 4, 0), ("OT", fc, tt // 4, 1)]
                    add("pe", MM(bank(b), OT[:, fc, tt * 128:(tt + 1) * 128], woutg[:, fc, nh * 512:(nh + 1) * 512], start=(fc == 0), stop=(fc == 7)),
                        reads=rk + wg_keys, writes=[("ps", b)])
                add("dve", TT(xr(tt)[:, nh * 512:(nh + 1) * 512], bank(b), xr(tt)[:, nh * 512:(nh + 1) * 512], ALU.add),
                    reads=[("ps", b), ("xr", tt)], writes=[("xr", tt)])
        mark("p3")
        xn2 = RF[:, :].rearrange("p (j f) -> p j f", j=8)
        h2T = RC[:, :].rearrange("p (c t) -> p c t", c=8)
        junkb = junk[:, :]
        rf_keys = [("at", 0), ("at", 1), ("at", 2), ("negr", 0), ("negr", 1), ("et", 0), ("et", 1), ("et", 2), ("et", 3),
                   ("tmpd", 0), ("tmpd", 1)] + [("od32", j) for j in range(4)]
        def n2_group(U_):
            ot_blk = [("OT", c, R_) for c in range(4) for R_ in (2 * U_, 2 * U_ + 1)] + \
                     [("OT", 4 + c, U_, k) for c in range(4) for k in range(2)]
            norm_phase(lambda tt: xr(tt), ssq2, rstd2, xn2, a_f, modT[:, 16:24], h2T,
                       lambda tt: [("xr", tt)], "xn2", "h2T", [0, 1, 2, 3], junkb,
                       ["a_f", ("modT", 16), ("modT", 20)],
                       xn_after=rf_keys + [("od", 0, j) for j in range(2)] + [("od", 1, j) for j in range(2)],
                       dst_after=ot_blk, groups=[U_])

        p3_group(0)
        p3_group(1)
        n2_group(0)
        p3_group(2)
        n2_group(1)
        p3_group(3)
        n2_group(2)
        n2_group(3)

        w1_v = w1_d.rearrange("(k p) n -> p k n", p=P)
        rbuf = [RF[:, i * 1024:(i + 1) * 1024].bitcast(F32) for i in range(4)]
        xn2_keys = [("xn2", s_) for s_ in range(8)]
        rc_ = [0]
        p4 = [0]
        for fb in range(4):
            ws1 = []
            for half in range(2):
                s = half
                add("poolq", DMA(wslot[s][:, :, :], w1_v[:, :, fb * 1024 + half * 512: fb * 1024 + (half + 1) * 512]),
                    reads=wg_keys if fb == 0 else [], writes=[("w", s)])
                ws1.append(s)
            for half in range(2):
                for fc in range(4):
                    ss_ = stc[0] % 2
                    stc[0] += 1
                    r_ = fb * 1024 + half * 512 + fc * 128
                    add("sp", DMA(stage[ss_], w2_d[r_:r_ + 128, :]), writes=[("stage", ss_)])
                    add("pool", TT(w2slot[half][:, fc, :], stage[ss_], gf_bc[:, :], ALU.mult),
                        reads=[("stage", ss_), ("gf", 0), ("gf", 1)], writes=[("w", 2 + half)] if fc == 0 else [("w2g", half, fc)])
            for half in range(2):
                for fc in range(4):
                    for U in range(NU):
                        b = p4[0] % 8
                        p4[0] += 1
                        for kc in range(8):
                            add("pe", MM(bank(b), wslot[ws1[half]][:, kc, fc * 128:(fc + 1) * 128], h2T[:, kc, U * 512:(U + 1) * 512],
                                         start=(kc == 0), stop=(kc == 7)),
                                reads=[("w", ws1[half]), ("h2T", kc, U)], writes=[("ps", b)])
                        rb_ = rc_[0] % 4
                        rc_[0] += 1
                        add("act", ACT(rbuf[rb_], bank(b), AF.Relu), reads=[("ps", b)], after=(xn2_keys if rc_[0] <= 4 else []), writes=[("rbuf", rb_)])
                        eng = "dve" if rc_[0] % 2 == 0 else "pool"
                        add(eng, TT(uT[:, half * 4 + fc, U * 512:(U + 1) * 512], rbuf[rb_], rbuf[rb_], ALU.mult),
                            reads=[("rbuf", rb_)], after=(ka_keys + [("dv", t) for t in range(16)] + [("sv", t) for t in range(16)] if fb == 0 else []),
                            writes=[("uT", half * 4 + fc, U)])
            for tt in range(NT):
                for nh in range(2):
                    b = p4[0] % 8
                    p4[0] += 1
                    for j in range(8):
                        half, fc = j // 4, j % 4
                        add("pe", MM(bank(b), uT[:, j, tt * 128:(tt + 1) * 128], w2slot[half][:, fc, nh * 512:(nh + 1) * 512],
                                     start=(j == 0), stop=(j == 7)),
                            reads=[("uT", j, tt // 4), ("w", 2 + half)] + [("w2g", half, f_) for f_ in range(1, 4)], writes=[("ps", b)])
                    add("dve", TT(xr(tt)[:, nh * 512:(nh + 1) * 512], bank(b), xr(tt)[:, nh * 512:(nh + 1) * 512], ALU.add),
                        reads=[("ps", b), ("xr", tt)], writes=[("xr", tt)])
                if fb == 3:
                    add("act", ACT(junk if tt % 2 == 0 else junk_alt, xr(tt), AF.Square, accum_out=ssq3[:, tt:tt + 1]),
                        reads=[("xr", tt), "ssq3"], writes=[("ssq3", tt), ("junk", tt % 2)])
                    if tt % 4 == 3:
                        g0 = tt - 3
                        add("act", ACT(rstd3[:, g0:g0 + 4], ssq3[:, g0:g0 + 4], AF.Ln, bias=D * EPS), reads=[("ssq3", g0 + j) for j in range(4)], writes=[("ln3", g0)])
                        add("act", ACT(rstd3[:, g0:g0 + 4], rstd3[:, g0:g0 + 4], AF.Exp, scale=-0.5), reads=[("ln3", g0)], writes=[("rstd3", g0)])
                        for t_ in range(g0, g0 + 4):
                            add("dve", STT(xr(t_), xr(t_), rstd3[:, t_:t_ + 1], fngs[:, :], ALU.mult, ALU.mult),
                                reads=[("xr", t_), ("rstd3", g0), "fngs"], writes=[("xr", t_)])
                            add("sp", DMA(out_d[t_ * 128:(t_ + 1) * 128, :], xr(t_)), reads=[("xr", t_)], writes=[("out", t_)])

        mark("p4")
        if debug:
            add("sp", DMA(dbg["hT"], RC[:, :]), reads=[("h2T", kc, U) for kc in range(8) for U in range(4)], writes=["dbghT"])
        _cut[0] = False
        Sd.emit(nc, final_wait_ops=list(Sd.dma_ops["sp"]))
    return nc


def _rope_tables_host():
    try:
        import jax
        import jax.numpy as jnp
        cpu = jax.devices("cpu")[0]
        with jax.default_device(cpu):
            inv = 1.0 / (10000.0 ** (jnp.arange(0, 64, 2, dtype=jnp.float32) / 64))
            ang = jnp.arange(S, dtype=jnp.float32)[:, None] * inv[None, :]
            ang = jnp.concatenate([ang, ang], axis=-1)
            cos = np.asarray(jnp.cos(ang), dtype=np.float32)
            sin = np.asarray(jnp.sin(ang), dtype=np.float32)
        if cos.shape == (S, 64) and np.isfinite(cos).all() and np.isfinite(sin).all():
            return cos, sin
    except Exception:
        pass
    inv = (1.0 / (np.float32(10000.0) ** (np.arange(0, 64, 2, dtype=np.float32) / np.float32(64)))).astype(np.float32)
    ang = np.arange(S, dtype=np.float32)[:, None] * inv[None, :]
    ang = np.concatenate([ang, ang], axis=-1)
    return np.cos(ang).astype(np.float32), np.sin(ang).astype(np.float32)


def _consts():
    bf = ml_dtypes.bfloat16
    j = np.arange(128)
    cbm = np.zeros((128, 528), np.float32)
    cbm[:, 0:128] = np.eye(128)
    cbm[:, 128:256] = -(j[:, None] >= j[None, :]).astype(np.float32)
    cbm[:, 256:384] = (j[None, :] > j[:, None]).astype(np.float32)
    cbm[:, 384:400] = 1.0
    ind = np.zeros((128, 16, 128), np.float32)
    for i in range(16):
        ind[i, i, :] = -1.0
    cf = np.eye(128, dtype=np.float32)
    cos, sin = _rope_tables_host()
    sgn = np.concatenate([-np.ones(32, np.float32), np.ones(32, np.float32)])
    cosT = np.tile(cos.T, (2, 1))
    sinT = np.tile((sin * sgn[None, :]).T, (2, 1))
    cs = np.concatenate([cosT, sinT], axis=1).astype(np.float32)
    return cbm.astype(bf), ind.reshape(128, 2048).astype(bf), cf, np.ascontiguousarray(cs)


def _prep_inputs(x, c, ada_w, ada_b, mix_norm_g, w_in, lambda_q1, lambda_k1, lambda_q2, lambda_k2,
                 diff_subln_g, w_out, ffn_norm_g, w_ff1, w_ff2, final_norm_g):
    f = np.float32
    x = np.asarray(x, f)
    c = np.asarray(c, f)
    w_in0 = np.asarray(w_in, f)[0]
    perm = np.arange(512).reshape(8, 2, 32)[:, ::-1, :].reshape(512)
    w_in_ext = np.concatenate([w_in0, w_in0[:, 0:512][:, perm], w_in0[:, 512:1024][:, perm]], axis=1)
    cbm, ind, cf, cs = _consts()
    shared = {
        "ada_w": np.ascontiguousarray(np.asarray(ada_w, f)[0]),
        "ada_b": np.ascontiguousarray(np.broadcast_to(np.asarray(ada_b, f)[0][None, :], (P, 6 * D))),
        "gfm": np.ascontiguousarray(np.concatenate([np.asarray(mix_norm_g, f)[0].reshape(8, P).T,
                                                    np.asarray(ffn_norm_g, f)[0].reshape(8, P).T], axis=1)),
        "w_in": np.ascontiguousarray(w_in_ext),
        "lam": np.ascontiguousarray(np.broadcast_to(np.stack([np.asarray(v, f)[0] for v in
                                    (lambda_q1, lambda_k1, lambda_q2, lambda_k2)])[None], (P, 4, 64))),
        "subg": np.ascontiguousarray(np.broadcast_to(np.asarray(diff_subln_g, f)[0][None, :], (P, 128))),
        "w_out": np.ascontiguousarray(np.asarray(w_out, f)[0]),
        "w_ff1": np.ascontiguousarray(np.asarray(w_ff1, f)[0]),
        "w_ff2": np.ascontiguousarray(np.asarray(w_ff2, f)[0]),
        "fng": np.ascontiguousarray(np.broadcast_to(np.asarray(final_norm_g, f)[None, :], (P, D))),
        "cb": cbm, "ind": ind, "cf": cf, "cs": cs,
    }
    in_maps = []
    for b in range(x.shape[0]):
        m = dict(shared)
        m["x"] = np.ascontiguousarray(x[b])
        m["cfm"] = np.ascontiguousarray(c[b].reshape(8, P).T)
        in_maps.append(m)
    return in_maps


_NC_CACHE = {}


def kernel(**inputs):
    in_maps = _prep_inputs(**inputs)
    if "nc" not in _NC_CACHE:
        _NC_CACHE["nc"] = build_nc(False)
    nc = _NC_CACHE["nc"]
    n = len(in_maps)
    res = run_bass_kernel_spmd(nc, in_maps, core_ids=list(range(n)))
    return np.stack([np.asarray(r["out"], np.float32) for r in res.results], axis=0)
```

```python
import contextlib
import math
import numpy as np
import ml_dtypes
import concourse.bass as bass
import concourse.mybir as mybir
from concourse.bass_utils import run_bass_kernel_spmd

F32 = mybir.dt.float32
BF16 = mybir.dt.bfloat16
AF = mybir.ActivationFunctionType
ALU = mybir.AluOpType
AX = mybir.AxisListType

P = 128
S = 2048
D = 1024
NT = 16
NU = 4
DFF = 4096
EPS = 1e-6
LAMBDA_INIT = 0.8 - 0.6 * math.exp(-0.3 * 0)

COMPUTE = ("pe", "act", "dve", "pool")
DMAQ = ("sp", "poolq")
STREAM_OF = {"pe": "pe", "act": "act", "dve": "dve", "pool": "pool", "sp": "sp", "poolq": "pool"}
NDMASEM = 8


class Op:
    __slots__ = ("eng", "fn", "deps", "signal", "sem", "val", "inc", "idx", "prewait")

    def __init__(self, eng, fn):
        self.eng = eng
        self.fn = fn
        self.deps = []
        self.signal = False
        self.sem = None
        self.val = None
        self.inc = 1
        self.idx = 0
        self.prewait = None


class Sched:
    def __init__(self):
        self.streams = {s: [] for s in ("pe", "act", "dve", "pool", "sp")}
        self.last_w = {}
        self.readers = {}
        self.dma_ops = {q: [] for q in DMAQ}

    def add(self, eng, fn, reads=(), writes=(), after=()):
        op = Op(eng, fn)
        raw = set()
        other = set()
        for k in reads:
            w = self.last_w.get(k)
            if w is not None:
                raw.add(w)
            if isinstance(k, tuple) and k[0] == "ps":
                for r in self.readers.get(k, ()):
                    if r.eng != eng:
                        other.add(r)
        for k in list(writes) + list(after):
            w = self.last_w.get(k)
            if w is not None:
                other.add(w)
            for r in self.readers.get(k, ()):
                other.add(r)
        deps = set()
        for d in raw:
            if d.eng == "pe" and eng == "pe":
                continue
            deps.add(d)
        for d in other:
            if d.eng == "pe" and eng == "pe":
                continue
            deps.add(d)
        for d in deps:
            d.signal = True
        op.deps = list(deps)
        if eng in DMAQ:
            op.signal = True
            k = len(self.dma_ops[eng])
            op.idx = k
            if k >= NDMASEM:
                op.prewait = self.dma_ops[eng][k - NDMASEM]
            self.dma_ops[eng].append(op)
        for k in reads:
            self.readers.setdefault(k, []).append(op)
        for k in writes:
            self.last_w[k] = op
            self.readers[k] = []
        self.streams[STREAM_OF[eng]].append(op)
        return op

    def emit(self, nc, final_wait_ops=()):
        with contextlib.ExitStack() as st:
            sems = {}
            for e in COMPUTE:
                sems[e] = st.enter_context(nc.semaphore("s_" + e))
            for q in DMAQ:
                sems[q] = [st.enter_context(nc.semaphore("d_%s%d" % (q, i))) for i in range(NDMASEM)]
            cnt = {e: 0 for e in COMPUTE}
            for ops in self.streams.values():
                for op in ops:
                    if op.eng in COMPUTE:
                        if op.signal:
                            cnt[op.eng] += 1
                            op.sem = sems[op.eng]
                            op.val = cnt[op.eng]
                            op.inc = 1
                    else:
                        op.sem = sems[op.eng][op.idx % NDMASEM]
                        op.val = 16 * (op.idx // NDMASEM + 1)
                        op.inc = 16
            block = st.enter_context(nc.Block())
            engines = {"pe": "tensor", "act": "scalar", "dve": "vector", "pool": "gpsimd", "sp": "sync"}

            def make(stream):
                ops = self.streams[stream]

                def body(eng):
                    waited = {}
                    for op in ops:
                        dl = list(op.deps)
                        if op.prewait is not None:
                            dl.append(op.prewait)
                        mx = {}
                        for d in dl:
                            key = id(d.sem)
                            if waited.get(key, 0) >= d.val:
                                continue
                            if key not in mx or mx[key][1] < d.val:
                                mx[key] = (d.sem, d.val)
                        wl = list(mx.items())
                        for key, (sm, v) in wl[:-1]:
                            waited[key] = v
                            eng.wait_ge(sm, v)
                        ins = op.fn(eng)
                        if wl:
                            key, (sm, v) = wl[-1]
                            waited[key] = v
                            ins._wait_ge(sm, v)
                        if op.signal:
                            ins.then_inc(op.sem, op.inc)
                    if stream == "sp":
                        for d in final_wait_ops:
                            eng.wait_ge(d.sem, d.val)

                return body

            for stream, attr in engines.items():
                getattr(block, attr)(make(stream))


def MM(out, lhsT, rhs, start=True, stop=True, skip=False):
    if skip:
        return lambda g: g.matmul(out, lhsT, rhs, start=start, stop=stop, skip_group_check=True)
    return lambda g: g.matmul(out, lhsT, rhs, start=start, stop=stop)


def TR(out, in_, ident):
    return lambda g: g.transpose(out, in_, ident)


def ACT(out, in_, func, **kw):
    return lambda g: g.activation(out=out, in_=in_, func=func, **kw)


def TS(out, in0, s1, s2, op0, op1=None):
    if op1 is None:
        return lambda g: g.tensor_scalar(out, in0, s1, None, op0)
    return lambda g: g.tensor_scalar(out, in0, s1, s2, op0, op1)


def TT(out, in0, in1, op):
    return lambda g: g.tensor_tensor(out, in0, in1, op)


def STT(out, in0, scalar, in1, op0, op1):
    return lambda g: g.scalar_tensor_tensor(out, in0, scalar, in1, op0, op1)


def CP(out, in_):
    return lambda g: g.tensor_copy(out, in_)


def MEMSET(ap, v):
    return lambda g: g.memset(ap, v)


def DMA(out, in_):
    return lambda g: g.dma_start(out=out, in_=in_)


def RED(out, in_, op=ALU.add):
    return lambda g: g.tensor_reduce(out, in_, AX.X, op)


def RECIP(out, in_):
    return lambda g: g.reciprocal(out, in_)


def build_nc(debug=False):
    nc = bass.Bass("TRN2", target_bir_lowering=False)
    dt = nc.dram_tensor
    x_d = dt("x", [S, D], F32, kind="ExternalInput").ap()
    cfm_d = dt("cfm", [P, 8], F32, kind="ExternalInput").ap()
    adaw_d = dt("ada_w", [D, 6 * D], F32, kind="ExternalInput").ap()
    adab_d = dt("ada_b", [P, 6 * D], F32, kind="ExternalInput").ap()
    gfm_d = dt("gfm", [P, 16], F32, kind="ExternalInput").ap()
    win_d = dt("w_in", [D, 4096], F32, kind="ExternalInput").ap()
    lam_d = dt("lam", [P, 4, 64], F32, kind="ExternalInput").ap()
    subg_d = dt("subg", [P, 128], F32, kind="ExternalInput").ap()
    wout_d = dt("w_out", [D, D], F32, kind="ExternalInput").ap()
    w1_d = dt("w_ff1", [D, DFF], F32, kind="ExternalInput").ap()
    w2_d = dt("w_ff2", [DFF, D], F32, kind="ExternalInput").ap()
    fng_d = dt("fng", [P, D], F32, kind="ExternalInput").ap()
    cb_d = dt("cb", [P, 528], BF16, kind="ExternalInput").ap()
    ind_d = dt("ind", [P, 2048], BF16, kind="ExternalInput").ap()
    cf_d = dt("cf", [P, 128], F32, kind="ExternalInput").ap()
    cs_d = dt("cs", [P, 4096], F32, kind="ExternalInput").ap()
    out_d = dt("out", [S, D], F32, kind="ExternalOutput").ap()
    dbg = {}
    if debug:
        dbg["hT"] = dt("dbg_hT", [P, 8 * S], BF16, kind="ExternalOutput").ap()
        dbg["qk"] = dt("dbg_qk", [P, 8 * S], BF16, kind="ExternalOutput").ap()
        dbg["V"] = dt("dbg_V", [P, 16512], BF16, kind="ExternalOutput").ap()
        dbg["OT"] = dt("dbg_OT", [P, 8 * S], BF16, kind="ExternalOutput").ap()
        dbg["x1"] = dt("dbg_x1", [P, 16 * D], F32, kind="ExternalOutput").ap()
        dbg["mod"] = dt("dbg_mod", [P, 64], F32, kind="ExternalOutput").ap()
        dbg["h1"] = dt("dbg_h1", [P, 8 * S], BF16, kind="ExternalOutput").ap()
        dbg["qk0"] = dt("dbg_qk0", [P, 8 * S], BF16, kind="ExternalOutput").ap()

    Sd = Sched()
    import os as _os
    _stop = _os.environ.get("KSTOP", "")
    _cut = [False]

    def add(*a, **k):
        if _cut[0]:
            return None
        return Sd.add(*a, **k)

    def mark(name):
        if name == _stop:
            _cut[0] = True
    with contextlib.ExitStack() as st:
        sb = lambda name, shape, dtype: st.enter_context(nc.sbuf_tensor(name, shape, dtype))
        RA = sb("RA", [P, 16384], BF16)
        RDK = sb("RDK", [P, 8192 + 8320], BF16)
        RB = sb("RB", [P, 16384], BF16)
        RC = sb("RC", [P, 16384], BF16)
        RE = sb("RE", [P, 16384], BF16)
        RF = sb("RF", [P, 8192], BF16)
        RG = sb("RG", [P, 2048], F32)
        cb = sb("cbs", [P, 528], BF16)
        ind = sb("inds", [P, 2048], BF16)
        identf = sb("identf", [P, 128], F32)
        gm_bc = sb("gm_bc", [P, D], F32)
        gf_bc = sb("gf_bc", [P, D], F32)
        small = sb("small", [P, 256], F32)
        PS = st.enter_context(nc.psum_tensor("PS", [P, 4096], F32))

        ident = cb[:, 0:128]
        trineg = cb[:, 128:256]
        masku = cb[:, 256:384]
        vsel = cb[:, 384:528]

        def wsel_i(i):
            return vsel[:, 16 - i:16 - i + 128]
        indv = ind[:, :].rearrange("p (i s) -> p i s", i=16)

        def bank(b, n=1):
            return PS[:, b * 512:(b + n) * 512]

        def bankbf(b):
            return PS[:, b * 512:(b + 1) * 512].bitcast(BF16)

        qz = RA[:, :].rearrange("p (c t) -> p c t", c=8)
        ka = RDK[:, 0:8192].rearrange("p (c t) -> p c t", c=4)
        hT = RB[:, :].rearrange("p (c t) -> p c t", c=8)
        OT = RC[:, :].rearrange("p (c t) -> p c t", c=8)
        xn_bf = RC[:, 0:8192].rearrange("p (j f) -> p j f", j=8)
        cs_sb = RC[:, 8192:16384].bitcast(F32)
        cosT = cs_sb[:, 0:2048]
        sinT = cs_sb[:, 2048:4096]
        xs = RA[:, :].bitcast(F32).rearrange("p (j f) -> p j f", j=8)
        dvaug = RDK[:, 8192:16512].rearrange("p (t h d) -> p t h d", t=16, h=4)
        svb = RDK[:, 8192:16384].rearrange("p (t f) -> p t f", t=16)
        uT = RDK[:, 0:16384].rearrange("p (c t) -> p c t", c=8)
        wslot = [RE[:, s * 4096:(s + 1) * 4096].rearrange("p (k n) -> p k n", k=8) for s in range(4)]
        woutg = RE[:, 0:8192].rearrange("p (k n) -> p k n", k=8)
        w2slot = [RE[:, 8192 + s * 4096:8192 + (s + 1) * 4096].rearrange("p (k n) -> p k n", k=4) for s in range(2)]
        xr_lo = RA[:, :].bitcast(F32).rearrange("p (j f) -> p j f", j=8)
        xr_hi = RB[:, :].bitcast(F32).rearrange("p (j f) -> p j f", j=8)

        def xr(tt):
            return xr_lo[:, tt, :] if tt < 8 else xr_hi[:, tt - 8, :]

        lneg = RB[:, 0:8192].rearrange("p (i t) -> p i t", i=16)
        stage = [RG[:, 0:1024], RG[:, 1024:2048]]
        cfm = small[:, 0:8]
        cact = small[:, 8:16]
        gfm = small[:, 16:32]
        modT = small[:, 32:64]
        a_m = small[:, 64:72]
        a_f = small[:, 72:80]
        ssq = small[:, 80:96]
        rstd = small[:, 96:112]
        lamw = small[:, 112:116]
        neglam = small[:, 116:117]
        ssq2 = small[:, 120:136]
        rstd2 = small[:, 136:152]
        ssq3 = small[:, 152:168]
        rstd3 = small[:, 168:184]
        dsm = small[:, 184:256]
        crep = sb("crep", [P, 8, 128], BF16)
        lam_sb = sb("lam_sb", [P, 4, 64], F32)
        subgs = sb("subgs", [P, 128], F32)
        fngs = gm_bc
        junk_t = sb("junk", [P, 2 * D], BF16)
        junk = junk_t[:, 0:D]
        junk_alt = junk_t[:, D:2 * D]

        add("sp", DMA(cb[:, :], cb_d), writes=["cb"])
        add("sp", DMA(small[:, 0:8], cfm_d), writes=["cfm"])
        add("sp", DMA(small[:, 16:32], gfm_d), writes=["gfm"])
        add("sp", DMA(identf[:, :], cf_d), writes=["identf"])
        add("sp", DMA(ind[:, :], ind_d), writes=["ind"])
        add("sp", DMA(lam_sb[:, :, :], lam_d), writes=["lam_sb"])
        add("sp", DMA(subgs[:, :], subg_d), writes=["subgs"])
        add("pool", MEMSET(small[:, 80:96], 0.0), writes=["xn_ss"])
        add("pool", MEMSET(small[:, 120:136], 0.0), writes=["xn2_ss"])
        add("pool", MEMSET(small[:, 152:168], 0.0), writes=["ssq3"])
        add("pool", MEMSET(small[:, 184:256], 0.0), writes=[("dsm", k_, 3) for k_ in range(2)])

        add("act", ACT(cact, cfm, AF.Silu), reads=["cfm"], writes=["cact"])
        add("dve", CP(crep[:, :, :], cact.unsqueeze(2).to_broadcast([P, 8, 128])), reads=["cact"], writes=["crep"])
        add("dve", TT(lam_sb[:, 0:2, :], lam_sb[:, 0:4:2, :], lam_sb[:, 1:4:2, :], ALU.mult), reads=["lam_sb"], writes=["lam_sb2"])
        add("dve", RED(lamw[:, 0:2], lam_sb[:, 0:2, :]), reads=["lam_sb2"], writes=["lamw"])
        add("act", ACT(lamw[:, 2:4], lamw[:, 0:2], AF.Exp), reads=["lamw"], writes=["lamw2"])
        add("dve", STT(neglam, lamw[:, 3:4], -LAMBDA_INIT, lamw[:, 2:3], ALU.add, ALU.subtract), reads=["lamw2"], writes=["neglam"])
        add("dve", TS(subgs[:, :], subgs[:, :], (1.0 - LAMBDA_INIT) * math.sqrt(128.0), None, ALU.mult), reads=["subgs"], writes=["subgs"])

        for t_ in range(8):
            add("sp", DMA(xs[:, t_ % 8, :], x_d[t_ * 128:(t_ + 1) * 128, :]), writes=[("xs", t_ % 8)])
        add("sp", DMA(cs_sb, cs_d), writes=["cs"])
        adaw_v = adaw_d.rearrange("(k p) n -> p k n", p=P)
        wcount = [0]

        def wslot_load(src_ap, tag):
            s = wcount[0] % 4
            wcount[0] += 1
            add("poolq", DMA(wslot[s][:, :, :], src_ap), writes=[("w", s)])
            return s

        adab_slots = [RF[:, 0:1024].bitcast(F32), RF[:, 1024:2048].bitcast(F32)]
        tmp_mod_e = RF[:, 2048:3072].bitcast(F32)
        tmp_mod2_e = RF[:, 3072:4096].bitcast(F32)
        piece_order = [2, 3, 0, 1, 4, 5, 6, 7, 8, 9, 10, 11]
        modcol = {0: 0, 1: 4, 2: 8, 3: 12, 6: 16, 7: 20, 8: 24, 9: 28}
        def ada_piece(n_, pc, late=None):
            if late is None:
                s = wslot_load(adaw_v[:, :, pc * 512:(pc + 1) * 512], "ada")
                ab = adab_slots[n_ % 2]
                abk = ("adab", n_ % 2)
                add("sp", DMA(ab, adab_d[:, pc * 512:(pc + 1) * 512]), writes=[abk])
                b = n_ % 2
                tmp_mod, tmp_mod2, tk, tk2, aft = tmp_mod_e, tmp_mod2_e, "tmp_mod", "tmp_mod2", []
            else:
                s, ab, abk, b = late["slot"], late["ab"], late["abk"], 7
                tmp_mod, tmp_mod2, tk, tk2, aft = late["tmp"], late["tmp2"], "tmp_modB", "tmp_mod2B", late["after"]
                if late["stage"] == "load":
                    add("poolq", DMA(wslot[s][:, :, :], adaw_v[:, :, pc * 512:(pc + 1) * 512]), writes=[("w", s)])
                    add("sp", DMA(ab, adab_d[:, pc * 512:(pc + 1) * 512]), writes=[abk], after=aft)
                    return
            for kc in range(8):
                add("pe", MM(bank(b), crep[:, kc, :], wslot[s][:, kc, :], start=(kc == 0), stop=(kc == 7)),
                    reads=["crep", ("w", s)], writes=[("ps", b)])
            if pc in (4, 5):
                add("dve", TT(gm_bc[:, (pc - 4) * 512:(pc - 3) * 512], bank(b), ab, ALU.add),
                    reads=[("ps", b), abk], writes=[("gm", pc - 4)])
            elif pc in (10, 11):
                add("dve", TT(gf_bc[:, (pc - 10) * 512:(pc - 9) * 512], bank(b), ab, ALU.add),
                    reads=[("ps", b), abk], writes=[("gf", pc - 10)])
            else:
                add("dve", TT(tmp_mod, bank(b), ab, ALU.add), reads=[("ps", b), abk], writes=[tk], after=aft)
                add("dve", TT(tmp_mod2.rearrange("p (a b) -> p a b", a=4), tmp_mod.rearrange("p (a b) -> p a b", a=4),
                              identf[:, :].unsqueeze(1).to_broadcast([P, 4, 128]), ALU.mult),
                    reads=[tk, "identf"], writes=[tk2], after=aft)
                c0 = modcol[pc]
                add("dve", RED(modT[:, c0:c0 + 4], tmp_mod2.rearrange("p (a b) -> p a b", a=4)),
                    reads=[tk2], writes=[("modT", c0)])
            if pc == 3:
                add("dve", STT(a_m, modT[:, 8:16], 1.0, gfm[:, 0:8], ALU.add, ALU.mult),
                    reads=[("modT", 8), ("modT", 12), "gfm"], writes=["a_m"])
            if pc == 9:
                add("dve", STT(a_f, modT[:, 24:32], 1.0, gfm[:, 8:16], ALU.add, ALU.mult),
                    reads=[("modT", 24), ("modT", 28), "gfm"], writes=["a_f"])

        for n_, pc in enumerate(piece_order[:4]):
            ada_piece(n_, pc)

        mark("p0")
        def norm_phase(src_tile, ssq_, rstd_, xnbuf, aff_a, aff_sh, dstT, src_reads, xn_key, dst_key, banks, junk, extra_reads, xn_after=(), dst_after=(), pre_group=None, groups=None):
            bi = [0]
            for tt in [4 * g_ + j_ for g_ in (groups if groups is not None else range(NU)) for j_ in range(4)]:
                U = tt // 4
                slot = tt % 8
                if pre_group is not None and tt % 4 == 0:
                    pre_group(U)
                add("act", ACT(junk if tt % 2 == 0 else junk_alt, src_tile(tt), AF.Square, accum_out=ssq_[:, tt:tt + 1]),
                    reads=src_reads(tt) + [xn_key + "_ss"], writes=[(xn_key + "_ssq", tt), ("junk", tt % 2)])
                if tt % 4 == 3:
                    g0 = 4 * U
                    mark("n_sq")
                    add("act", ACT(rstd_[:, g0:g0 + 4], ssq_[:, g0:g0 + 4], AF.Ln, bias=D * EPS),
                        reads=[(xn_key + "_ssq", g0 + j) for j in range(4)], writes=[(xn_key + "_ln", U)])
                    add("act", ACT(rstd_[:, g0:g0 + 4], rstd_[:, g0:g0 + 4], AF.Exp, scale=-0.5),
                        reads=[(xn_key + "_ln", U)], writes=[(xn_key + "_rstd", U)])
                    mark("n_ln")
                    for j in range(4):
                        t_ = g0 + j
                        add("dve", TS(xnbuf[:, t_ % 8, :], src_tile(t_), rstd_[:, t_:t_ + 1], 32.0, ALU.mult, ALU.mult),
                            reads=src_reads(t_) + [(xn_key + "_rstd", U)], writes=[(xn_key, t_ % 8)], after=xn_after)
                    mark("n_norm")
                    for kp in range(4):
                        b = banks[bi[0] % len(banks)]
                        bi[0] += 1
                        pb = bankbf(b)
                        for k2 in range(2):
                            kc = kp * 2 + k2
                            for j in range(4):
                                add("pe", TR(pb[:, k2 * 512 + j * 128:k2 * 512 + (j + 1) * 128],
                                             xnbuf[:, (U % 2) * 4 + j, kc * 128:(kc + 1) * 128], ident),
                                    reads=[(xn_key, (U % 2) * 4 + j), "cb"], writes=[("ps", b)])
                        mark("n_tr")
                        for k2 in range(2):
                            kc = kp * 2 + k2
                            dst = dstT[:, kc, U * 512:(U + 1) * 512]
                            src = pb[:, k2 * 512:(k2 + 1) * 512]
                            if kp % 2 == 0:
                                add("act", ACT(dst, src, AF.Identity, scale=aff_a[:, kc:kc + 1], bias=aff_sh[:, kc:kc + 1]),
                                    reads=[("ps", b)] + extra_reads, writes=[(dst_key, kc, U)], after=dst_after)
                                mark("n_ea")
                            else:
                                add("dve", TS(dst, src, aff_a[:, kc:kc + 1], aff_sh[:, kc:kc + 1], ALU.mult, ALU.add),
                                    reads=[("ps", b)] + extra_reads, writes=[(dst_key, kc, U)], after=dst_after)

        win_v = win_d.rearrange("(k p) n -> p k n", p=P)
        pre_slots = {}
        for pcs_ in ((0, 6), (1, 7)):
            pre_slots[pcs_] = (wslot_load(win_v[:, :, pcs_[0] * 512:(pcs_[0] + 1) * 512], "win"),
                               wslot_load(win_v[:, :, pcs_[1] * 512:(pcs_[1] + 1) * 512], "win"))

        def load_x_group(U):
            groups = [] if U == 0 else ([U + 1] if U + 1 < NU else [])
            for g_ in groups:
                for j in range(4):
                    t_ = 4 * g_ + j
                    add("sp", DMA(xs[:, t_ % 8, :], x_d[t_ * 128:(t_ + 1) * 128, :]), writes=[("xs", t_ % 8)])

        norm_phase(lambda tt: xs[:, tt % 8, :], ssq, rstd, xn_bf, a_m, modT[:, 0:8], hT,
                   lambda tt: [("xs", tt % 8)], "xn", "hT", [0, 1, 2, 3], junk, ["a_m", ("modT", 0), ("modT", 4)],
                   pre_group=load_x_group)

        mark("p1")
        t1s = [RF[:, 5120 + i * 1024:5120 + (i + 1) * 1024].bitcast(F32) for i in range(2)]
        t2s = [RF[:, 7168 + i * 512:7168 + (i + 1) * 512] for i in range(0)]
        t2a = [RG[:, 0:512], RG[:, 512:1024], RG[:, 1024:1536], RG[:, 1536:2048]]
        ropec = [0]
        pbk = [4, 5, 6, 7]
        xs_keys = [("xs", i) for i in range(8)]
        for c_ in range(4):
            add("pool", MEMSET(qz[64:128, 2 * c_, :], 0.0), after=xs_keys, writes=[("qzero", 2 * c_)])
            add("dve", MEMSET(qz[0:64, 2 * c_ + 1, :], 0.0), after=xs_keys, writes=[("qzero", 2 * c_ + 1)])
        for (pa, pb_, dest, dkey, scale) in [(0, 6, None, "qz", 0.125), (1, 7, ka, "ka", 1.0)]:
            sa, sb_ = pre_slots[(pa, pb_)]
            for cc in range(4):
                for U in range(NU):
                    r = ropec[0]
                    ropec[0] += 1
                    ba = pbk[(2 * r) % 4]
                    bb = pbk[(2 * r + 1) % 4]
                    for kc in range(8):
                        add("pe", MM(bank(ba), wslot[sa][:, kc, cc * 128:(cc + 1) * 128], hT[:, kc, U * 512:(U + 1) * 512],
                                     start=(kc == 0), stop=(kc == 7)),
                            reads=[("w", sa), ("hT", kc, U)], writes=[("ps", ba)])
                    for kc in range(8):
                        add("pe", MM(bank(bb), wslot[sb_][:, kc, cc * 128:(cc + 1) * 128], hT[:, kc, U * 512:(U + 1) * 512],
                                     start=(kc == 0), stop=(kc == 7)),
                            reads=[("w", sb_), ("hT", kc, U)], writes=[("ps", bb)])
                    t1 = t1s[r % 2]
                    t2 = t2a[r % 2]
                    add("dve", STT(t1, bank(ba), scale, cosT[:, U * 512:(U + 1) * 512], ALU.mult, ALU.mult),
                        reads=[("ps", ba), "cs"], writes=[("t1", r % 2)])
                    add("dve", STT(t2, bank(bb), scale, sinT[:, U * 512:(U + 1) * 512], ALU.mult, ALU.mult),
                        reads=[("ps", bb), "cs"], writes=[("t2", r % 2)])
                    if dest is None:
                        for m_ in range(2):
                            add("dve", TT(qz[m_ * 64:(m_ + 1) * 64, 2 * cc + m_, U * 512:(U + 1) * 512],
                                           t1[m_ * 64:(m_ + 1) * 64, :], t2[m_ * 64:(m_ + 1) * 64, :], ALU.add),
                                reads=[("t1", r % 2), ("t2", r % 2), ("qzero", 2 * cc + m_)], writes=[("qz", 2 * cc + m_, U)])
                    else:
                        add("pool", TT(dest[:, cc, U * 512:(U + 1) * 512], t1, t2, ALU.add),
                            reads=[("t1", r % 2), ("t2", r % 2)], writes=[(dkey, cc, U)])
            if dest is None:
                sv_ = wslot_load(win_v[:, :, 2 * 512:3 * 512], "win")
                s_sq = wslot_load(win_v[:, :, 3 * 512:4 * 512], "win")
            else:
                s_sk = wslot_load(win_v[:, :, 4 * 512:5 * 512], "win")
                s_sv = wslot_load(win_v[:, :, 5 * 512:6 * 512], "win")
        add("pool", MEMSET(dvaug[:, :, :, 128:130], 1.0), writes=["dvones"])
        for tt in range(NT):
            b = pbk[tt % 4]
            for kc in range(8):
                add("pe", MM(bank(b), hT[:, kc, tt * 128:(tt + 1) * 128], wslot[sv_][:, kc, :], start=(kc == 0), stop=(kc == 7)),
                    reads=[("w", sv_), ("hT", kc, tt // 4)], writes=[("ps", b)])
            eng = "act" if tt % 2 == 0 else "dve"
            src = bank(b).rearrange("p (h d) -> p h d", h=4)
            if eng == "act":
                add("act", ACT(dvaug[:, tt, :, 0:128], src, AF.Copy), reads=[("ps", b), "dvones"], writes=[("dv", tt)])
            else:
                add("dve", CP(dvaug[:, tt, :, 0:128], src), reads=[("ps", b), "dvones"], writes=[("dv", tt)])

        if debug:
            add("sp", DMA(dbg["h1"], RB[:, :]), reads=[("hT", kc, U) for kc in range(8) for U in range(NU)], writes=["dbgh1"])

        mark("p1a")
        et = [RF[:, k_ * 512:(k_ + 1) * 512].rearrange("p (m t) -> p m t", m=2) for k_ in range(4)]
        od = [RF[:, 2048 + k_ * 256:2048 + (k_ + 1) * 256].rearrange("p (j f) -> p j f", j=2) for k_ in range(2)]
        tmpd = [RF[:, 3072:3328].bitcast(F32), RF[:, 3328:3584].bitcast(F32)]
        od32 = [RF[:, 3584 + i * 256:3840 + i * 256].bitcast(F32) for i in range(4)]
        phase_rf = ["tmp_mod", "tmp_mod2", ("adab", 0), ("adab", 1)]
        NSL = 4
        DIST = 3
        djobs = []
        pend = []
        rnd = 0
        for h in range(4):
            for R in range(8):
                for i in range(2 * R + 2):
                    djobs.append(("t", h, R, i, rnd % 2))
                    for p_ in pend:
                        p_[0] -= 1
                    while pend and pend[0][0] <= 0:
                        djobs.append(pend.pop(0)[1])
                pend.append([5, ("f3", h, R, 0, rnd % 2)])
                rnd += 1
        for _ in range(DIST + 2):
            djobs.append(("nop", 0, 0, 0, 0))
        for p_ in pend:
            djobs.append(p_[1])

        def d_s1(job, sl, first):
            kind, h, R, i, fs = job
            if kind == "nop":
                return
            if kind == "f3":
                sm = dsm[:, fs * 16:fs * 16 + 16]
                for j in range(2):
                    add("dve", (lambda o_, i_, a_: (lambda g: g.scalar_tensor_tensor(o_, i_, 1.0, i_, ALU.mult, ALU.mult, accum_out=a_)))(
                        junk_t[:, (fs * 2 + j) * 128:(fs * 2 + j + 1) * 128], od32[fs * 2 + j], sm[:, 8 + j:9 + j]),
                        reads=[("od32", fs * 2 + j), ("dsm", fs, 3)], writes=[("dsm", fs, 4, j), ("junk", 0)])
                add("act", ACT(sm[:, 12:14], sm[:, 8:10], AF.Ln, bias=128.0 * EPS), reads=[("dsm", fs, 4, j) for j in range(2)], writes=[("dsm", fs, 5)])
                add("act", ACT(sm[:, 12:14], sm[:, 12:14], AF.Exp, scale=-0.5), reads=[("dsm", fs, 5)], writes=[("dsm", fs, 6)])
                for j in range(2):
                    add("dve", STT(od[fs][:, j, :], od32[fs * 2 + j], sm[:, 12 + j:13 + j], subgs[:, :], ALU.mult, ALU.mult),
                        reads=[("od32", fs * 2 + j), ("dsm", fs, 6), "subgs"], writes=[("od", fs, j)])
                add("dve", MEMSET(sm[:, 8:10], 0.0), reads=[("dsm", fs, 5)], writes=[("dsm", fs, 3)])
                return
            t0 = max(R * 256, i * 128)
            w = (R + 1) * 256 - t0
            for m in range(2):
                add("pe", MM(bank(sl)[:, m * 256:m * 256 + w], ka[:, h, i * 128:(i + 1) * 128], qz[:, 2 * h + m, t0:t0 + w]),
                    reads=[("ka", h, i // 4), ("qzero", 2 * h + m), ("qz", 2 * h + m, t0 // 512)], writes=[("ps", sl)])
            scv = bank(sl).rearrange("p (m t) -> p m t", m=2)
            add("act", ACT(et[sl][:, :, 0:w], scv[:, :, 0:w], AF.Exp),
                reads=[("ps", sl)], writes=[("et", sl)], after=(phase_rf + [("t1", 0), ("t1", 1)] if first else []))
            if i >= 2 * R:
                add("pool", MEMSET(et[sl][64:128, :, 0:64], 0.0), reads=[], writes=[("et", sl)])

        def d_s2(job, sl):
            kind, h, R, i, fs = job
            if kind == "nop":
                return
            if kind == "f3":
                pb = bankbf(sl)
                for j in range(2):
                    add("pe", TR(pb[:, j * 128:(j + 1) * 128], od[fs][:, j, :], ident),
                        reads=[("od", fs, j), "cb"], writes=[("ps", sl)])
                add("dve", CP(OT[:, h, R * 256:(R + 1) * 256], pb[:, 0:256]),
                    reads=[("ps", sl)], after=["cs"] + [("xn", s_) for s_ in range(8)], writes=[("OT", h, R)])
                return
            t0 = max(R * 256, i * 128)
            for T in range(max(2 * R, i), 2 * R + 2):
                bacc = 4 + 2 * fs + (T % 2)
                c0 = T * 128 - t0
                for m in range(2):
                    add("pe", MM(bank(bacc)[:, m * 132:m * 132 + 129], et[sl][:, m, c0:c0 + 128], dvaug[:, i, h, 0:129],
                                 start=(i == 0 and m == 0), stop=(i == T), skip=True),
                        reads=[("et", sl), ("dv", i)], writes=[("ps", bacc)])
            if i != 2 * R + 1:
                return
            sm = dsm[:, fs * 16:fs * 16 + 16]
            ab0 = 4 + 2 * fs
            accs = [("ps", ab0), ("ps", ab0 + 1)]
            add("dve", RECIP(sm[:, 0:2], PS[:, ab0 * 512 + 128:(ab0 + 2) * 512:512]), reads=accs, writes=[("dsm", fs, 0)])
            add("dve", RECIP(sm[:, 4:6], PS[:, ab0 * 512 + 260:(ab0 + 2) * 512:512]), reads=accs, writes=[("dsm", fs, 1)])
            add("dve", TS(sm[:, 4:6], sm[:, 4:6], neglam, None, ALU.mult), reads=[("dsm", fs, 1), "neglam"], writes=[("dsm", fs, 1)])
            for j in range(2):
                bacc = ab0 + j
                acc0 = bank(bacc)[:, 0:128]
                acc1 = bank(bacc)[:, 132:260]
                o32 = od32[fs * 2 + j]
                add("dve", TS(tmpd[j], acc1, sm[:, 4 + j:5 + j], None, ALU.mult), reads=[("ps", bacc), ("dsm", fs, 1)], writes=[("tmpd", j)])
                add("dve", STT(o32, acc0, sm[:, j:j + 1], tmpd[j], ALU.mult, ALU.add),
                    reads=[("ps", bacc), ("dsm", fs, 0), ("tmpd", j)], writes=[("od32", fs * 2 + j)])

        hist = []
        for n_, job in enumerate(djobs + [None] * DIST):
            if job is not None:
                d_s1(job, n_ % NSL, n_ == 0)
            hist.append(job)
            if n_ >= DIST and hist[n_ - DIST] is not None:
                d_s2(hist[n_ - DIST], (n_ - DIST) % NSL)

        mark("p2a")
        pc_ = [0]
        dv_all = [("dv", t) for t in range(16)] + ["dvones"]
        for (sw, dest, dkey, scale) in [(s_sq, None, "qz", 0.125), (s_sk, ka, "ka", 1.0)]:
            for cc in range(4):
                for U in range(NU):
                    b = pc_[0] % 8
                    pc_[0] += 1
                    for kc in range(8):
                        add("pe", MM(bank(b), wslot[sw][:, kc, cc * 128:(cc + 1) * 128], hT[:, kc, U * 512:(U + 1) * 512],
                                     start=(kc == 0), stop=(kc == 7)),
                            reads=[("w", sw), ("hT", kc, U)], writes=[("ps", b)])
                    if dest is None:
                        for m_ in range(2):
                            dst_ = qz[m_ * 64:(m_ + 1) * 64, 2 * cc + m_, U * 512:(U + 1) * 512]
                            src_ = bank(b)[m_ * 64:(m_ + 1) * 64, :]
                            if pc_[0] % 2 == 0:
                                add("act", ACT(dst_, src_, AF.Identity, scale=scale), reads=[("ps", b)], writes=[("qz", 2 * cc + m_, U)])
                            else:
                                add("dve", TS(dst_, src_, scale, None, ALU.mult), reads=[("ps", b)], writes=[("qz", 2 * cc + m_, U)])
                    elif pc_[0] % 2 == 0:
                        add("act", ACT(dest[:, cc, U * 512:(U + 1) * 512], bank(b), AF.Identity, scale=scale),
                            reads=[("ps", b)], writes=[(dkey, cc, U)])
                    else:
                        add("dve", TS(dest[:, cc, U * 512:(U + 1) * 512], bank(b), scale, None, ALU.mult),
                            reads=[("ps", b)], writes=[(dkey, cc, U)])
        for tt in range(NT):
            b = pc_[0] % 8
            pc_[0] += 1
            for kc in range(8):
                add("pe", MM(bank(b), hT[:, kc, tt * 128:(tt + 1) * 128], wslot[s_sv][:, kc, :], start=(kc == 0), stop=(kc == 7)),
                    reads=[("w", s_sv), ("hT", kc, tt // 4)], writes=[("ps", b)])
            if tt % 2 == 0:
                add("act", ACT(svb[:, tt, :], bank(b), AF.Copy), reads=[("ps", b)], after=dv_all, writes=[("sv", tt)])
            else:
                add("dve", CP(svb[:, tt, :], bank(b)), reads=[("ps", b)], after=dv_all, writes=[("sv", tt)])

        stc = [0]

        def wout_prep(kcs, fin_):
            for kc in kcs:
                ss_ = stc[0] % 2
                stc[0] += 1
                add("sp", DMA(stage[ss_], wout_d[kc * 128:(kc + 1) * 128, :]), writes=[("stage", ss_), ("t2", 0), ("t2", 1)])
                add("pool", TT(woutg[:, kc, :], stage[ss_], gm_bc[:, :], ALU.mult),
                    reads=[("stage", ss_), ("gm", 0), ("gm", 1)], writes=[("w", 0), ("w", 1)] if kc == 0 else [("woutg", kc)])
            if fin_:
                add("sp", DMA(fngs[:, :], fng_d), writes=["fngs"], after=[("gm", 0), ("gm", 1)])
                add("pool", TS(fngs[:, :], fngs[:, :], 32.0, None, ALU.mult), reads=["fngs"], writes=["fngs"])

        mark("p1b")
        etmp = [RF[:, 4096:5120].bitcast(F32), RF[:, 5120:6144].bitcast(F32)]
        at = [RF[:, 6144:6656], RF[:, 6656:7168], RF[:, 7168:7680]]
        negr = [RF[:, 7680:8192], RF[:, 0:512]]
        hT_keys = [("hT", kc, U) for kc in range(8) for U in range(NU)]
        units = [(U, h) for U in range(NU) for h in range(8)]
        BB = 2
        sbc = {"za": 0, "zb": 0, "at": 0}

        def geom(U, i):
            t0 = max(U * 512, i * 128)
            w = (U + 1) * 512 - t0
            return t0, w, t0 - U * 512

        def qk_reads(cc, U, i, t0, h_):
            return [("ka", cc, i // 4)] + [("qz", h_, uu) for uu in range(t0 // 512, U + 1)]

        def A1(un, i, st):
            U, h = units[un]
            cc, r0 = h // 2, (h % 2) * 64
            t0, w, c0 = geom(U, i)
            b = sbc["za"] % 2
            sbc["za"] += 1
            st["zab"] = b
            add("pe", MM(bank(b)[:, 0:w], ka[:, cc, i * 128:(i + 1) * 128], qz[:, h, t0:t0 + w]),
                reads=qk_reads(cc, U, i, t0, h), writes=[("ps", b)])
            add("act", ACT(bank(b)[:, 0:w], bank(b)[:, 0:w], AF.Exp), reads=[("ps", b)], writes=[("ps", b)])
            add("act", ACT(lneg[:, i, 0:w], bank(b)[:, 0:w], AF.Ln, bias=1.0), reads=[("ps", b)],
                after=(hT_keys if un < 8 else []), writes=[("lneg", i)])
            if i >= 4 * U:
                add("dve", TT(lneg[:, i, 0:128], lneg[:, i, 0:128], masku, ALU.mult), reads=[("lneg", i), "cb"], writes=[("lneg", i)])

        def A2(un, i):
            U, h = units[un]
            nI = 4 * U + 4
            t0, w, c0 = geom(U, i)
            add("pe", MM(bank(BB)[:, c0:c0 + w], wsel_i(i), lneg[:, i, 0:w], start=(i == 0), stop=(i == nI - 1)),
                reads=[("lneg", i), "cb"], writes=[("ps", BB)])
            if i == nI - 1:
                add("dve", CP(negr[un % 2][:, :], bank(BB)[:, :]), reads=[("ps", BB)], writes=[("negr", un % 2)],
                    after=([("et", 0), ("et", 1), ("et", 2), ("et", 3)] if un < 2 else []))

        def B1(un, i, st):
            U, h = units[un]
            cc, r0 = h // 2, (h % 2) * 64
            nI = 4 * U + 4
            t0, w, c0 = geom(U, i)
            b = 3 + (sbc["zb"] % 2)
            sbc["zb"] += 1
            add("pe", MM(bank(b)[:, 0:w], ka[:, cc, i * 128:(i + 1) * 128], qz[:, h, t0:t0 + w], start=True, stop=False),
                reads=qk_reads(cc, U, i, t0, h), writes=[("ps", b)])
            if i < nI - 1:
                add("pe", MM(bank(b)[:, 0:w], indv[:, i, :], negr[un % 2][:, c0:c0 + w], start=False, stop=False),
                    reads=[("negr", un % 2), "ind"], writes=[("ps", b)])
            add("pe", MM(bank(b)[:, 0:w], trineg, lneg[:, i, 0:w], start=False, stop=True),
                reads=[("lneg", i), "cb"], writes=[("ps", b)])
            k = sbc["at"] % 3
            sbc["at"] += 1
            st[("at", i)] = k
            add("act", ACT(at[k][:, 0:w], bank(b)[:, 0:w], AF.Exp), reads=[("ps", b)], writes=[("at", k)],
                after=([("t1", 0), ("t1", 1)] if un == 0 else []))
            if i >= 4 * U:
                add("dve", TT(at[k][:, 0:128], at[k][:, 0:128], masku, ALU.mult), reads=[("at", k), "cb"], writes=[("at", k)])

        def B2(un, i, st):
            U, h = units[un]
            cc, r0 = h // 2, (h % 2) * 64
            nI = 4 * U + 4
            t0, w, c0 = geom(U, i)
            ob = 5 + (un % 2)
            k = st[("at", i)]
            add("pe", MM(bank(ob)[:, c0:c0 + w], svb[:, i, cc * 128:(cc + 1) * 128], at[k][:, 0:w], start=(i == 0), stop=(i == nI - 1)),
                reads=[("at", k), ("sv", i)], writes=[("ps", ob)])
            if i == nI - 1:
                if False:
                    add("act", ACT(OT[r0:r0 + 64, 4 + cc, U * 512:(U + 1) * 512], bank(ob)[r0:r0 + 64, :], AF.Copy),
                        reads=[("ps", ob)], writes=[("OT", 4 + cc, U, h % 2)])
                else:
                    add("dve", CP(OT[r0:r0 + 64, 4 + cc, U * 512:(U + 1) * 512], bank(ob)[r0:r0 + 64, :]),
                        reads=[("ps", ob)], writes=[("OT", 4 + cc, U, h % 2)])

        nI0 = 4 * units[0][0] + 4
        stA = {}
        for t in range(nI0 + 1):
            if t < nI0:
                A1(0, t, stA)
            if t >= 1:
                A2(0, t - 1)
        late_pcs = piece_order[4:]
        diff_scratch = [("et", k_) for k_ in range(4)] + [("od", f_, j_) for f_ in range(2) for j_ in range(2)] + \
                       [("tmpd", 0), ("tmpd", 1)] + [("od32", j_) for j_ in range(4)]
        late_ab = [RF[:, 512:1536].bitcast(F32), RF[:, 4096:5120].bitcast(F32)]

        def late_cfg(k_, stage_):
            return dict(slot=2 + (k_ % 2), ab=late_ab[k_ % 2], abk=("adab2", k_ % 2), tmp=RF[:, 2048:3072].bitcast(F32),
                        tmp2=RF[:, 3072:4096].bitcast(F32), after=(diff_scratch if k_ < 2 else []), stage=stage_)

        for un in range(len(units)):
            if un % 2 == 0 and 10 <= un <= 10 + 2 * (len(late_pcs) - 1):
                k_ = (un - 10) // 2
                ada_piece(4 + k_, late_pcs[k_], late=late_cfg(k_, "compute"))
            if un % 2 == 0 and 8 <= un <= 8 + 2 * (len(late_pcs) - 1):
                k_ = (un - 8) // 2
                ada_piece(4 + k_, late_pcs[k_], late=late_cfg(k_, "load"))
            if 17 <= un <= 24:
                wout_prep([un - 17], un == 24)
            nIu = 4 * units[un][0] + 4
            nIn = 4 * units[un + 1][0] + 4 if un + 1 < len(units) else 0
            stB = {}
            for t in range(max(nIu, nIn) + 1):
                if t < nIu:
                    B1(un, t, stB)
                if t < nIn:
                    A1(un + 1, t, stA)
                if 1 <= t <= nIu:
                    B2(un, t - 1, stB)
                if 1 <= t <= nIn:
                    A2(un + 1, t - 1)


        mark("p2b")
        qk_keys = [("qz", c, U) for c in range(8) for U in range(4)] + [("qzero", c) for c in range(8)]
        ka_keys = [("ka", c, U) for c in range(4) for U in range(4)]
        rb_keys = [("lneg", i) for i in range(16)] + hT_keys
        OT_all = [("OT", c, R_) for c in range(4) for R_ in range(8)] + [("OT", 4 + c, U, k) for c in range(4) for U in range(4) for k in range(2)]
        wg_keys = [("w", 0), ("w", 1)] + [("woutg", kc) for kc in range(1, 8)]
        for tt in range(NT):
            add("sp", DMA(xr(tt), x_d[tt * 128:(tt + 1) * 128, :]),
                writes=[("xr", tt)], after=(["dbgqk"] if debug else []) + (qk_keys if tt < 8 else rb_keys))
        p3 = [0]

        def p3_group(U_):
          for tt in range(4 * U_, 4 * U_ + 4):
            for nh in range(2):
                b = p3[0] % 8
                p3[0] += 1
                for fc in range(8):
                    rk = [("OT", fc, tt // 2)] if fc < 4 else [("OT", fc, tt // 4, 0), ("OT", fc, tt // 4, 1)]
                    add("pe", MM(bank(b), OT[:, fc, tt * 128:(tt + 1) * 128], woutg[:, fc, nh * 512:(nh + 1) * 512], start=(fc == 0), stop=(fc == 7)),
                        reads=rk + wg_keys, writes=[("ps", b)])
                add("dve", TT(xr(tt)[:, nh * 512:(nh + 1) * 512], bank(b), xr(tt)[:, nh * 512:(nh + 1) * 512], ALU.add),
                    reads=[("ps", b), ("xr", tt)], writes=[("xr", tt)])
        mark("p3")
        xn2 = RF[:, :].rearrange("p (j f) -> p j f", j=8)
        h2T = RC[:, :].rearrange("p (c t) -> p c t", c=8)
        junkb = junk[:, :]
        rf_keys = [("at", 0), ("at", 1), ("at", 2), ("negr", 0), ("negr", 1), ("et", 0), ("et", 1), ("et", 2), ("et", 3),
                   ("tmpd", 0), ("tmpd", 1)] + [("od32", j) for j in range(4)]
        def n2_group(U_):
            ot_blk = [("OT", c, R_) for c in range(4) for R_ in (2 * U_, 2 * U_ + 1)] + \
                     [("OT", 4 + c, U_, k) for c in range(4) for k in range(2)]
            norm_phase(lambda tt: xr(tt), ssq2, rstd2, xn2, a_f, modT[:, 16:24], h2T,
                       lambda tt: [("xr", tt)], "xn2", "h2T", [0, 1, 2, 3], junkb,
                       ["a_f", ("modT", 16), ("modT", 20)],
                       xn_after=rf_keys + [("od", 0, j) for j in range(2)] + [("od", 1, j) for j in range(2)],
                       dst_after=ot_blk, groups=[U_])

        p3_group(0)
        p3_group(1)
        n2_group(0)
        p3_group(2)
        n2_group(1)
        p3_group(3)
        n2_group(2)
        n2_group(3)

        w1_v = w1_d.rearrange("(k p) n -> p k n", p=P)
        rbuf = [RF[:, i * 1024:(i + 1) * 1024].bitcast(F32) for i in range(4)]
        xn2_keys = [("xn2", s_) for s_ in range(8)]
        rc_ = [0]
        p4 = [0]
        for fb in range(4):
            ws1 = []
            for half in range(2):
                s = half
                add("poolq", DMA(wslot[s][:, :, :], w1_v[:, :, fb * 1024 + half * 512: fb * 1024 + (half + 1) * 512]),
                    reads=wg_keys if fb == 0 else [], writes=[("w", s)])
                ws1.append(s)
            for half in range(2):
                for fc in range(4):
                    ss_ = stc[0] % 2
                    stc[0] += 1
                    r_ = fb * 1024 + half * 512 + fc * 128
                    add("sp", DMA(stage[ss_], w2_d[r_:r_ + 128, :]), writes=[("stage", ss_)])
                    add("pool", TT(w2slot[half][:, fc, :], stage[ss_], gf_bc[:, :], ALU.mult),
                        reads=[("stage", ss_), ("gf", 0), ("gf", 1)], writes=[("w", 2 + half)] if fc == 0 else [("w2g", half, fc)])
            for half in range(2):
                for fc in range(4):
                    for U in range(NU):
                        b = p4[0] % 8
                        p4[0] += 1
                        for kc in range(8):
                            add("pe", MM(bank(b), wslot[ws1[half]][:, kc, fc * 128:(fc + 1) * 128], h2T[:, kc, U * 512:(U + 1) * 512],
                                         start=(kc == 0), stop=(kc == 7)),
                                reads=[("w", ws1[half]), ("h2T", kc, U)], writes=[("ps", b)])
                        rb_ = rc_[0] % 4
                        rc_[0] += 1
                        add("act", ACT(rbuf[rb_], bank(b), AF.Relu), reads=[("ps", b)], after=(xn2_keys if rc_[0] <= 4 else []), writes=[("rbuf", rb_)])
                        eng = "dve"
                        add(eng, TT(uT[:, half * 4 + fc, U * 512:(U + 1) * 512], rbuf[rb_], rbuf[rb_], ALU.mult),
                            reads=[("rbuf", rb_)], after=(ka_keys + [("dv", t) for t in range(16)] + [("sv", t) for t in range(16)] if fb == 0 else []),
                            writes=[("uT", half * 4 + fc, U)])
            for tt in range(NT):
                for nh in range(2):
                    b = p4[0] % 8
                    p4[0] += 1
                    for j in range(8):
                        half, fc = j // 4, j % 4
                        add("pe", MM(bank(b), uT[:, j, tt * 128:(tt + 1) * 128], w2slot[half][:, fc, nh * 512:(nh + 1) * 512],
                                     start=(j == 0), stop=(j == 7)),
                            reads=[("uT", j, tt // 4), ("w", 2 + half)] + [("w2g", half, f_) for f_ in range(1, 4)], writes=[("ps", b)])
                    add("dve", TT(xr(tt)[:, nh * 512:(nh + 1) * 512], bank(b), xr(tt)[:, nh * 512:(nh + 1) * 512], ALU.add),
                        reads=[("ps", b), ("xr", tt)], writes=[("xr", tt)])
                if fb == 3:
                    add("act", ACT(junk if tt % 2 == 0 else junk_alt, xr(tt), AF.Square, accum_out=ssq3[:, tt:tt + 1]),
                        reads=[("xr", tt), "ssq3"], writes=[("ssq3", tt), ("junk", tt % 2)])
                    if tt % 4 == 3:
                        g0 = tt - 3
                        add("act", ACT(rstd3[:, g0:g0 + 4], ssq3[:, g0:g0 + 4], AF.Ln, bias=D * EPS), reads=[("ssq3", g0 + j) for j in range(4)], writes=[("ln3", g0)])
                        add("act", ACT(rstd3[:, g0:g0 + 4], rstd3[:, g0:g0 + 4], AF.Exp, scale=-0.5), reads=[("ln3", g0)], writes=[("rstd3", g0)])
                        for t_ in range(g0, g0 + 4):
                            add("dve", STT(xr(t_), xr(t_), rstd3[:, t_:t_ + 1], fngs[:, :], ALU.mult, ALU.mult),
                                reads=[("xr", t_), ("rstd3", g0), "fngs"], writes=[("xr", t_)])
                            add("sp", DMA(out_d[t_ * 128:(t_ + 1) * 128, :], xr(t_)), reads=[("xr", t_)], writes=[("out", t_)])

        mark("p4")
        if debug:
            add("sp", DMA(dbg["hT"], RC[:, :]), reads=[("h2T", kc, U) for kc in range(8) for U in range(4)], writes=["dbghT"])
        _cut[0] = False
        Sd.emit(nc, final_wait_ops=list(Sd.dma_ops["sp"]))
    return nc


def _rope_tables_host():
    try:
        import jax
        import jax.numpy as jnp
        cpu = jax.devices("cpu")[0]
        with jax.default_device(cpu):
            inv = 1.0 / (10000.0 ** (jnp.arange(0, 64, 2, dtype=jnp.float32) / 64))
            ang = jnp.arange(S, dtype=jnp.float32)[:, None] * inv[None, :]
            ang = jnp.concatenate([ang, ang], axis=-1)
            cos = np.asarray(jnp.cos(ang), dtype=np.float32)
            sin = np.asarray(jnp.sin(ang), dtype=np.float32)
        if cos.shape == (S, 64) and np.isfinite(cos).all() and np.isfinite(sin).all():
            return cos, sin
    except Exception:
        pass
    inv = (1.0 / (np.float32(10000.0) ** (np.arange(0, 64, 2, dtype=np.float32) / np.float32(64)))).astype(np.float32)
    ang = np.arange(S, dtype=np.float32)[:, None] * inv[None, :]
    ang = np.concatenate([ang, ang], axis=-1)
    return np.cos(ang).astype(np.float32), np.sin(ang).astype(np.float32)


def _consts():
    bf = ml_dtypes.bfloat16
    j = np.arange(128)
    cbm = np.zeros((128, 528), np.float32)
    cbm[:, 0:128] = np.eye(128)
    cbm[:, 128:256] = -(j[:, None] >= j[None, :]).astype(np.float32)
    cbm[:, 256:384] = (j[None, :] > j[:, None]).astype(np.float32)
    cbm[:, 384:400] = 1.0
    ind = np.zeros((128, 16, 128), np.float32)
    for i in range(16):
        ind[i, i, :] = -1.0
    cf = np.eye(128, dtype=np.float32)
    cos, sin = _rope_tables_host()
    sgn = np.concatenate([-np.ones(32, np.float32), np.ones(32, np.float32)])
    cosT = np.tile(cos.T, (2, 1))
    sinT = np.tile((sin * sgn[None, :]).T, (2, 1))
    cs = np.concatenate([cosT, sinT], axis=1).astype(np.float32)
    return cbm.astype(bf), ind.reshape(128, 2048).astype(bf), cf, np.ascontiguousarray(cs)


def _prep_inputs(x, c, ada_w, ada_b, mix_norm_g, w_in, lambda_q1, lambda_k1, lambda_q2, lambda_k2,
                 diff_subln_g, w_out, ffn_norm_g, w_ff1, w_ff2, final_norm_g):
    f = np.float32
    x = np.asarray(x, f)
    c = np.asarray(c, f)
    w_in0 = np.asarray(w_in, f)[0]
    perm = np.arange(512).reshape(8, 2, 32)[:, ::-1, :].reshape(512)
    w_in_ext = np.concatenate([w_in0, w_in0[:, 0:512][:, perm], w_in0[:, 512:1024][:, perm]], axis=1)
    cbm, ind, cf, cs = _consts()
    shared = {
        "ada_w": np.ascontiguousarray(np.asarray(ada_w, f)[0]),
        "ada_b": np.ascontiguousarray(np.broadcast_to(np.asarray(ada_b, f)[0][None, :], (P, 6 * D))),
        "gfm": np.ascontiguousarray(np.concatenate([np.asarray(mix_norm_g, f)[0].reshape(8, P).T,
                                                    np.asarray(ffn_norm_g, f)[0].reshape(8, P).T], axis=1)),
        "w_in": np.ascontiguousarray(w_in_ext),
        "lam": np.ascontiguousarray(np.broadcast_to(np.stack([np.asarray(v, f)[0] for v in
                                    (lambda_q1, lambda_k1, lambda_q2, lambda_k2)])[None], (P, 4, 64))),
        "subg": np.ascontiguousarray(np.broadcast_to(np.asarray(diff_subln_g, f)[0][None, :], (P, 128))),
        "w_out": np.ascontiguousarray(np.asarray(w_out, f)[0]),
        "w_ff1": np.ascontiguousarray(np.asarray(w_ff1, f)[0]),
        "w_ff2": np.ascontiguousarray(np.asarray(w_ff2, f)[0]),
        "fng": np.ascontiguousarray(np.broadcast_to(np.asarray(final_norm_g, f)[None, :], (P, D))),
        "cb": cbm, "ind": ind, "cf": cf, "cs": cs,
    }
    in_maps = []
    for b in range(x.shape[0]):
        m = dict(shared)
        m["x"] = np.ascontiguousarray(x[b])
        m["cfm"] = np.ascontiguousarray(c[b].reshape(8, P).T)
        in_maps.append(m)
    return in_maps


_NC_CACHE = {}


def kernel(**inputs):
    in_maps = _prep_inputs(**inputs)
    if "nc" not in _NC_CACHE:
        _NC_CACHE["nc"] = build_nc(False)
    nc = _NC_CACHE["nc"]
    n = len(in_maps)
    res = run_bass_kernel_spmd(nc, in_maps, core_ids=list(range(n)))
    return np.stack([np.asarray(r["out"], np.float32) for r in res.results], axis=0)
```
